# Optimizing a Trainium2 kernel written in Bass

```python
import jax, jax.numpy as jnp
from jax import lax
import numpy as np

D_MODEL = 1024
BATCH = 8
SEQ = 4096
DEPTH = 1

A_HEADS = 4
A_DK = 128
A_DV = 128
A_CHUNK = 64
A_QK_W = A_HEADS * A_DK
A_V_W = A_HEADS * A_DV
B_HEADS = 8
B_KV_GROUPS = 2
B_HPG = B_HEADS // B_KV_GROUPS
B_HEAD_DIM = 64
B_Q_W = B_HEADS * B_HEAD_DIM
B_KV_W = B_KV_GROUPS * B_HEAD_DIM
CMP_LEN = 32
CMP_STRIDE = 16
CMP_HIDDEN = 256
SEL_LEN = 64
SEL_TOPK = 16
WINDOW = 512
Q_BLOCK = 128
FORCE_SCORE = 1e4
NEG = -1e30
ROPE_THETA = 500000.0
ROT_DIM = B_HEAD_DIM // 4
D_FF = 4 * D_MODEL
EPS = 1e-6

IN_SIZES = [A_QK_W, A_QK_W, A_V_W, A_V_W,
            B_Q_W, 6 * B_KV_W, 3 * B_HEADS,
            2 * D_MODEL]
N_IN = int(sum(IN_SIZES))
SPLIT_POINTS = [int(v) for v in np.cumsum(IN_SIZES)[:-1]]

kernel_name = "hgrn2_nsa_gated_hybrid"


def rmsnorm(x, g):
    xf = x.astype(jnp.float32)
    y = xf * lax.rsqrt(jnp.mean(xf * xf, axis=-1, keepdims=True) + EPS) * g.astype(jnp.float32)
    return y.astype(x.dtype)


def masked_softmax(s, mask):
    s = jnp.where(mask, s.astype(jnp.float32), NEG)
    m = jnp.max(s, axis=-1, keepdims=True)
    p = jnp.where(mask, jnp.exp(s - m), 0.0)
    return p / jnp.maximum(jnp.sum(p, axis=-1, keepdims=True), 1e-30)


def partial_rope(t, positions):
    half = ROT_DIM // 2
    inv_freq = ROPE_THETA ** (-jnp.arange(0, ROT_DIM, 2, dtype=jnp.float32) / ROT_DIM)
    ang = positions.astype(jnp.float32)[..., None] * inv_freq
    cos = jnp.cos(ang)[:, :, None, :].astype(t.dtype)
    sin = jnp.sin(ang)[:, :, None, :].astype(t.dtype)
    t1, t2, rest = t[..., :half], t[..., half:ROT_DIM], t[..., ROT_DIM:]
    return jnp.concatenate([t1 * cos - t2 * sin, t2 * cos + t1 * sin, rest], axis=-1)


def hgrn2_mixer(q, f_pre, v, g, lb, norm_g):
    bsz, s, _ = q.shape
    n = s // A_CHUNK
    f32 = jnp.float32
    f = lb + (1.0 - lb) * jax.nn.sigmoid(f_pre.astype(f32))
    k = 1.0 - f

    def heads(t, d):
        return t.astype(f32).reshape(bsz, n, A_CHUNK, A_HEADS, d).transpose(0, 3, 1, 2, 4)

    qh, kh, vh, lfh = heads(q, A_DK), heads(k, A_DK), heads(v, A_DV), heads(jnp.log(f), A_DK)
    b = jnp.cumsum(lfh, axis=3)
    b_last = b[..., -1:, :]
    q_dec = qh * jnp.exp(b)
    k_dec = kh * jnp.exp(-b)
    causal = jnp.tril(jnp.ones((A_CHUNK, A_CHUNK), dtype=bool))
    attn = jnp.where(causal, jnp.einsum('bhnsk,bhntk->bhnst', q_dec, k_dec), 0.0)
    o_intra = jnp.einsum('bhnst,bhntv->bhnsv', attn, vh)
    kv = jnp.einsum('bhntk,bhntv->nbhkv', kh * jnp.exp(b_last - b), vh)
    decay = jnp.exp(b_last[..., 0, :]).transpose(2, 0, 1, 3)

    def step(state, inp):
        dec, kv_n = inp
        return dec[..., None] * state + kv_n, state

    s0 = jnp.zeros((bsz, A_HEADS, A_DK, A_DV), f32)
    _, s_start = lax.scan(step, s0, (decay, kv))
    o_inter = jnp.einsum('bhnsk,nbhkv->bhnsv', q_dec, s_start)
    o = (o_intra + o_inter).transpose(0, 2, 3, 1, 4).reshape(bsz, s, A_HEADS, A_DV)
    o = rmsnorm(o, norm_g) * jax.nn.silu(g.astype(f32).reshape(bsz, s, A_HEADS, A_DV))
    return o.reshape(bsz, s, A_V_W).astype(q.dtype)


def nsa_mixer(q, kc_tok, vc_tok, ks, vs, kw, vw, gate_pre, positions,
              pe_k, pe_v, w1_k, w2_k, w1_v, w2_v):
    bsz, s, _ = q.shape
    G, HPG, HD = B_KV_GROUPS, B_HPG, B_HEAD_DIM

    def kvh(t):
        return t.reshape(bsz, s, G, HD)

    q = partial_rope(q.reshape(bsz, s, B_HEADS, HD), positions)
    kc_tok = partial_rope(kvh(kc_tok), positions)
    vc_tok = kvh(vc_tok)
    ks = partial_rope(kvh(ks), positions)
    kw = partial_rope(kvh(kw), positions)

    n_cmp = (s - CMP_LEN) // CMP_STRIDE + 1
    blk_idx = np.arange(n_cmp)[:, None] * CMP_STRIDE + np.arange(CMP_LEN)[None, :]

    def compress(tok, pe, w1, w2):
        blocks = tok[:, blk_idx] + pe[None, None, :, None, :]
        flat = blocks.transpose(0, 1, 3, 2, 4).reshape(bsz, n_cmp, G, CMP_LEN * HD)
        return (jax.nn.gelu(flat @ w1) @ w2).transpose(0, 2, 1, 3)

    kc = compress(kc_tok, pe_k, w1_k, w2_k)
    vc = compress(vc_tok, pe_v, w1_v, w2_v)
    cmp_end = jnp.asarray(blk_idx[:, -1], jnp.int32)

    n_slc = s // SEL_LEN
    cs = np.arange(n_cmp)[:, None] * CMP_STRIDE
    ss = np.arange(n_slc)[None, :] * SEL_LEN
    overlap = np.clip(np.minimum(cs + CMP_LEN, ss + SEL_LEN) - np.maximum(cs, ss), 0, None) / CMP_LEN
    overlap = jnp.asarray(overlap, jnp.float32)
    n_top = min(SEL_TOPK, n_slc)

    qg = q.reshape(bsz, s, G, HPG, HD).transpose(0, 2, 3, 1, 4) * (HD ** -0.5)
    ks_g = ks.transpose(0, 2, 1, 3)
    vs_g = kvh(vs).transpose(0, 2, 1, 3)
    pad = ((0, 0), (0, 0), (WINDOW, 0), (0, 0))
    kw_pad = jnp.pad(kw.transpose(0, 2, 1, 3), pad)
    vw_pad = jnp.pad(kvh(vw).transpose(0, 2, 1, 3), pad)
    gates = jax.nn.sigmoid(gate_pre.reshape(bsz, s, G, HPG, 3)).transpose(0, 2, 3, 1, 4)
    bi = jnp.arange(bsz)[:, None, None]
    gi = jnp.arange(G)[None, :, None]
    sel_off = jnp.arange(SEL_LEN)
    jblk = jnp.arange(n_slc)

    def block_fn(blk):
        q0 = blk * Q_BLOCK
        qb = lax.dynamic_slice_in_dim(qg, q0, Q_BLOCK, axis=3)
        gb = lax.dynamic_slice_in_dim(gates, q0, Q_BLOCK, axis=3)
        t = q0 + jnp.arange(Q_BLOCK)
        p_c = masked_softmax(jnp.einsum('bghqd,bgcd->bghqc', qb, kc),
                             cmp_end[None, :] <= t[:, None])
        o_c = jnp.einsum('bghqc,bgcd->bghqd', p_c.astype(vc.dtype), vc)
        imp = jnp.einsum('bghqc,cj->bgqj', p_c, overlap)
        cur = t // SEL_LEN
        causal_blk = jblk[None, :] <= cur[:, None]
        forced = (jblk[None, :] == 0) | (jblk[None, :] == cur[:, None]) | (jblk[None, :] == cur[:, None] - 1)
        imp = jnp.where(forced & causal_blk, FORCE_SCORE, imp)
        imp = jnp.where(causal_blk, imp, NEG)
        top_val, top_idx = lax.top_k(imp, n_top)
        tok_idx = top_idx[..., None] * SEL_LEN + sel_off
        tok_ok = (top_val > 0.5 * NEG)[..., None] & (tok_idx <= t[None, None, :, None, None])
        flat_idx = tok_idx.reshape(bsz, G, Q_BLOCK * n_top * SEL_LEN)
        k_sel = ks_g[bi, gi, flat_idx].reshape(bsz, G, Q_BLOCK, n_top * SEL_LEN, HD)
        v_sel = vs_g[bi, gi, flat_idx].reshape(bsz, G, Q_BLOCK, n_top * SEL_LEN, HD)
        p_s = masked_softmax(jnp.einsum('bghqd,bgqkd->bghqk', qb, k_sel),
                             tok_ok.reshape(bsz, G, 1, Q_BLOCK, n_top * SEL_LEN))
        o_s = jnp.einsum('bghqk,bgqkd->bghqd', p_s.astype(v_sel.dtype), v_sel)
        k_w = lax.dynamic_slice_in_dim(kw_pad, q0, WINDOW + Q_BLOCK, axis=2)
        v_w = lax.dynamic_slice_in_dim(vw_pad, q0, WINDOW + Q_BLOCK, axis=2)
        s_pos = q0 - WINDOW + jnp.arange(WINDOW + Q_BLOCK)
        m_w = (s_pos[None, :] <= t[:, None]) & (s_pos[None, :] > t[:, None] - WINDOW) & (s_pos[None, :] >= 0)
        p_w = masked_softmax(jnp.einsum('bghqd,bgkd->bghqk', qb, k_w), m_w)
        o_w = jnp.einsum('bghqk,bgkd->bghqd', p_w.astype(v_w.dtype), v_w)
        return gb[..., 0:1] * o_c + gb[..., 1:2] * o_s + gb[..., 2:3] * o_w

    out = lax.map(block_fn, jnp.arange(s // Q_BLOCK))
    return out.transpose(1, 0, 4, 2, 3, 5).reshape(bsz, s, B_Q_W)


def setup_inputs(seed: int = 0) -> dict:
    key = jax.random.key(seed)
    ks = jax.random.split(key, 20)
    f32 = jnp.float32

    def nrm(k, shape, scale):
        return jax.random.normal(k, shape, f32) * scale

    L = DEPTH
    x = jax.random.normal(ks[0], (BATCH, SEQ, D_MODEL), f32)
    offsets = jax.random.randint(ks[1], (BATCH, 1), 0, 2048, dtype=jnp.int32)
    positions = (offsets + jnp.arange(SEQ, dtype=jnp.int32)[None, :]).astype(jnp.int32)
    return {
        "x": x,
        "positions": positions,
        "norm1_g": 1.0 + nrm(ks[2], (L, D_MODEL), 0.02),
        "w_in": nrm(ks[3], (L, D_MODEL, N_IN), D_MODEL ** -0.5),
        "lb_param": nrm(ks[4], (L + 1, A_QK_W), 0.1),
        "hgrn_norm_g": 1.0 + nrm(ks[5], (L, A_DV), 0.02),
        "cmp_pe_k": nrm(ks[6], (L, CMP_LEN, B_HEAD_DIM), 0.1),
        "cmp_pe_v": nrm(ks[7], (L, CMP_LEN, B_HEAD_DIM), 0.1),
        "cmp_w1_k": nrm(ks[8], (L, CMP_LEN * B_HEAD_DIM, CMP_HIDDEN), (CMP_LEN * B_HEAD_DIM) ** -0.5),
        "cmp_w2_k": nrm(ks[9], (L, CMP_HIDDEN, B_HEAD_DIM), CMP_HIDDEN ** -0.5),
        "cmp_w1_v": nrm(ks[10], (L, CMP_LEN * B_HEAD_DIM, CMP_HIDDEN), (CMP_LEN * B_HEAD_DIM) ** -0.5),
        "cmp_w2_v": nrm(ks[11], (L, CMP_HIDDEN, B_HEAD_DIM), CMP_HIDDEN ** -0.5),
        "w_br_a": nrm(ks[12], (L, A_V_W, D_MODEL), A_V_W ** -0.5),
        "w_br_b": nrm(ks[13], (L, B_Q_W, D_MODEL), B_Q_W ** -0.5),
        "w_out": nrm(ks[14], (L, D_MODEL, D_MODEL), D_MODEL ** -0.5),
        "norm2_g": 1.0 + nrm(ks[15], (L, D_MODEL), 0.02),
        "w_ff1": nrm(ks[16], (L, D_MODEL, D_FF), D_MODEL ** -0.5),
        "w_ff2": nrm(ks[17], (L, D_FF, D_MODEL), D_FF ** -0.5),
        "final_g": 1.0 + nrm(ks[18], (D_MODEL,), 0.02),
    }


def reference(x, positions, norm1_g, w_in, lb_param, hgrn_norm_g, cmp_pe_k, cmp_pe_v,
              cmp_w1_k, cmp_w2_k, cmp_w1_v, cmp_w2_v, w_br_a, w_br_b, w_out,
              norm2_g, w_ff1, w_ff2, final_g):
    lb_all = jnp.cumsum(jax.nn.softmax(lb_param.astype(jnp.float32), axis=0), axis=0)
    h = x
    for l in range(DEPTH):
        xn = rmsnorm(h, norm1_g[l])
        proj = xn @ w_in[l]
        a_q, a_f, a_i, a_g, b_q, b_kv, b_gate, m_gate = jnp.split(proj, SPLIT_POINTS, axis=-1)
        kc, vc, ks_, vs_, kw_, vw_ = jnp.split(b_kv, 6, axis=-1)
        y_a = hgrn2_mixer(a_q, a_f, a_i, a_g, lb_all[l], hgrn_norm_g[l])
        y_b = nsa_mixer(b_q, kc, vc, ks_, vs_, kw_, vw_, b_gate, positions,
                        cmp_pe_k[l], cmp_pe_v[l], cmp_w1_k[l], cmp_w2_k[l], cmp_w1_v[l], cmp_w2_v[l])
        g_a, g_b = jnp.split(jax.nn.sigmoid(m_gate), 2, axis=-1)
        merged = g_a * (y_a @ w_br_a[l]) + g_b * (y_b @ w_br_b[l])
        h = h + merged @ w_out[l]
        hn = rmsnorm(h, norm2_g[l])
        h = h + jnp.square(jax.nn.relu(hn @ w_ff1[l])) @ w_ff2[l]
    return rmsnorm(h, final_g)
```

```python
import numpy as np
import ml_dtypes
import concourse.bass as bass
import concourse.mybir as mybir
from concourse.bass_utils import run_bass_kernel_spmd

F32, BF16, I32 = mybir.dt.float32, mybir.dt.bfloat16, mybir.dt.int32
AF = mybir.ActivationFunctionType
ALU = mybir.AluOpType
AX = mybir.AxisListType

S = 4096
D = 1024
NT = S // 128
EPS = 1e-6
NMAIN = 4632
BIG = 30000.0
PI = float(np.pi)


class Prog:
    def __init__(self, nc, nslots):
        self.nc = nc
        self.names = ["pe", "act", "dve", "pool", "sp"]
        self.semobj = {}
        for e in self.names:
            self.semobj[e] = nc.alloc_semaphore("sem_" + e)
        self.cnt = {e: 0 for e in self.names}
        self.known = {e: {} for e in self.names}
        self.streams = {e: [] for e in self.names}
        self.lastw = {}
        self.readers = {}
        self.slots = {}
        self.slot_next = {}
        self.qeng = {"sp": "sp", "pool": "pool", "act": "act", "pf": "pool"}
        for q, n in nslots.items():
            self.slots[q] = []
            for i in range(n):
                k = f"dq_{q}{i}"
                self.semobj[k] = nc.alloc_semaphore(k)
                self.slots[q].append([k, 0])
            self.slot_next[q] = 0

    def _deps(self, eng, r, w, same_ok):
        deps = {}

        def add(ev):
            if ev is None:
                return
            k, v = ev
            if deps.get(k, 0) < v:
                deps[k] = v

        for res in r:
            add(self.lastw.get(res))
        for res in w:
            add(self.lastw.get(res))
            for k, v in self.readers.get(res, {}).items():
                if k == eng and same_ok and eng == "pe":
                    continue
                add((k, v))
        waits = []
        for k, v in deps.items():
            if k == eng and eng == "pe":
                continue
            if self.known[eng].get(k, 0) < v:
                self.known[eng][k] = v
                waits.append((k, v))
        return waits

    def _record(self, ev, r, w):
        k, v = ev
        for res in r:
            d = self.readers.setdefault(res, {})
            if d.get(k, 0) < v:
                d[k] = v
        for res in w:
            self.lastw[res] = ev
            self.readers[res] = {}

    def capture(self, fn):
        self._cap = []
        fn()
        lst, self._cap = self._cap, None
        return lst

    def commit(self, lst):
        for kind, a in lst:
            if kind == "op":
                self.op(*a)
            else:
                self.dma(*a)

    def op(self, eng, fns, r=(), w=()):
        if not isinstance(fns, (list, tuple)):
            fns = [fns]
        if getattr(self, "_cap", None) is not None:
            self._cap.append(("op", (eng, fns, tuple(r), tuple(w))))
            return
        waits = self._deps(eng, r, w, True)
        self.cnt[eng] += 1
        ev = (eng, self.cnt[eng])
        self._record(ev, r, w)
        self.streams[eng].append((waits, list(fns), None))

    def pe(self, fns, r=(), w=()):
        self.op("pe", fns, r, w)

    def act(self, fn, r=(), w=()):
        self.op("act", fn, r, w)

    def dve(self, fn, r=(), w=()):
        self.op("dve", fn, r, w)

    def pool(self, fn, r=(), w=()):
        self.op("pool", fn, r, w)

    def dma(self, q, out, in_, r=(), w=()):
        if getattr(self, "_cap", None) is not None:
            self._cap.append(("dma", (q, out, in_, tuple(r), tuple(w))))
            return
        eng = self.qeng[q]
        waits = self._deps(eng, r, w, False)
        i = self.slot_next[q]
        self.slot_next[q] = (i + 1) % len(self.slots[q])
        slot = self.slots[q][i]
        k = slot[0]
        if slot[1] > 0 and self.known[eng].get(k, 0) < 16 * slot[1]:
            self.known[eng][k] = 16 * slot[1]
            waits.append((k, 16 * slot[1]))
        slot[1] += 1
        ev = (k, 16 * slot[1])
        self._record(ev, r, w)
        self.streams[eng].append((waits, [lambda e: e.dma_start(out=out, in_=in_)], k))

    def barrier(self):
        evs = {e: self.cnt[e] for e in self.names if self.cnt[e] > 0}
        for q in self.slots:
            if q == "pf":
                continue
            for k, n in self.slots[q]:
                if n > 0:
                    evs[k] = 16 * n
        for e in self.names:
            waits = []
            for k, v in evs.items():
                if k == e:
                    continue
                if self.known[e].get(k, 0) < v:
                    self.known[e][k] = v
                    waits.append((k, v))
            if waits:
                self.streams[e].append((waits, [], None))
        self.lastw = {r_: ev for r_, ev in self.lastw.items() if ev[0].startswith("dq_pf")}
        self.readers = {}

    def finish(self, out_res):
        waits = self._deps("sp", out_res, [], False)
        self.streams["sp"].append((waits, [], None))

    def emit(self):
        nc = self.nc
        engmap = {"pe": "tensor", "act": "scalar", "dve": "vector", "pool": "gpsimd", "sp": "sync"}
        with nc.Block() as block:
            for name in self.names:
                def body(e, name=name):
                    for waits, fns, dsem in self.streams[name]:
                        for k, v in waits:
                            e.wait_ge(self.semobj[k], v)
                        ins = None
                        for f in fns:
                            ins = f(e)
                        if ins is not None:
                            if dsem is not None:
                                ins.then_inc(self.semobj[dsem], 16)
                            else:
                                ins.then_inc(self.semobj[name], 1)
                getattr(block, engmap[name])(body)


class Mem:
    def __init__(self, nc, base=16512, cap=229344):
        self.nc = nc
        self.off = base
        self.cap = cap
        self.n = 0

    def alloc(self, shape, dtype, name=None):
        sz = {F32: 4, BF16: 2, I32: 4}[dtype]
        nb = int(np.prod(shape[1:])) * sz
        nb = (nb + 63) // 64 * 64
        self.n += 1
        h = self.nc.alloc_sbuf_tensor_at(f"{name or 't'}_{self.n}", list(shape), dtype, offset=self.off)
        self.off += nb
        assert self.off <= self.cap, f"SBUF overflow {self.off}"
        return h

    def alloc_at(self, shape, dtype, name, offset):
        self.n += 1
        return self.nc.alloc_sbuf_tensor_at(f"{name}_{self.n}", list(shape), dtype, offset=offset)

    def mark(self):
        return self.off

    def release(self, m):
        self.off = m


def bc(ap, shape):
    return ap.broadcast_to(list(shape))


def host_consts():
    c = {}
    c["ident"] = np.eye(128, dtype=np.float32).astype(ml_dtypes.bfloat16)
    t = np.arange(128)
    same = (t[:, None] // 64) == (t[None, :] // 64)
    c["U"] = (same & (t[:, None] <= t[None, :])).astype(np.float32)
    c["L"] = (same & (t[:, None] > t[None, :])).astype(np.float32)
    c["cind"] = np.stack([(t < 64), (t >= 64)], axis=1).astype(np.float32)
    inv = (500000.0 ** (-np.arange(0, 16, 2, dtype=np.float32) / 16)).astype(np.float32)
    c["invf"] = np.tile(inv[None, :], (128, 1)).astype(np.float32)
    s = np.arange(S)
    c["E"] = ((s[None, :] // 64) == np.arange(64)[:, None]).astype(np.float32).astype(ml_dtypes.bfloat16)
    q = np.arange(128)
    triA = np.where(t[:, None] <= q[None, :], 0.0, -BIG).astype(np.float32)
    triB = np.where(t[:, None] > q[None, :], 0.0, -BIG).astype(np.float32)
    c["triA"] = np.tile(triA, (1, 4)).astype(ml_dtypes.bfloat16)
    c["triB"] = np.tile(triB, (1, 4)).astype(ml_dtypes.bfloat16)
    cp = np.arange(8) - 1
    c["stair"] = ((16 * cp[None, :] + 31) <= q[:, None]).astype(np.float32)
    rel = np.arange(126) - 62
    ci = (q >= 64).astype(np.int64)
    wb = np.zeros((128, 126), np.float32)
    wb[rel[None, :] > ci[:, None]] = -1e30
    wb[(rel[None, :] == ci[:, None]) | (rel[None, :] == ci[:, None] - 1)] = 1e4
    c["WB"] = wb
    return c


CONST_DT = {"ident": BF16, "U": F32, "L": F32, "cind": F32, "invf": F32, "E": BF16, "triA": BF16,
            "triB": BF16, "stair": F32, "WB": F32}


def build(dbg=()):
    nc = bass.Bass("TRN2", target_bir_lowering=False)
    Dm = {}

    def din(name, shape, dt):
        Dm[name] = nc.dram_tensor(name, list(shape), dt, kind="ExternalInput").ap()

    def dscr(name, shape, dt):
        kind = "ExternalOutput" if name in dbg else "Internal"
        Dm[name] = nc.dram_tensor(name, list(shape), dt, kind=kind).ap()

    hc = host_consts()
    for k, v in hc.items():
        din("c_" + k, v.shape, CONST_DT[k])
    din("x", [S, D], F32)
    din("pos", [128, NT], I32)
    din("g1", [128, 8], F32)
    din("w_main", [D, NMAIN], F32)
    din("w_kv", [D, 768], F32)
    din("lb_param", [1, 1024], F32)
    din("hng", [1, 128], F32)
    din("pe_k", [64, 32], F32)
    din("pe_v", [64, 32], F32)
    din("w1_k", [2048, 256], F32)
    din("w2_k", [256, 64], F32)
    din("w1_v", [2048, 256], F32)
    din("w2_v", [256, 64], F32)
    din("w_br_a", [512, D], F32)
    din("w_br_b", [512, D], F32)
    din("w_out", [D, D], F32)
    din("g2", [128, 8], F32)
    din("w_ff1", [D, 4096], F32)
    din("w_ff2", [4096, D], F32)
    din("final_g", [1, D], F32)
    Dm["out"] = nc.dram_tensor("out", [S, D], F32, kind="ExternalOutput").ap()
    dscr("s_gap", [S, D], F32)
    dscr("s_gb", [S, D], BF16)
    dscr("s_qt", [NT, 128, 512], BF16)
    dscr("s_h", [S, D], F32)
    for nm, shp, dt in (("d_ke0", [128, S], BF16), ("d_kw", [128, S], BF16), ("d_vs", [128, NT * 130], BF16),
                        ("d_kc", [128, 256], BF16), ("d_vc", [128, 256], BF16), ("d_kcT", [128, S], BF16),
                        ("d_sin", [128, NT * 8], F32), ("d_lb", [128, 512], F32), ("d_ya", [S, 512], BF16),
                        ("d_gates", [128, NT * 24], F32), ("d_yb", [S, 512], BF16)):
        if nm in dbg:
            dscr(nm, shp, dt)

    P = Prog(nc, {"sp": 8, "pool": 4, "act": 2, "pf": 40})
    M = Mem(nc)
    psb = [nc.alloc_psum_tensor(f"psb{i}", [128, 512], F32) for i in range(8)]
    PSN = [f"ps{i}" for i in range(8)]

    def load_weight_cast0(dst, dres, src, ncols, nk=8, q="pf"):
        for k in range(nk):
            c0 = 0
            while c0 < ncols:
                cw = min(2048, ncols - c0)
                P.dma(q, dst[:, k, c0:c0 + cw], src[k * 128:(k + 1) * 128, c0:c0 + cw], w=[dres])
                c0 += cw
    def make_pieces(dst, src, ncols, nk, pw):
        out = []
        for k in range(nk):
            c0 = 0
            while c0 < ncols:
                cw = min(pw, ncols - c0)
                out.append((dst[:, k, c0:c0 + cw], src[k * 128:(k + 1) * 128, c0:c0 + cw], cw))
                c0 += cw
        return out

    ring_ctr = [0]

    def issue_piece(piece, dres, eng, ring, ring_names):
        dst, src, cw = piece
        i = ring_ctr[0] % len(ring)
        ring_ctr[0] += 1
        st, sn = ring[i], ring_names[i]
        P.dma("sp", st[:, 0:cw], src, w=[sn])
        if eng == "act":
            P.act(lambda e: e.activation(out=dst, in_=st[:, 0:cw], func=AF.Copy), r=[sn], w=[dres])
        elif eng == "pool":
            P.pool(lambda e: e.tensor_copy(out=dst, in_=st[:, 0:cw]), r=[sn], w=[dres])
        else:
            P.dve(lambda e: e.tensor_copy(out=dst, in_=st[:, 0:cw]), r=[sn], w=[dres])

    TOP = (229344 - (74112 + 8192)) // 64 * 64
    M.cap = TOP
    wm = M.alloc_at([128, 8, NMAIN], BF16, "wm", TOP)
    wbra = M.alloc_at([128, 4, D], BF16, "wbra", TOP + 74112)
    wm_pieces = make_pieces(wm, Dm["w_main"], NMAIN, 8, 2048) + make_pieces(wbra, Dm["w_br_a"], D, 4, 2048)
    C = {}
    for k, v in hc.items():
        if k == "E":
            continue
        C[k] = M.alloc(list(v.shape), CONST_DT[k], "c_" + k)
        P.dma("sp", C[k][:], Dm["c_" + k], w=["c_" + k])
    ident = C["ident"]
    m_consts = M.mark()
    SIN = M.alloc([128, NT, 8], F32, "SIN")
    COS = M.alloc([128, NT, 8], F32, "COS")
    RSTD1 = M.alloc([128, NT], F32, "RSTD1")
    GATES = M.alloc([128, NT, 24], F32, "GATES")
    KE = [M.alloc([128, S], BF16, "KE0"), M.alloc([128, S], BF16, "KE1")]
    KW = M.alloc([128, S], BF16, "KW")
    VS = M.alloc([128, NT, 2, 65], BF16, "VS")
    VW = M.alloc([128, NT, 2, 65], BF16, "VW")
    KC = M.alloc([128, 256], BF16, "KC")
    VC = M.alloc([128, 2, 2, 64], BF16, "VC")
    m_lb = M.mark()
    lbt = M.alloc([128, 512], F32, "lb")
    omlt = M.alloc([128, 512], F32, "oml")
    hng = M.alloc([128, 128], F32, "hng")
    g1t = M.alloc([128, 8], F32, "g1t")
    g2t = M.alloc([128, 8], F32, "g2t")
    P.dma("sp", g1t[:], Dm["g1"], w=["g1t"])
    P.dma("sp", g2t[:], Dm["g2"], w=["g2t"])
    P.dma("sp", hng[:], Dm["hng"].partition_broadcast(128), w=["hng"])
    P.dma("sp", KE[0][64:128, :], Dm["c_E"], w=["KE0e"])
    P.dma("sp", KE[1][0:64, :], Dm["c_E"], w=["KE1e"])
    P.pool(lambda e: e.memset(VS[:, :, :, 64:65], 1.0), w=["VSone"])
    P.pool(lambda e: e.memset(VW[:, :, :, 64:65], 1.0), w=["VWone"])
    P.pool(lambda e: e.memset(KC[:], 0.0), w=["KC"])

    m_phase = M.mark()

    posi = M.alloc([128, NT], I32, "posi")
    posf = M.alloc([128, NT], F32, "posf")
    ang = M.alloc([128, NT, 8], F32, "ang")
    a2 = M.alloc([128, NT, 8], F32, "a2")
    nfi = M.alloc([128, NT, 8], I32, "nfi")
    nff = M.alloc([128, NT, 8], F32, "nff")
    msk = M.alloc([128, NT, 8], F32, "msk")
    lbp = M.alloc([128, 1024], F32, "lbp")
    P.dma("sp", posi[:], Dm["pos"], w=["posi"])
    P.dma("sp", lbp[:], Dm["lb_param"].partition_broadcast(128), w=["lbp"])
    P.dve(lambda e: e.tensor_copy(out=posf[:], in_=posi[:]), r=["posi"], w=["posf"])
    P.dve(lambda e: e.tensor_tensor(out=ang[:], in0=bc(posf[:].unsqueeze(2), [128, NT, 8]),
                                    in1=bc(C["invf"][:].unsqueeze(1), [128, NT, 8]), op=ALU.mult),
          r=["posf", "c_invf"], w=["ang"])
    C1 = 6.28125
    C2 = 2.0 * PI - C1
    for tab, shift, nm in ((SIN, 0.0, "SIN"), (COS, PI / 2, "COS")):
        P.dve(lambda e, shift=shift: e.tensor_scalar(out=a2[:], in0=ang[:], scalar1=shift, scalar2=None, op0=ALU.add),
              r=["ang"], w=["a2"])
        P.dve(lambda e: e.tensor_scalar(out=nff[:], in0=a2[:], scalar1=1.0 / (2 * PI), scalar2=None, op0=ALU.mult),
              r=["a2"], w=["nff"])
        P.dve(lambda e: e.tensor_copy(out=nfi[:], in_=nff[:]), r=["nff"], w=["nfi"])
        P.dve(lambda e: e.tensor_copy(out=nff[:], in_=nfi[:]), r=["nfi"], w=["nff"])
        P.dve(lambda e: e.scalar_tensor_tensor(out=a2[:], in0=nff[:], scalar=-C1, in1=a2[:], op0=ALU.mult, op1=ALU.add),
              r=["nff", "a2"], w=["a2"])
        P.dve(lambda e: e.scalar_tensor_tensor(out=a2[:], in0=nff[:], scalar=-C2, in1=a2[:], op0=ALU.mult, op1=ALU.add),
              r=["nff", "a2"], w=["a2"])
        P.dve(lambda e: e.tensor_scalar(out=msk[:], in0=a2[:], scalar1=PI, scalar2=None, op0=ALU.is_gt), r=["a2"], w=["msk"])
        P.dve(lambda e: e.scalar_tensor_tensor(out=a2[:], in0=msk[:], scalar=-2 * PI, in1=a2[:], op0=ALU.mult, op1=ALU.add),
              r=["msk", "a2"], w=["a2"])
        P.dve(lambda e: e.tensor_scalar(out=msk[:], in0=a2[:], scalar1=-PI, scalar2=None, op0=ALU.is_lt), r=["a2"], w=["msk"])
        P.dve(lambda e: e.scalar_tensor_tensor(out=a2[:], in0=msk[:], scalar=2 * PI, in1=a2[:], op0=ALU.mult, op1=ALU.add),
              r=["msk", "a2"], w=["a2"])
        P.dve(lambda e: e.tensor_scalar(out=a2[:], in0=a2[:], scalar1=-PI, scalar2=PI, op0=ALU.max, op1=ALU.min),
              r=["a2"], w=["a2"])
        P.act(lambda e, tab=tab: e.activation(out=tab[:], in_=a2[:], func=AF.Sin), r=["a2"], w=[nm])
    P.dve(lambda e: e.tensor_tensor(out=lbt[:], in0=lbp[:, 0:512], in1=lbp[:, 512:1024], op=ALU.subtract), r=["lbp"], w=["lb"])
    P.act(lambda e: e.activation(out=lbt[:], in_=lbt[:], func=AF.Exp, scale=-1.0), r=["lb"], w=["lb"])
    P.dve(lambda e: e.tensor_scalar(out=lbt[:], in0=lbt[:], scalar1=1.0, scalar2=None, op0=ALU.add), r=["lb"], w=["lb"])
    P.dve(lambda e: e.reciprocal(out=lbt[:], in_=lbt[:]), r=["lb"], w=["lb"])
    P.dve(lambda e: e.tensor_scalar(out=omlt[:], in0=lbt[:], scalar1=-1.0, scalar2=1.0, op0=ALU.mult, op1=ALU.add),
          r=["lb"], w=["oml"])
    P.dve(lambda e: e.scalar_tensor_tensor(out=lbt[:], in0=omlt[:], scalar=0.5, in1=lbt[:], op0=ALU.mult, op1=ALU.add),
          r=["lb", "oml"], w=["lb"])
    P.dve(lambda e: e.tensor_scalar(out=omlt[:], in0=omlt[:], scalar1=0.5, scalar2=None, op0=ALU.mult), r=["oml", "lb"], w=["oml"])
    if "d_sin" in dbg:
        P.dma("sp", Dm["d_sin"], SIN[:].rearrange("p a b -> p (a b)"), r=["SIN"], w=["d_sin"])
    if "d_lb" in dbg:
        P.dma("sp", Dm["d_lb"], lbt[:], r=["lb"], w=["d_lb"])
    P.barrier()
    M.release(m_phase)

    def load_weight_cast(dst, dres, src, ncols, nk=8, q="pool"):
        for k in range(nk):
            c0 = 0
            while c0 < ncols:
                cw = min(2048, ncols - c0)
                P.dma(q, dst[:, k, c0:c0 + cw], src[k * 128:(k + 1) * 128, c0:c0 + cw], w=[dres])
                c0 += cw

    def load_weight_scaled(dst, dres, src, gcol, gres, ncols, stage, stage_names):
        i = 0
        for k in range(8):
            c0 = 0
            while c0 < ncols:
                cw = min(2048, ncols - c0)
                st = stage[i % 2]
                sn = stage_names[i % 2]
                P.dma("sp", st[:, 0:cw], src[k * 128:(k + 1) * 128, c0:c0 + cw], w=[sn])
                eng = P.dve if i % 2 == 0 else P.pool
                eng(lambda e, st=st, k=k, c0=c0, cw=cw: e.tensor_scalar(
                    out=dst[:, k, c0:c0 + cw], in0=st[:, 0:cw], scalar1=gcol[:, k:k + 1], scalar2=None, op0=ALU.mult),
                    r=[sn, gres], w=[dres])
                c0 += cw
                i += 1

    def x_tile_prep(t, XT, xb, xT, psbank, want_rstd, xTn="xT"):
        xt = XT[t % 2]
        xn = f"xt{t % 2}"
        P.dma("sp", xt[:], Dm["x"][t * 128:(t + 1) * 128, :], w=[xn])
        if want_rstd:
            P.act(lambda e: e.activation(out=junk[:], in_=xt[:], func=AF.Square, accum_out=ssq[:, 0:1]),
                  r=[xn], w=["junk", "ssq"])
            P.act(lambda e: e.activation(out=ssq[:, 1:2], in_=ssq[:, 0:1], func=AF.Ln, scale=1.0 / D, bias=epsb[:, 0:1]),
                  r=["ssq", "epsb"], w=["ssq2"])
            P.act(lambda e: e.activation(out=RSTD1[:, t:t + 1], in_=ssq[:, 1:2], func=AF.Exp, scale=-0.5),
                  r=["ssq2"], w=["RSTD1"])
        P.dve(lambda e: e.tensor_copy(out=xb[:, 0:512], in_=xt[:, 0:512]), r=[xn], w=["xb"])
        P.pool(lambda e: e.tensor_copy(out=xb[:, 512:1024], in_=xt[:, 512:1024]), r=[xn], w=["xb2"])
        pv = psb[psbank][:].bitcast(BF16)
        P.pe([lambda e, k=k: e.transpose(out=pv[:, k * 128:(k + 1) * 128], in_=xb[:, k * 128:(k + 1) * 128], identity=ident[:])
              for k in range(8)], r=["xb", "xb2", "c_ident"], w=[PSN[psbank]])
        P.dve(lambda e: e.tensor_tensor(out=xT[:], in0=pv[:, 0:1024].rearrange("p (k t) -> p k t", k=8),
                                        in1=bc(g1t[:].unsqueeze(2), [128, 8, 128]), op=ALU.mult),
              r=[PSN[psbank], "g1t"], w=[xTn])

    epsb = M.alloc([128, 1], F32, "epsb")
    ssq = M.alloc([128, 2], F32, "ssq")
    junk = M.alloc([128, D], BF16, "junk")
    P.pool(lambda e: e.memset(epsb[:], EPS), w=["epsb"])
    m_phase = M.mark()

    wkv = M.alloc([128, 8, 768], BF16, "wkv")
    kcT = M.alloc([128, S], BF16, "kcT")
    vcT = M.alloc([128, S], BF16, "vcT")
    m_p1 = M.mark()
    stage = [M.alloc([128, 2048], F32, "stg0"), M.alloc([128, 2048], F32, "stg1")]
    stage_n = ["stgA", "stgB"]
    XT = [M.alloc([128, D], F32, "XT0"), M.alloc([128, D], F32, "XT1")]
    xb = M.alloc([128, D], BF16, "xb")
    xT = M.alloc([128, 8, 128], BF16, "xT")
    kvs = M.alloc([128, 768], F32, "kvs")
    kvb = M.alloc([128, 768], BF16, "kvb")
    rt = M.alloc([128, 4, 48], F32, "rt")
    load_weight_cast(wkv, "wkv", Dm["w_kv"], 768)
    def p1_tile(t):
        x_tile_prep(t, XT, xb, xT, 0, True)
        for grp in range(2):
            bk = 1 + grp
            P.pe([lambda e, k=k, grp=grp, bk=bk: e.matmul(psb[bk][:, 0:384], lhsT=xT[:, k, :], rhs=wkv[:, k, grp * 384:(grp + 1) * 384],
                                                           start=(k == 0), stop=(k == 7)) for k in range(8)],
                 r=["xT", "wkv"], w=[PSN[bk]])
            P.act(lambda e, grp=grp, bk=bk: e.activation(out=kvs[:, grp * 384:(grp + 1) * 384], in_=psb[bk][:, 0:384], func=AF.Copy,
                                                         scale=RSTD1[:, t:t + 1]), r=[PSN[bk], "RSTD1"], w=[f"kvs{grp}"])
        K4 = kvs[:].rearrange("p (a g d) -> p a g d", a=6, g=2)
        t1 = K4[:, 0:6:2, :, 0:8]
        t2 = K4[:, 0:6:2, :, 8:16]
        cosb = bc(COS[:, t, :].unsqueeze(1).unsqueeze(1), [128, 3, 2, 8])
        sinb = bc(SIN[:, t, :].unsqueeze(1).unsqueeze(1), [128, 3, 2, 8])
        rts = [rt[:, i, :].rearrange("p (a g d) -> p a g d", a=3, g=2) for i in range(4)]
        kr = ["kvs0", "kvs1", "SIN", "COS"]
        P.dve(lambda e: e.tensor_tensor(out=rts[0], in0=t1, in1=cosb, op=ALU.mult), r=kr, w=["rt0"])
        P.dve(lambda e: e.tensor_tensor(out=rts[1], in0=t2, in1=sinb, op=ALU.mult), r=kr, w=["rt1"])
        P.dve(lambda e: e.tensor_tensor(out=rts[2], in0=t2, in1=cosb, op=ALU.mult), r=kr, w=["rt2"])
        P.dve(lambda e: e.tensor_tensor(out=rts[3], in0=t1, in1=sinb, op=ALU.mult), r=kr, w=["rt3"])
        P.dve(lambda e: e.tensor_tensor(out=t1, in0=rts[0], in1=rts[1], op=ALU.subtract), r=["rt0", "rt1", "rt2", "rt3"], w=["kvs0", "kvs1"])
        P.dve(lambda e: e.tensor_tensor(out=t2, in0=rts[2], in1=rts[3], op=ALU.add), r=["rt2", "rt3"], w=["kvs0", "kvs1"])
        P.dve(lambda e: e.tensor_copy(out=kvb[:], in_=kvs[:]), r=["kvs0", "kvs1"], w=["kvb"])
        pv = psb[3][:].bitcast(BF16)
        srcs = [0, 128, 256, 512]
        P.pe([lambda e, i=i: e.transpose(out=pv[:, i * 128:(i + 1) * 128], in_=kvb[:, srcs[i]:srcs[i] + 128], identity=ident[:])
              for i in range(4)], r=["kvb", "c_ident"], w=[PSN[3]])
        cs = slice(t * 128, (t + 1) * 128)
        P.act(lambda e, cs=cs: e.activation(out=kcT[:, cs], in_=pv[:, 0:128], func=AF.Copy), r=[PSN[3]], w=["kcT"])
        P.act(lambda e, cs=cs: e.activation(out=vcT[:, cs], in_=pv[:, 128:256], func=AF.Copy), r=[PSN[3]], w=["vcT"])
        P.dve(lambda e, cs=cs: e.tensor_copy(out=KE[0][0:64, cs], in_=pv[0:64, 256:384]), r=[PSN[3]], w=["KE0"])
        P.dve(lambda e, cs=cs: e.tensor_copy(out=KE[1][64:128, cs], in_=pv[64:128, 256:384]), r=[PSN[3]], w=["KE1"])
        P.act(lambda e, cs=cs: e.activation(out=KW[:, cs], in_=pv[:, 384:512], func=AF.Copy), r=[PSN[3]], w=["KW"])
        P.pool(lambda e, t=t: e.tensor_copy(out=VS[:, t, :, 0:64], in_=kvb[:, 384:512].rearrange("p (g d) -> p g d", g=2)),
               r=["kvb"], w=["VS"])
        P.pool(lambda e, t=t: e.tensor_copy(out=VW[:, t, :, 0:64], in_=kvb[:, 640:768].rearrange("p (g d) -> p g d", g=2)),
               r=["kvb"], w=["VW"])

    for t in range(NT):
        if t < len(wm_pieces):
            issue_piece(wm_pieces[t], "wmres", "act" if t % 2 == 0 else "pool", stage, stage_n)
        p1_tile(t)
    for t in range(NT, len(wm_pieces)):
        issue_piece(wm_pieces[t], "wmres", "act", stage, stage_n)
    P.barrier()
    M.release(m_p1)

    W1 = [M.alloc([128, 32, 256], BF16, "W1k"), M.alloc([128, 32, 256], BF16, "W1v")]
    W2f = M.alloc([128, 2, 2, 64], F32, "W2f")
    W2k = M.alloc([128, 2, 2, 128], BF16, "W2k")
    W2v = M.alloc([128, 2, 64], BF16, "W2v")
    pef = M.alloc([64, 2, 32], F32, "pef")
    peb = M.alloc([64, 2, 32], BF16, "peb")
    cst = M.alloc([128, 2, 2], F32, "cst")
    hx = M.alloc([128, 256], F32, "hx")
    hu = M.alloc([128, 256], F32, "hu")
    hid = M.alloc([128, 2, 2, 2, 256], BF16, "hid")
    for kv, (w1n, w2n, pen) in enumerate((("w1_k", "w2_k", "pe_k"), ("w1_v", "w2_v", "pe_v"))):
        w1v = Dm[w1n].rearrange("(l d) j -> d l j", d=64)
        for hh in range(2):
            for lc in range(4):
                P.dma("pool", W1[kv][hh * 64:(hh + 1) * 64, lc * 8:(lc + 1) * 8, :], w1v[:, lc * 8:(lc + 1) * 8, :], w=[f"W1_{kv}"])
        P.dma("sp", W2f[:, kv, :, :], Dm[w2n].rearrange("(h j) d -> j h d", j=128), w=[f"W2f{kv}"])
        P.dma("sp", pef[:, kv, :], Dm[pen], w=[f"pef{kv}"])
    P.pool(lambda e: e.memset(W2k[:], 0.0), w=["W2k"])
    P.dve(lambda e: e.tensor_copy(out=W2k[:, 0, :, 0:64], in_=W2f[:, 0, :, :]), r=["W2f0", "W2k"], w=["W2k"])
    P.dve(lambda e: e.tensor_copy(out=W2k[:, 1, :, 64:128], in_=W2f[:, 0, :, :]), r=["W2f0", "W2k"], w=["W2k"])
    P.dve(lambda e: e.tensor_copy(out=W2v[:], in_=W2f[:, 1, :, :]), r=["W2f1"], w=["W2v"])
    P.dve(lambda e: e.tensor_copy(out=peb[:], in_=pef[:]), r=["pef0", "pef1"], w=["peb"])
    for kv in range(2):
        for half in range(2):
            col = kv * 2 + half
            P.pe([lambda e, kv=kv, half=half, col=col, m=m: e.matmul(psb[4][:, col:col + 1], lhsT=W1[kv][0:64, m, half * 128:(half + 1) * 128],
                                                                    rhs=peb[:, kv, m:m + 1], start=(m == 0), stop=(m == 31))
                  for m in range(32)], r=[f"W1_{kv}", "peb"], w=[PSN[4]])
            P.act(lambda e, kv=kv, half=half, col=col: e.activation(out=cst[:, kv, half:half + 1], in_=psb[4][:, col:col + 1], func=AF.Copy),
                  r=[PSN[4]], w=["cst"])
    tokT = [kcT, vcT]
    for kv in range(2):
        for g in range(2):
            for half in range(2):
                bk = g * 2 + half
                pr = slice(g * 64, (g + 1) * 64)
                P.pe([lambda e, kv=kv, half=half, bk=bk, pr=pr, l=l: e.matmul(
                    psb[bk][:, 0:255], lhsT=W1[kv][pr, l, half * 128:(half + 1) * 128],
                    rhs=tokT[kv][pr, l:l + 16 * 254 + 1:16], start=(l == 0), stop=(l == 31)) for l in range(32)],
                    r=[f"W1_{kv}", "kcT", "vcT"], w=[PSN[bk]])
                P.act(lambda e, kv=kv, half=half, bk=bk: e.activation(out=hx[:, 0:255], in_=psb[bk][:, 0:255], func=AF.Identity,
                                                                      bias=cst[:, kv, half:half + 1]), r=[PSN[bk], "cst"], w=["hx"])
                P.dve(lambda e: e.tensor_tensor(out=hu[:, 0:255], in0=hx[:, 0:255], in1=hx[:, 0:255], op=ALU.mult), r=["hx"], w=["hu"])
                P.dve(lambda e: e.tensor_scalar(out=hu[:, 0:255], in0=hu[:, 0:255], scalar1=0.044715, scalar2=1.0, op0=ALU.mult, op1=ALU.add),
                      r=["hu"], w=["hu"])
                P.dve(lambda e: e.tensor_tensor(out=hu[:, 0:255], in0=hu[:, 0:255], in1=hx[:, 0:255], op=ALU.mult), r=["hu", "hx"], w=["hu"])
                P.act(lambda e: e.activation(out=hu[:, 0:255], in_=hu[:, 0:255], func=AF.Exp, scale=-1.5957691216), r=["hu"], w=["hu"])
                P.dve(lambda e: e.tensor_scalar(out=hu[:, 0:255], in0=hu[:, 0:255], scalar1=1.0, scalar2=None, op0=ALU.add), r=["hu"], w=["hu"])
                P.dve(lambda e: e.reciprocal(out=hu[:, 0:255], in_=hu[:, 0:255]), r=["hu"], w=["hu"])
                P.dve(lambda e, kv=kv, g=g, half=half: e.tensor_tensor(out=hid[:, kv, g, half, 0:255], in0=hu[:, 0:255], in1=hx[:, 0:255], op=ALU.mult),
                      r=["hu", "hx"], w=["hid"])
    mm = []
    for g in range(2):
        for half in range(2):
            first = (g == 0 and half == 0)
            last = (g == 1 and half == 1)
            mm.append(lambda e, g=g, half=half, first=first, last=last: e.matmul(
                psb[4][:, 0:255], lhsT=W2k[:, g, half, :], rhs=hid[:, 0, g, half, 0:255], start=first, stop=last))
    P.pe(mm, r=["W2k", "hid"], w=[PSN[4]])
    P.act(lambda e: e.activation(out=KC[:, 0:255], in_=psb[4][:, 0:255], func=AF.Copy), r=[PSN[4], "KC"], w=["KC"])
    for ch in range(2):
        cn = 128 if ch == 0 else 127
        for g in range(2):
            P.pe([lambda e, ch=ch, cn=cn, g=g, half=half: e.matmul(
                psb[5][0:cn, (ch * 2 + g) * 64:(ch * 2 + g + 1) * 64], lhsT=hid[:, 1, g, half, ch * 128:ch * 128 + cn],
                rhs=W2v[:, half, :], start=(half == 0), stop=(half == 1)) for half in range(2)],
                r=["hid", "W2v"], w=[PSN[5]])
    P.pool(lambda e: e.memset(VC[:], 0.0), w=["VC"])
    P.act(lambda e: e.activation(out=VC[:, 0, :, :], in_=psb[5][:, 0:128].rearrange("p (g d) -> p g d", g=2), func=AF.Copy),
          r=[PSN[5], "VC"], w=["VC"])
    P.act(lambda e: e.activation(out=VC[0:127, 1, :, :], in_=psb[5][0:127, 128:256].rearrange("p (g d) -> p g d", g=2), func=AF.Copy),
          r=[PSN[5], "VC"], w=["VC"])
    for nm, src in (("d_ke0", KE[0][:]), ("d_kw", KW[:]), ("d_vs", VS[:].rearrange("p a g d -> p (a g d)")),
                    ("d_kc", KC[:]), ("d_vc", VC[:].rearrange("p a g d -> p (a g d)")), ("d_kcT", kcT[:])):
        if nm in dbg:
            P.dma("sp", Dm[nm], src, r=["KE0", "KE0e", "KW", "VS", "VSone", "KC", "VC", "kcT"], w=[nm])
    P.barrier()
    M.release(m_phase)

    ctx = dict(make_pieces=make_pieces, issue_piece=issue_piece, m_lb=m_lb, wm=wm, wbra=wbra, TOP=TOP, load_weight_cast=load_weight_cast, m_consts=m_consts, nc=nc, P=P, M=M, Dm=Dm, psb=psb, PSN=PSN, C=C, SIN=SIN, COS=COS, RSTD1=RSTD1, GATES=GATES, KE=KE, KW=KW,
               VS=VS, VW=VW, KC=KC, VC=VC, lbt=lbt, omlt=omlt, hng=hng, g1t=g1t, g2t=g2t, dbg=dbg,
               x_tile_prep=x_tile_prep, load_weight_scaled=load_weight_scaled, epsb=epsb, ssq=ssq)
    return ctx


def finish(ctx, out_res):
    P = ctx["P"]
    P.finish(out_res)
    P.emit()
    return ctx["nc"]


def prep_inputs(inputs, b):
    f = np.float32
    w_in = np.asarray(inputs["w_in"][0], f)
    a = w_in[:, 0:2048]
    bq = w_in[:, 2048:2560].reshape(D, 2, 4, 64).transpose(0, 2, 1, 3).reshape(D, 512)
    bkv = w_in[:, 2560:3328]
    bg = w_in[:, 3328:3352]
    mg = w_in[:, 3352:5400]
    w_main = np.ascontiguousarray(np.concatenate([a, bq, bg, mg], axis=1))
    m = {}
    for k, v in host_consts().items():
        m["c_" + k] = v
    m["x"] = np.ascontiguousarray(inputs["x"][b], f)
    m["pos"] = np.ascontiguousarray(np.asarray(inputs["positions"][b], np.int32).reshape(NT, 128).T)
    m["g1"] = np.ascontiguousarray(np.asarray(inputs["norm1_g"][0], f).reshape(8, 128).T)
    m["w_main"] = w_main
    m["w_kv"] = np.ascontiguousarray(bkv)
    m["lb_param"] = np.ascontiguousarray(np.asarray(inputs["lb_param"], f).reshape(1, 1024))
    m["hng"] = np.ascontiguousarray(np.asarray(inputs["hgrn_norm_g"][0], f).reshape(1, 128))
    m["pe_k"] = np.ascontiguousarray(np.asarray(inputs["cmp_pe_k"][0], f).T)
    m["pe_v"] = np.ascontiguousarray(np.asarray(inputs["cmp_pe_v"][0], f).T)
    m["w1_k"] = np.ascontiguousarray(inputs["cmp_w1_k"][0], f)
    m["w2_k"] = np.ascontiguousarray(inputs["cmp_w2_k"][0], f)
    m["w1_v"] = np.ascontiguousarray(inputs["cmp_w1_v"][0], f)
    m["w2_v"] = np.ascontiguousarray(inputs["cmp_w2_v"][0], f)
    m["w_br_a"] = np.ascontiguousarray(inputs["w_br_a"][0], f)
    m["w_br_b"] = np.ascontiguousarray(inputs["w_br_b"][0], f)
    m["w_out"] = np.ascontiguousarray(inputs["w_out"][0], f)
    m["g2"] = np.ascontiguousarray(np.asarray(inputs["norm2_g"][0], f).reshape(8, 128).T)
    m["w_ff1"] = np.ascontiguousarray(inputs["w_ff1"][0], f)
    m["w_ff2"] = np.ascontiguousarray(inputs["w_ff2"][0], f)
    m["final_g"] = np.ascontiguousarray(np.asarray(inputs["final_g"], f).reshape(1, D))
    return m


def phase3a(ctx):
    nc, P, M, Dm, psb, PSN, C = (ctx[k] for k in ("nc", "P", "M", "Dm", "psb", "PSN", "C"))
    SIN, COS, RSTD1, GATES, lbt, omlt, hng, g1t = (ctx[k] for k in ("SIN", "COS", "RSTD1", "GATES", "lbt", "omlt", "hng", "g1t"))
    dbg = ctx["dbg"]
    ident = C["ident"]
    m0 = M.mark()
    wm, wbra = ctx["wm"], ctx["wbra"]
    NRS = M.alloc([128, NT], F32, "NRS")
    RS8 = M.alloc([128, NT], F32, "RS8")
    XT = [M.alloc([128, D], F32, "XT0"), M.alloc([128, D], F32, "XT1")]
    xb = M.alloc([128, D], BF16, "xb")
    xTs = [M.alloc([128, 8, 128], BF16, "xTa"), M.alloc([128, 8, 128], BF16, "xTb")]
    HRS = M.alloc([128, NT], F32, "HRS")
    P.dve(lambda e: e.tensor_scalar(out=NRS[:], in0=RSTD1[:], scalar1=-1.0, scalar2=None, op0=ALU.mult), r=["RSTD1"], w=["NRS"])
    P.dve(lambda e: e.tensor_scalar(out=HRS[:], in0=RSTD1[:], scalar1=0.5, scalar2=None, op0=ALU.mult), r=["RSTD1"], w=["HRS"])
    P.dve(lambda e: e.tensor_scalar(out=RS8[:], in0=RSTD1[:], scalar1=0.125, scalar2=None, op0=ALU.mult), r=["RSTD1"], w=["RS8"])

    def A(shape, dt, nm):
        return M.alloc(shape, dt, nm)

    def A2(shape, dt, nm):
        return [M.alloc(shape, dt, nm + "0"), M.alloc(shape, dt, nm + "1")]
    ft = A([128, 512], F32, "ft")
    kk = A([128, 512], F32, "kk")
    lf = A([128, 512], F32, "lf")
    eb = A([128, 512], F32, "eb")
    enb = A([128, 512], F32, "enb")
    eb2 = A([128, 512], F32, "eb2")
    egs = A2([128, 512], F32, "eg")
    qds = A2([128, 512], BF16, "qd")
    kds = A2([128, 512], BF16, "kd")
    kd2s = [A2([128, 512], BF16, "kd2a"), A2([128, 512], BF16, "kd2b")]
    vbs = A2([128, 512], BF16, "vb")
    decs = A2([128, 8], F32, "dec")
    qTf = A([128, 4, 128], BF16, "qTf")
    qTc = [A([128, 4, 128], BF16, "qT1"), A([128, 4, 128], BF16, "qT2")]
    kT = A([128, 4, 128], BF16, "kT")
    Asb = A([128, 4, 128], BF16, "Asb")
    Sf = A([128, 4, 128], F32, "Sf")
    Sbf = [A([128, 4, 128], BF16, "Sbf0"), A([128, 4, 128], BF16, "Sbf1")]
    oss = A([128, 8], F32, "oss")
    junk = A([128, 128], F32, "junk3")
    ya = A([128, 512], BF16, "ya")
    yaT = A([128, 4, 128], BF16, "yaT")
    sga = A([128, D], F32, "sga")
    sgb = A([128, D], F32, "sgb")
    gap_ = A([128, D], F32, "gap0")
    gap = [gap_, gap_]
    qn = A([128, 512], F32, "qn")
    qnb = A([128, 512], BF16, "qnb")
    qTt_ = A([128, 512], BF16, "qTt0")
    qTt = [qTt_, qTt_]
    rt = A([128, 4, 64], F32, "rt3")
    gt = A([128, 24], F32, "gt")
    P.pool(lambda e: e.memset(qTc[0][:], 0.0), w=["qT1"])
    P.pool(lambda e: e.memset(qTc[1][:], 0.0), w=["qT2"])
    P.pool(lambda e: e.memset(Sf[:], 0.0), w=["Sf"])
    P.pool(lambda e: e.memset(Sbf[0][:], 0.0), w=["Sbf0"])
    Ub = bc(C["U"][:].unsqueeze(1), [128, 4, 128])
    hngb = bc(hng[:].unsqueeze(1), [128, 4, 128])

    def mkproj(xT, xTn):
        def proj(bank, c0, n):
            P.pe([lambda e, k=k: e.matmul(psb[bank][:, 0:n], lhsT=xT[:, k, :], rhs=wm[:, k, c0:c0 + n], start=(k == 0), stop=(k == 7))
                  for k in range(8)], r=[xTn, "wm"], w=[PSN[bank]])
        return proj

    def partA(t):
        p = t % 2
        xT = xTs[p]
        xTn = f"xT{p}"
        ctx["x_tile_prep"](t, XT, xb, xT, 0, False, xTn)
        proj = mkproj(xT, xTn)
        rs = RSTD1[:, t:t + 1]
        nrs = NRS[:, t:t + 1]
        hrs = HRS[:, t:t + 1]
        eg, qd, kd, vb, dec = egs[p], qds[p], kds[p], vbs[p], decs[p]
        kd2 = [kd2s[0][p], kd2s[1][p]]
        BF, BQ, BI, BG = 0, 1, 2, 3
        proj(BF, 512, 512)
        proj(BQ, 0, 512)
        proj(BI, 1024, 512)
        proj(BG, 1536, 512)
        P.act(lambda e: e.activation(out=ft[:], in_=psb[BF][:], func=AF.Tanh, scale=hrs), r=[PSN[BF], "HRS"], w=["ft"])
        P.act(lambda e: e.activation(out=vb[:], in_=psb[BI][:], func=AF.Copy, scale=rs), r=[PSN[BI], "RSTD1"], w=[f"vb{p}"])
        P.act(lambda e: e.activation(out=eg[:], in_=psb[BG][:], func=AF.Tanh, scale=hrs), r=[PSN[BG], "HRS"], w=[f"eg{p}"])
        P.dve(lambda e: e.tensor_tensor(out=ft[:], in0=ft[:], in1=omlt[:], op=ALU.mult), r=["ft", "oml"], w=["ft"])
        P.pool(lambda e: e.tensor_tensor(out=ft[:], in0=ft[:], in1=lbt[:], op=ALU.add), r=["ft", "lb"], w=["ft"])
        P.pool(lambda e: e.tensor_scalar(out=kk[:], in0=ft[:], scalar1=-1.0, scalar2=1.0, op0=ALU.mult, op1=ALU.add), r=["ft"], w=["kk"])
        P.act(lambda e: e.activation(out=lf[:], in_=ft[:], func=AF.Ln), r=["ft"], w=["lf"])
        P.dve(lambda e: e.scalar_tensor_tensor(out=eg[:], in0=eg[:], scalar=1.0, in1=psb[BG][:], op0=ALU.add, op1=ALU.mult),
              r=[PSN[BG], f"eg{p}"], w=[f"eg{p}"])
        P.pool(lambda e: e.tensor_tensor(out=eg[:].rearrange("p (h v) -> p h v", h=4), in0=eg[:].rearrange("p (h v) -> p h v", h=4),
                                         in1=hngb, op=ALU.mult), r=[f"eg{p}", "hng"], w=[f"eg{p}"])
        P.pe([lambda e: e.matmul(psb[BF][:], lhsT=C["U"][:], rhs=lf[:], start=True, stop=True)], r=["lf", "c_U"], w=[PSN[BF]])
        P.pe([lambda e: e.matmul(psb[BI][:], lhsT=C["L"][:], rhs=lf[:], start=True, stop=True)], r=["lf", "c_L"], w=[PSN[BI]])
        P.pe([lambda e, h=h: e.matmul(psb[BG][:, 2 * h:2 * h + 2], lhsT=lf[:, h * 128:(h + 1) * 128], rhs=C["cind"][:], start=True, stop=True)
              for h in range(4)], r=["lf", "c_cind"], w=[PSN[BG]])
        P.act(lambda e: e.activation(out=eb[:], in_=psb[BF][:], func=AF.Exp), r=[PSN[BF]], w=["eb"])
        P.act(lambda e: e.activation(out=enb[:], in_=psb[BF][:], func=AF.Exp, scale=-1.0), r=[PSN[BF]], w=["enb"])
        P.act(lambda e: e.activation(out=eb2[:], in_=psb[BI][:], func=AF.Exp), r=[PSN[BI]], w=["eb2"])
        P.act(lambda e: e.activation(out=dec[:], in_=psb[BG][:, 0:8], func=AF.Exp), r=[PSN[BG]], w=[f"dec{p}"])
        P.dve(lambda e: e.scalar_tensor_tensor(out=qd[:], in0=psb[BQ][:], scalar=rs, in1=eb[:], op0=ALU.mult, op1=ALU.mult),
              r=[PSN[BQ], "eb", "RSTD1"], w=[f"qd{p}"])
        P.pool(lambda e: e.tensor_tensor(out=kd[:], in0=kk[:], in1=enb[:], op=ALU.mult), r=["kk", "enb"], w=[f"kd{p}"])
        for c in range(2):
            P.dve(lambda e, c=c: e.scalar_tensor_tensor(out=kd2[c][:], in0=eb2[:], scalar=C["cind"][:, c:c + 1], in1=kk[:],
                                                        op0=ALU.mult, op1=ALU.mult), r=["eb2", "kk", "c_cind"], w=[f"kd2{c}{p}"])

    def partBC(t):
        p = t % 2
        xT = xTs[p]
        xTn = f"xT{p}"
        proj = mkproj(xT, xTn)
        rs = RSTD1[:, t:t + 1]
        nrs = NRS[:, t:t + 1]
        hrs = HRS[:, t:t + 1]
        rs8 = RS8[:, t:t + 1]
        par = p
        eg, qd, kd, vb, dec = egs[p], qds[p], kds[p], vbs[p], decs[p]
        kd2 = [kd2s[0][p], kd2s[1][p]]
        pv = psb[5][:].bitcast(BF16)
        P.pe([lambda e, h=h: e.transpose(out=pv[:, h * 128:(h + 1) * 128], in_=qd[:, h * 128:(h + 1) * 128], identity=ident[:]) for h in range(4)] +
             [lambda e, h=h: e.transpose(out=pv[:, 512 + h * 128:512 + (h + 1) * 128], in_=kd[:, h * 128:(h + 1) * 128], identity=ident[:]) for h in range(4)],
             r=[f"qd{p}", f"kd{p}", "c_ident"], w=[PSN[5]])
        if ctx.get("bc_stop", 99) <= 0:
            return
        pq = pv[:, 0:512].rearrange("p (h s) -> p h s", h=4)
        P.act(lambda e: e.activation(out=qTf[:], in_=pq, func=AF.Copy), r=[PSN[5]], w=["qTf"])
        if ctx.get("bc_stop", 99) <= 0.5:
            return
        P.act(lambda e: e.activation(out=qTc[0][:, :, 0:64], in_=pq[:, :, 0:64], func=AF.Copy), r=[PSN[5]], w=["qT1"])
        P.act(lambda e: e.activation(out=qTc[1][:, :, 64:128], in_=pq[:, :, 64:128], func=AF.Copy), r=[PSN[5]], w=["qT2"])
        if ctx.get("bc_stop", 99) <= 0.75:
            return
        P.act(lambda e: e.activation(out=kT[:], in_=pv[:, 512:1024].rearrange("p (h s) -> p h s", h=4), func=AF.Copy), r=[PSN[5]], w=["kT"])
        if ctx.get("bc_stop", 99) <= 1:
            return
        P.pe([lambda e, h=h: e.matmul(psb[4][:, h * 128:(h + 1) * 128], lhsT=kT[:, h, :], rhs=qTf[:, h, :], start=True, stop=True)
              for h in range(4)], r=["kT", "qTf"], w=[PSN[4]])
        P.dve(lambda e: e.tensor_tensor(out=Asb[:], in0=psb[4][:].rearrange("p (h s) -> p h s", h=4), in1=Ub, op=ALU.mult),
              r=[PSN[4], "c_U"], w=["Asb"])
        if ctx.get("bc_stop", 99) <= 2:
            return
        for c in range(2):
            P.pe([lambda e, h=h, c=c: e.matmul(psb[6][:, h * 128:(h + 1) * 128], lhsT=kd2[c][:, h * 128:(h + 1) * 128], rhs=vb[:, h * 128:(h + 1) * 128],
                                               start=True, stop=True) for h in range(4)], r=[f"kd2{c}{p}", f"vb{p}"], w=[PSN[6]])
            for h in range(4):
                P.dve(lambda e, h=h, c=c: e.scalar_tensor_tensor(out=Sf[:, h, :], in0=Sf[:, h, :], scalar=dec[:, 2 * h + c:2 * h + c + 1],
                                                                 in1=psb[6][:, h * 128:(h + 1) * 128], op0=ALU.mult, op1=ALU.add),
                      r=["Sf", f"dec{p}", PSN[6]], w=["Sf"])
            dst = 1 - c
            P.pool(lambda e, dst=dst: e.tensor_copy(out=Sbf[dst][:], in_=Sf[:]), r=["Sf"], w=[f"Sbf{dst}"])
            if c == 0:
                mm = []
                for h in range(4):
                    osl = psb[7][:, h * 128:(h + 1) * 128]
                    mm.append(lambda e, h=h, osl=osl: e.matmul(osl, lhsT=Asb[:, h, :], rhs=vb[:, h * 128:(h + 1) * 128], start=True, stop=False))
                    mm.append(lambda e, h=h, osl=osl: e.matmul(osl, lhsT=qTc[0][:, h, :], rhs=Sbf[0][:, h, :], start=False, stop=False))
                    mm.append(lambda e, h=h, osl=osl: e.matmul(osl, lhsT=qTc[1][:, h, :], rhs=Sbf[1][:, h, :], start=False, stop=True))
                P.pe(mm, r=["Asb", f"vb{p}", "qT1", "qT2", "Sbf0", "Sbf1"], w=[PSN[7]])
        if ctx.get("bc_stop", 99) <= 3:
            return
        for h in range(4):
            P.act(lambda e, h=h: e.activation(out=junk[:], in_=psb[7][:, h * 128:(h + 1) * 128], func=AF.Square, accum_out=oss[:, h:h + 1]),
                  r=[PSN[7]], w=["junk3", "oss"])
        P.act(lambda e: e.activation(out=oss[:, 4:8], in_=oss[:, 0:4], func=AF.Ln, scale=1.0 / 128, bias=ctx["epsb"][:, 0:1]), r=["oss", "epsb"], w=["oss2"])
        P.act(lambda e: e.activation(out=oss[:, 4:8], in_=oss[:, 4:8], func=AF.Exp, scale=-0.5), r=["oss2"], w=["oss2"])
        P.dve(lambda e: e.tensor_scalar(out=oss[:, 4:8], in0=oss[:, 4:8], scalar1=hrs, scalar2=None, op0=ALU.mult), r=["oss2", "HRS"], w=["oss2"])
        for h in range(4):
            P.dve(lambda e, h=h: e.scalar_tensor_tensor(out=ya[:, h * 128:(h + 1) * 128], in0=psb[7][:, h * 128:(h + 1) * 128], scalar=oss[:, 4 + h:5 + h],
                                                        in1=eg[:, h * 128:(h + 1) * 128], op0=ALU.mult, op1=ALU.mult),
                  r=[PSN[7], "oss2", f"eg{p}"], w=["ya"])
        if "d_ya" in dbg:
            P.dma("sp", Dm["d_ya"][t * 128:(t + 1) * 128, :], ya[:], r=["ya"], w=["d_ya"])
        if ctx.get("bc_stop", 99) <= 4:
            return
        pv0 = psb[5][:].bitcast(BF16)
        P.pe([lambda e, h=h: e.transpose(out=pv0[:, h * 128:(h + 1) * 128], in_=ya[:, h * 128:(h + 1) * 128], identity=ident[:]) for h in range(4)],
             r=["ya", "c_ident"], w=[PSN[5]])
        P.act(lambda e: e.activation(out=yaT[:], in_=pv0[:, 0:512].rearrange("p (h s) -> p h s", h=4), func=AF.Copy), r=[PSN[5]], w=["yaT"])
        for half, bank in ((0, 4), (1, 6)):
            P.pe([lambda e, c=c, half=half, bank=bank: e.matmul(psb[bank][:], lhsT=yaT[:, c, :], rhs=wbra[:, c, half * 512:(half + 1) * 512],
                                                                start=(c == 0), stop=(c == 3)) for c in range(4)], r=["yaT", "wbra"], w=[PSN[bank]])
        if ctx.get("bc_stop", 99) <= 5:
            return
        MG0 = 2048 + 512 + 24
        for half, bank in ((0, 5), (1, 7)):
            proj(bank, MG0 + half * 512, 512)
            P.act(lambda e, half=half, bank=bank: e.activation(out=sga[:, half * 512:(half + 1) * 512], in_=psb[bank][:], func=AF.Tanh, scale=hrs),
                  r=[PSN[bank], "HRS"], w=[f"sga{half}"])
        for half, bank in ((0, 4), (1, 6)):
            P.dve(lambda e, half=half, bank=bank: e.scalar_tensor_tensor(out=gap[par][:, half * 512:(half + 1) * 512], in0=sga[:, half * 512:(half + 1) * 512],
                                                                         scalar=1.0, in1=psb[bank][:], op0=ALU.add, op1=ALU.mult),
                  r=[PSN[bank], "sga0", "sga1"], w=["gap"])
        P.dma("sp", Dm["s_gap"][t * 128:(t + 1) * 128, :], gap[par][:], r=["gap"], w=["s_gap"])

    def partC2(t):
        p = t % 2
        xT = xTs[p]
        xTn = f"xT{p}"
        proj = mkproj(xT, xTn)
        rs = RSTD1[:, t:t + 1]
        nrs = NRS[:, t:t + 1]
        hrs = HRS[:, t:t + 1]
        rs8 = RS8[:, t:t + 1]
        par = p
        MG0 = 2048 + 512 + 24
        if ctx.get("bc_stop", 99) <= 6:
            return
        for half, bank in ((0, 0), (1, 1)):
            proj(bank, MG0 + 1024 + half * 512, 512)
            P.act(lambda e, half=half, bank=bank: e.activation(out=sgb[:, half * 512:(half + 1) * 512], in_=psb[bank][:], func=AF.Tanh, scale=hrs),
                  r=[PSN[bank], "HRS"], w=[f"sgb{half}"])
        P.dma("pool", Dm["s_gb"][t * 128:(t + 1) * 128, :], sgb[:], r=["sgb0", "sgb1"], w=["s_gb"])
        if ctx.get("bc_stop", 99) <= 7:
            return
        proj(2, 2048, 512)
        P.act(lambda e: e.activation(out=qn[:], in_=psb[2][:], func=AF.Copy, scale=rs8), r=[PSN[2], "RS8"], w=["qn"])
        Q3 = qn[:].rearrange("p (a d) -> p a d", a=8)
        t1 = Q3[:, :, 0:8]
        t2 = Q3[:, :, 8:16]
        cosb = bc(COS[:, t, :].unsqueeze(1), [128, 8, 8])
        sinb = bc(SIN[:, t, :].unsqueeze(1), [128, 8, 8])
        rts = [rt[:, i, :].rearrange("p (a d) -> p a d", a=8) for i in range(4)]
        kr = ["qn", "SIN", "COS"]
        P.dve(lambda e: e.tensor_tensor(out=rts[0], in0=t1, in1=cosb, op=ALU.mult), r=kr, w=["rt0"])
        P.dve(lambda e: e.tensor_tensor(out=rts[1], in0=t2, in1=sinb, op=ALU.mult), r=kr, w=["rt1"])
        P.pool(lambda e: e.tensor_tensor(out=rts[2], in0=t2, in1=cosb, op=ALU.mult), r=kr, w=["rt2"])
        P.pool(lambda e: e.tensor_tensor(out=rts[3], in0=t1, in1=sinb, op=ALU.mult), r=kr, w=["rt3"])
        P.dve(lambda e: e.tensor_tensor(out=t1, in0=rts[0], in1=rts[1], op=ALU.subtract), r=["rt0", "rt1", "rt2", "rt3"], w=["qn"])
        P.dve(lambda e: e.tensor_tensor(out=t2, in0=rts[2], in1=rts[3], op=ALU.add), r=["rt2", "rt3"], w=["qn"])
        P.pool(lambda e: e.tensor_copy(out=qnb[:], in_=qn[:]), r=["qn"], w=["qnb"])
        pv6 = psb[3][:].bitcast(BF16)
        P.pe([lambda e, h=h: e.transpose(out=pv6[:, h * 128:(h + 1) * 128], in_=qnb[:, h * 128:(h + 1) * 128], identity=ident[:]) for h in range(4)],
             r=["qnb", "c_ident"], w=[PSN[3]])
        P.act(lambda e: e.activation(out=qTt[par][:], in_=pv6[:, 0:512], func=AF.Copy), r=[PSN[3]], w=["qTt"])
        P.dma("sp", Dm["s_qt"][t], qTt[par][:], r=["qTt"], w=["s_qt"])
        if ctx.get("bc_stop", 99) <= 8:
            return
        proj(0, 2048 + 512, 24)
        P.act(lambda e: e.activation(out=gt[:], in_=psb[0][:, 0:24], func=AF.Exp, scale=nrs), r=[PSN[0], "NRS"], w=["gt"])
        P.dve(lambda e: e.tensor_scalar(out=gt[:], in0=gt[:], scalar1=1.0, scalar2=None, op0=ALU.add), r=["gt"], w=["gt"])
        P.dve(lambda e: e.reciprocal(out=GATES[:, t, :], in_=gt[:]), r=["gt"], w=["GATES"])

    def interleave(a, b):
        if not b:
            return a
        out = []
        na, nb_ = len(a), len(b)
        j = 0
        for i, x in enumerate(a):
            out.append(x)
            tgt = (i + 1) * nb_ // na
            while j < tgt:
                out.append(b[j])
                j += 1
        out += b[j:]
        return out

    NTA = ctx.get("nt3a", NT)
    partA(0)
    if ctx.get("onlyA"):
        NTA = 0
    for t in range(NTA):
        a = P.capture(lambda: partBC(t))
        b = P.capture(lambda: partA(t + 1)) if t + 1 < NTA else []
        b = b + P.capture(lambda: partC2(t))
        if len(a) >= len(b):
            P.commit(interleave(a, b))
        else:
            P.commit(interleave(b, a))
    if "d_gates" in dbg:
        P.dma("sp", Dm["d_gates"], GATES[:].rearrange("p a b -> p (a b)"), r=["GATES"], w=["d_gates"])
    P.barrier()
    M.release(m0)


def phase3b(ctx):
    nc, P, M, Dm, psb, PSN, C = (ctx[k] for k in ("nc", "P", "M", "Dm", "psb", "PSN", "C"))
    GATES, KE, KW, VS, VW, KC, VC = (ctx[k] for k in ("GATES", "KE", "KW", "VS", "VW", "KC", "VC"))
    dbg = ctx["dbg"]
    ident = C["ident"]
    M.release(ctx["m_lb"])
    m0 = M.mark()
    TOP2 = 229344 - 65536
    M.cap = TOP2
    wf1 = M.alloc_at([128, 8, 4096], BF16, "wf1", TOP2)
    ctx["wf1"] = wf1
    wbrb = M.alloc([128, 4, D], BF16, "wbrb")
    wout = M.alloc([128, 8, D], BF16, "wout")
    stg3 = [M.alloc([128, 512], F32, "stg3a"), M.alloc([128, 512], F32, "stg3b")]
    stg3n = ["stg3a", "stg3b"]
    for i_, pc_ in enumerate(ctx["make_pieces"](wbrb, Dm["w_br_b"], D, 4, 512)):
        ctx["issue_piece"](pc_, "wbrb", "pool" if i_ % 2 else "dve", stg3, stg3n)
    for i_, pc_ in enumerate(ctx["make_pieces"](wout, Dm["w_out"], D, 8, 512)):
        ctx["issue_piece"](pc_, "wout", "pool" if i_ % 2 else "dve", stg3, stg3n)
    wf1_pieces = ctx["make_pieces"](wf1, Dm["w_ff1"], 4096, 8, 512)

    def A(shape, dt, nm):
        return M.alloc(shape, dt, nm)
    QB = [[A([128, 512], BF16, f"QB{p}{g}") for g in range(2)] for p in range(2)]
    QW = [[A([128, 512], BF16, f"QW{p}{g}") for g in range(2)] for p in range(2)]
    Ec_ = A([128, 4, 256], F32, "Ec0")
    Ecs = [Ec_, Ec_]
    Eb = [A([128, 4, 256], BF16, "Eb0"), A([128, 4, 256], BF16, "Eb1")]
    EbTs = [A([128, 8, 128], BF16, "EbT0"), A([128, 8, 128], BF16, "EbT1")]
    ppad = A([128, 2, 260], F32, "ppad")
    imps = [A([128, 64], F32, "imp0"), A([128, 64], F32, "imp1")]
    imp2s = [A([128, 64], F32, "imp20"), A([128, 64], F32, "imp21")]
    m8s = [A([128, 16], F32, "m80"), A([128, 16], F32, "m81")]
    mxs = [A([128, 4], F32, "mx0"), A([128, 4], F32, "mx1")]
    sm = [A([128, 2, 4], F32, "sm0"), A([128, 2, 4], F32, "sm1")]
    ocs = [A([128, 512], F32, "ocs0"), A([128, 512], F32, "ocs1")]
    rsw = A([128, 3, 4], F32, "rsw")
    NB = A([128, 128], BF16, "NB")
    Es = [A([128, 512], BF16, f"Es{i}") for i in range(4)]
    YBf = A([128, 256], F32, "YBf")
    Ytmp = A([128, 256], F32, "Ytmp")
    YBs = [A([128, 512], BF16, "YB0"), A([128, 512], BF16, "YB1")]
    ybT = A([128, 4, 128], BF16, "ybT")
    gapt = [A([128, D], F32, "gapt0"), A([128, D], F32, "gapt1")]
    gbt = [A([128, D], BF16, "gbt0"), A([128, D], BF16, "gbt1")]
    xt0_ = A([128, D], F32, "xt0")
    xt = [xt0_, xt0_]
    mgf = A([128, 512], F32, "mgf")
    mg = A([128, D], BF16, "mg")
    mgT = A([128, 8, 128], BF16, "mgT")
    ht0_ = A([128, D], F32, "ht0")
    ht = [ht0_, ht0_]
    for p in range(2):
        for g in range(2):
            P.pool(lambda e, p=p, g=g: e.memset(QW[p][g][:], 0.0), w=[f"QW{p}{g}"])
            P.pool(lambda e, p=p, g=g: e.memset(QB[p][g][:], 0.0), w=[f"QB{p}{g}"])
    P.pool(lambda e: e.memset(Eb[0][:], 0.0), w=["Eb0"])
    P.pool(lambda e: e.memset(Eb[1][:], 0.0), w=["Eb1"])
    P.pool(lambda e: e.memset(ppad[:], 0.0), w=["ppadg0", "ppadg1"])
    P.pool(lambda e: e.memset(NB[:], 0.0), w=["NBg0", "NBg1"])
    es_ctr = [0]
    sc_ctr = [0]

    def comp_part(qb):
        par = qb % 2
        n = min(8 * qb + 7, 255)
        sq = Dm["s_qt"][qb]
        P.dma("sp", QB[par][0][0:64, :], sq[0:64, :], r=["s_qt"], w=[f"QB{par}0"])
        P.dma("sp", QB[par][1][64:128, :], sq[64:128, :], r=["s_qt"], w=[f"QB{par}1"])
        P.dma("sp", QW[par][0][0:64, :], sq[0:64, :], r=["s_qt"], w=[f"QW{par}0"])
        P.dma("sp", QW[par][1][64:128, :], sq[64:128, :], r=["s_qt"], w=[f"QW{par}1"])
        P.dma("sp", gapt[par][:], Dm["s_gap"][qb * 128:(qb + 1) * 128, :], r=["s_gap"], w=[f"gapt{par}"])
        P.dma("sp", gbt[par][:], Dm["s_gb"][qb * 128:(qb + 1) * 128, :], r=["s_gb"], w=[f"gbt{par}"])

        def comp(g):
            qw = QW[par][g]
            smg = sm[par][:, g, :]
            smn = f"sm{par}{g}"
            Ec, EbT, imp, imp2, m8, mx = Ecs[g], EbTs[g], imps[g], imp2s[g], m8s[g], mxs[g]
            B0 = 0
            B1 = 1
            G = f"g{g}"
            for hp in range(2):
                P.pe([lambda e, hp=hp, hh=hh: e.matmul(psb[B0 + hp][:, hh * 256:hh * 256 + n], lhsT=qw[:, (hp * 2 + hh) * 128:(hp * 2 + hh + 1) * 128],
                                                       rhs=KC[:, 0:n], start=True, stop=True) for hh in range(2)],
                     r=[f"QW{par}{g}", "KC"], w=[PSN[B0 + hp]])
            sv = [psb[B0 + hp][:].rearrange("p (h c) -> p h c", h=2)[:, :, 0:n] for hp in range(2)]
            for hp in range(2):
                P.dve(lambda e, hp=hp: e.tensor_reduce(out=mx[:, hp:hp + 1], in_=sv[hp], axis=AX.XY, op=ALU.max), r=[PSN[B0 + hp]], w=[f"mx{hp}" + G])
            P.dve(lambda e: e.tensor_tensor(out=mx[:, 2:3], in0=mx[:, 0:1], in1=mx[:, 1:2], op=ALU.max), r=["mx0" + G, "mx1" + G], w=["mx2" + G])
            P.dve(lambda e: e.tensor_scalar(out=mx[:, 3:4], in0=mx[:, 2:3], scalar1=-1.0, scalar2=None, op0=ALU.mult), r=["mx2" + G], w=["mx3" + G])
            for hp in range(2):
                P.act(lambda e, hp=hp: e.activation(out=Ec[:, hp * 2:hp * 2 + 2, 0:n], in_=sv[hp], func=AF.Exp, bias=mx[:, 3:4]),
                      r=[PSN[B0 + hp], "mx3" + G], w=[f"Ec{hp}"])
            if qb == 0:
                P.dve(lambda e: e.tensor_tensor(out=Ec[:, :, 0:7], in0=Ec[:, :, 0:7], in1=bc(C["stair"][:, 1:8].unsqueeze(1), [128, 4, 7]), op=ALU.mult),
                      r=["Ec0", "Ec1", "c_stair"], w=["Ec0", "Ec1"])
            else:
                P.dve(lambda e: e.tensor_tensor(out=Ec[:, :, n - 8:n], in0=Ec[:, :, n - 8:n], in1=bc(C["stair"][:].unsqueeze(1), [128, 4, 8]), op=ALU.mult),
                      r=["Ec0", "Ec1", "c_stair"], w=["Ec0", "Ec1"])
            P.pool(lambda e: e.tensor_copy(out=Eb[g][:, :, 0:n], in_=Ec[:, :, 0:n]), r=["Ec0", "Ec1"], w=[f"Eb{g}"])
            P.dve(lambda e: e.tensor_reduce(out=smg, in_=Ec[:, :, 0:n], axis=AX.X, op=ALU.add), r=["Ec0", "Ec1"], w=[smn])
            P.dve(lambda e: e.tensor_scalar(out=smg, in0=smg, scalar1=1e-30, scalar2=None, op0=ALU.max), r=[smn], w=[smn])
            P.dve(lambda e: e.reciprocal(out=smg, in_=smg), r=[smn], w=[smn])
            pp = ppad[:, g, 1:n + 1]
            P.dve(lambda e: e.tensor_scalar(out=pp, in0=Ec[:, 0, 0:n], scalar1=sm[par][:, g, 0:1], scalar2=None, op0=ALU.mult),
                  r=["Ec0", "Ec1", smn], w=["ppad" + G])
            for h in range(1, 4):
                P.dve(lambda e, h=h: e.scalar_tensor_tensor(out=pp, in0=Ec[:, h, 0:n], scalar=sm[par][:, g, h:h + 1], in1=pp, op0=ALU.mult, op1=ALU.add),
                      r=["Ec0", "Ec1", smn, "ppad" + G], w=["ppad" + G])
            w4 = ppad[:, g, 0:256].rearrange("p (j r) -> p j r", r=4)
            w4n = ppad[:, g, 4:260].rearrange("p (j r) -> p j r", r=4)
            P.dve(lambda e: e.tensor_reduce(out=imp[:], in_=w4, axis=AX.X, op=ALU.add), r=["ppad" + G], w=["imp" + G])
            P.dve(lambda e: e.scalar_tensor_tensor(out=imp[:], in0=w4[:, :, 0], scalar=-0.5, in1=imp[:], op0=ALU.mult, op1=ALU.add), r=["ppad" + G, "imp" + G], w=["imp" + G])
            P.dve(lambda e: e.scalar_tensor_tensor(out=imp[:], in0=w4n[:, :, 0], scalar=0.5, in1=imp[:], op0=ALU.mult, op1=ALU.add), r=["ppad" + G, "imp" + G], w=["imp" + G])
            P.dve(lambda e: e.tensor_tensor(out=imp[:], in0=imp[:], in1=C["WB"][:, 62 - 2 * qb:126 - 2 * qb], op=ALU.add), r=["imp" + G, "c_WB"], w=["imp" + G])
            P.dve(lambda e: e.memset(imp[:, 0:1], 2e4), r=["imp" + G], w=["imp" + G])
            P.dve(lambda e: e.max(out=m8[:, 0:8], in_=imp[:]), r=["imp" + G], w=["m8a" + G])
            P.dve(lambda e: e.match_replace(out=imp2[:], in_to_replace=m8[:, 0:8], in_values=imp[:], imm_value=-3e38), r=["imp" + G, "m8a" + G], w=["imp2" + G])
            P.dve(lambda e: e.max(out=m8[:, 8:16], in_=imp2[:]), r=["imp2" + G], w=["m8b" + G])
            nbc = slice(64, 128) if g == 0 else slice(0, 64)
            P.dve(lambda e: e.tensor_scalar(out=NB[:, nbc], in0=imp[:], scalar1=m8[:, 15:16], scalar2=-BIG, op0=ALU.is_lt, op1=ALU.mult),
                  r=["imp" + G, "m8b" + G], w=["NB" + G])
            pv = psb[B0][:].bitcast(BF16)
            nch = 2 if n > 128 else 1
            P.pe([lambda e, h=h, ch=ch: e.transpose(out=pv[:, (h * 2 + ch) * 128:(h * 2 + ch + 1) * 128], in_=Eb[g][:, h, ch * 128:(ch + 1) * 128],
                                                    identity=ident[:]) for h in range(4) for ch in range(nch)],
                 r=[f"Eb{g}", "c_ident"], w=[PSN[B0]])
            P.act(lambda e: e.activation(out=EbT[:].rearrange("p (h c) q -> p h c q", c=2)[:, :, 0:nch, :],
                                         in_=pv[:, 0:1024].rearrange("p (h c q) -> p h c q", h=4, c=2)[:, :, 0:nch, :], func=AF.Copy),
                  r=[PSN[B0]], w=["EbT" + G])
            mm = []
            for h in range(4):
                for ch in range(nch):
                    mm.append(lambda e, h=h, ch=ch: e.matmul(psb[B1][:, h * 64:(h + 1) * 64], lhsT=EbT[:, h * 2 + ch, :],
                                                             rhs=VC[:, ch, g, :], start=(ch == 0), stop=(ch == nch - 1)))
            P.pe(mm, r=["EbT" + G, "VC"], w=[PSN[B1]])
            P.dve(lambda e: e.tensor_copy(out=ocs[par][:, g * 256:(g + 1) * 256], in_=psb[B1][:, 0:256]), r=[PSN[B1]], w=[f"ocs{par}{g}"])
        comp(0)
        comp(1)
        pv = psb[0][:].bitcast(BF16)
        P.pe([lambda e: e.transpose(out=pv[:, 0:128], in_=NB[:], identity=ident[:])], r=["NBg0", "NBg1", "c_ident"], w=[PSN[0]])
        P.act(lambda e: e.activation(out=QB[par][0][64:128, :].rearrange("p (h q) -> p h q", h=4), in_=bc(pv[64:128, 0:128].unsqueeze(1), [64, 4, 128]),
                                     func=AF.Copy), r=[PSN[0]], w=[f"QB{par}0"])
        P.dve(lambda e: e.tensor_copy(out=QB[par][1][0:64, :].rearrange("p (h q) -> p h q", h=4), in_=bc(pv[0:64, 0:128].unsqueeze(1), [64, 4, 128])),
              r=[PSN[0]], w=[f"QB{par}1"])

    def selwin_part(qb):
        par = qb % 2
        for i_ in (2 * qb, 2 * qb + 1):
            if i_ < len(wf1_pieces):
                ctx["issue_piece"](wf1_pieces[i_], "wf1", "pool", stg3, stg3n)
        G4 = GATES[:, qb, :].rearrange("p (g h b) -> p g h b", g=2, h=4)
        steps = []

        def add_branch(g, kts, lhs_arr, lhs_res, rhs_t, rhs_res, Vt, Vres, obank, masks):
            for i, kt in enumerate(kts):
                first = (i == 0)
                last = (kt == kts[-1])
                mk = masks.get(kt)

                def qk(kt=kt, mk=mk):
                    sb = 2 + (sc_ctr[0] % 4)
                    sc_ctr[0] += 1
                    mm = [lambda e: e.matmul(psb[sb][:], lhsT=lhs_arr[:, kt * 128:(kt + 1) * 128], rhs=rhs_t[:], start=True, stop=(mk is None))]
                    rr = [lhs_res, rhs_res]
                    if mk is not None:
                        mm.append(lambda e: e.matmul(psb[sb][:], lhsT=ident[:], rhs=C[mk][:], start=False, stop=True))
                        rr += ["c_ident", "c_" + mk]
                    P.pe(mm, r=rr, w=[PSN[sb]])
                    return sb

                def ex(sb):
                    ei = es_ctr[0] % 4
                    es_ctr[0] += 1
                    P.act(lambda e: e.activation(out=Es[ei][:], in_=psb[sb][:], func=AF.Exp), r=[PSN[sb]], w=[f"Es{ei}"])
                    return ei

                def pvf(ei, kt=kt, first=first, last=last):
                    P.pe([lambda e, h=h: e.matmul(psb[obank][:, h * 65:(h + 1) * 65], lhsT=Es[ei][:, h * 128:(h + 1) * 128], rhs=Vt[:, kt, g, :],
                                                  start=(first and h == 0), stop=(last and h == 3), skip_group_check=True) for h in range(4)],
                         r=[f"Es{ei}", Vres, "VSone", "VWone"], w=[PSN[obank]])
                steps.append((qk, ex, pvf, None))

        def combine(g):
            os_v = psb[6][:, 0:260].rearrange("p (h e) -> p h e", e=65)
            ow_v = psb[7][:, 0:260].rearrange("p (h e) -> p h e", e=65)
            oc_v = ocs[par][:, g * 256:(g + 1) * 256].rearrange("p (h d) -> p h d", d=64)
            smn = f"sm{par}{g}"
            P.dve(lambda e: e.reciprocal(out=rsw[:, 1, :], in_=os_v[:, :, 64]), r=[PSN[6]], w=["rsw1"])
            P.dve(lambda e: e.reciprocal(out=rsw[:, 2, :], in_=ow_v[:, :, 64]), r=[PSN[7]], w=["rsw2"])
            P.dve(lambda e: e.tensor_tensor(out=rsw[:, 0, :], in0=sm[par][:, g, :], in1=G4[:, g, :, 0], op=ALU.mult), r=[smn, "GATES"], w=["rsw0"])
            P.dve(lambda e: e.tensor_tensor(out=rsw[:, 1, :], in0=rsw[:, 1, :], in1=G4[:, g, :, 1], op=ALU.mult), r=["rsw1", "GATES"], w=["rsw1"])
            P.dve(lambda e: e.tensor_tensor(out=rsw[:, 2, :], in0=rsw[:, 2, :], in1=G4[:, g, :, 2], op=ALU.mult), r=["rsw2", "GATES"], w=["rsw2"])
            yv = YBf[:].rearrange("p (h d) -> p h d", d=64)
            tv = Ytmp[:].rearrange("p (h d) -> p h d", d=64)
            P.pool(lambda e: e.tensor_tensor(out=yv, in0=oc_v, in1=bc(rsw[:, 0, :].unsqueeze(2), [128, 4, 64]), op=ALU.mult), r=[f"ocs{par}{g}", "rsw0"], w=["YBf"])
            P.dve(lambda e: e.tensor_tensor(out=tv, in0=os_v[:, :, 0:64], in1=bc(rsw[:, 1, :].unsqueeze(2), [128, 4, 64]), op=ALU.mult), r=[PSN[6], "rsw1"], w=["Ytmp"])
            P.pool(lambda e: e.tensor_tensor(out=YBf[:], in0=YBf[:], in1=Ytmp[:], op=ALU.add), r=["YBf", "Ytmp"], w=["YBf"])
            P.dve(lambda e: e.tensor_tensor(out=tv, in0=ow_v[:, :, 0:64], in1=bc(rsw[:, 2, :].unsqueeze(2), [128, 4, 64]), op=ALU.mult), r=[PSN[7], "rsw2"], w=["Ytmp"])
            P.pool(lambda e: e.tensor_tensor(out=YBs[par][:, g * 256:(g + 1) * 256], in0=YBf[:], in1=Ytmp[:], op=ALU.add), r=["YBf", "Ytmp"], w=[f"YB{par}"])

        for g in range(2):
            kts_s = list(range(0, qb + 1))
            add_branch(g, kts_s, KE[g], f"KE{g}", QB[par][g], f"QB{par}{g}", VS, "VS", 6, {qb: "triA"})
            kts_w = list(range(max(0, qb - 4), qb + 1))
            mw = {qb: "triA"}
            if qb - 4 >= 0:
                mw[qb - 4] = "triB"
            add_branch(g, kts_w, KW, "KW", QW[par][g], f"QW{par}{g}", VW, "VW", 7, mw)
            steps.append((None, None, None, lambda g=g: combine(g)))
        real = [s_ for s_ in steps]
        pend = None
        sb_next = None
        i = 0
        nsteps = len(real)
        idx_q = 0
        def next_q(start):
            j = start
            while j < nsteps and real[j][0] is None:
                j += 1
            return j
        LOOK = 3
        sbs = {}
        qpos = -1
        for _ in range(LOOK):
            qpos = next_q(qpos + 1)
            if qpos < nsteps and qpos not in sbs:
                sbs[qpos] = real[qpos][0]()
        for i in range(nsteps):
            qk, ex, pvf, cb = real[i]
            if cb is not None:
                cb()
                continue
            sb = sbs.pop(i)
            ei = ex(sb)
            nq = i
            for _ in range(LOOK):
                nq = next_q(nq + 1)
                if nq < nsteps and nq not in sbs:
                    sbs[nq] = real[nq][0]()
            pvf(ei)

    def merge_part(qb):
        par = qb % 2
        YB = YBs[par]
        P.dma("sp", xt[par][:], Dm["x"][qb * 128:(qb + 1) * 128, :], w=["xt3"])
        if "d_yb" in dbg:
            P.dma("sp", Dm["d_yb"][qb * 128:(qb + 1) * 128, :], YB[:], r=[f"YB{par}"], w=["d_yb"])
        pv = psb[0][:].bitcast(BF16)
        P.pe([lambda e, c=c: e.transpose(out=pv[:, c * 128:(c + 1) * 128], in_=YB[:, c * 128:(c + 1) * 128], identity=ident[:]) for c in range(4)],
             r=[f"YB{par}", "c_ident"], w=[PSN[0]])
        P.act(lambda e: e.activation(out=ybT[:].rearrange("p c q -> p (c q)"), in_=pv[:, 0:512], func=AF.Copy), r=[PSN[0]], w=["ybT"])
        for half in range(2):
            P.pe([lambda e, c=c, half=half: e.matmul(psb[1 - half][:], lhsT=ybT[:, c, :], rhs=wbrb[:, c, half * 512:(half + 1) * 512], start=(c == 0), stop=(c == 3))
                  for c in range(4)], r=["ybT", "wbrb"], w=[PSN[1 - half]])
            hs = slice(half * 512, (half + 1) * 512)
            P.dve(lambda e, half=half, hs=hs: e.scalar_tensor_tensor(out=mgf[:], in0=gbt[par][:, hs], scalar=1.0, in1=psb[1 - half][:],
                                                                     op0=ALU.add, op1=ALU.mult),
                  r=[PSN[1 - half], f"gbt{par}"], w=["mgf"])
            P.pool(lambda e, hs=hs: e.tensor_tensor(out=mg[:, hs], in0=mgf[:], in1=gapt[par][:, hs], op=ALU.add),
                   r=["mgf", f"gapt{par}"], w=[f"mg{half}"])
        P.pe([lambda e, c=c: e.transpose(out=pv[:, c * 128:(c + 1) * 128], in_=mg[:, c * 128:(c + 1) * 128], identity=ident[:]) for c in range(8)],
             r=["mg0", "mg1", "c_ident"], w=[PSN[0]])
        P.act(lambda e: e.activation(out=mgT[:].rearrange("p c q -> p (c q)"), in_=pv[:, 0:1024], func=AF.Copy, scale=0.5), r=[PSN[0]], w=["mgT"])
        for half, bank in ((0, 0), (1, 1)):
            P.pe([lambda e, c=c, half=half, bank=bank: e.matmul(psb[bank][:], lhsT=mgT[:, c, :], rhs=wout[:, c, half * 512:(half + 1) * 512],
                                                                start=(c == 0), stop=(c == 7)) for c in range(8)], r=["mgT", "wout"], w=[PSN[bank]])
            hs = slice(half * 512, (half + 1) * 512)
            P.dve(lambda e, bank=bank, hs=hs: e.tensor_tensor(out=ht[par][:, hs], in0=psb[bank][:], in1=xt[par][:, hs], op=ALU.add),
                  r=[PSN[bank], "xt3"], w=["ht"])
        P.dma("sp", Dm["s_h"][qb * 128:(qb + 1) * 128, :], ht[par][:], r=["ht"], w=["s_h"])

    def interleave(a, b):
        if not b:
            return a
        out = []
        na, nb_ = len(a), len(b)
        j = 0
        for i, x in enumerate(a):
            out.append(x)
            tgt = (i + 1) * nb_ // na
            while j < tgt:
                out.append(b[j])
                j += 1
        out += b[j:]
        return out

    NQ = ctx.get("nt3b", NT)
    comp_part(0)
    for qb in range(NQ):
        a = P.capture(lambda: selwin_part(qb))
        b = P.capture(lambda: merge_part(qb - 1)) if qb >= 1 else []
        if qb + 1 < NQ:
            b = b + P.capture(lambda: comp_part(qb + 1))
        P.commit(interleave(a, b))
    merge_part(NQ - 1)
    P.barrier()
    M.release(m0)


def phase4(ctx):
    nc, P, M, Dm, psb, PSN, C = (ctx[k] for k in ("nc", "P", "M", "Dm", "psb", "PSN", "C"))
    ident = C["ident"]
    M.release(ctx["m_consts"])
    wf1 = ctx["wf1"]
    wf2 = M.alloc([128, 32, D], BF16, "wf2")
    fg = M.alloc([128, D], F32, "fg")
    g2 = M.alloc([128, 8], F32, "g2p4")
    eps4 = M.alloc([128, 1], F32, "eps4")
    P.dma("sp", g2[:], Dm["g2"], w=["g2p4"])
    P.dma("sp", fg[:], Dm["final_g"].partition_broadcast(128), w=["fg"])
    P.pool(lambda e: e.memset(eps4[:], EPS), w=["eps4"])
    stg4 = [M.alloc([128, 1024], F32, f"stg4{i}") for i in range(2)]
    stg4n = [f"stg4{i}" for i in range(2)]
    for i_, pc_ in enumerate(ctx["make_pieces"](wf2, Dm["w_ff2"], D, 32, 1024)):
        ctx["issue_piece"](pc_, "wf2", "act" if i_ % 2 == 0 else "dve", stg4, stg4n)
    TS = 256
    NS = S // TS
    hT = M.alloc([128, 8, TS], BF16, "hT")
    uT = M.alloc([128, 32, TS], BF16, "uT")
    hts = [[M.alloc([128, D], F32, f"hts{p}{j}") for j in range(2)] for p in range(2)]
    hb = M.alloc([128, D], BF16, "hb")
    junk = M.alloc([128, D], BF16, "junk4")
    st = M.alloc([128, 8], F32, "st4")
    rl = [M.alloc([128, TS], F32, "rl0"), M.alloc([128, TS], F32, "rl1")]
    yo = [M.alloc([128, D], F32, "yo0"), M.alloc([128, D], F32, "yo1")]
    oo = [M.alloc([128, D], F32, "oo0"), M.alloc([128, D], F32, "oo1")]

    def stile(s_):
        par = s_ % 2
        for j in range(2):
            r0 = s_ * TS + j * 128
            h_ = hts[par][j]
            hn = f"hts{par}{j}"
            P.dma("sp", h_[:], Dm["s_h"][r0:r0 + 128, :], r=["s_h"], w=[hn])
            P.act(lambda e, h_=h_, j=j: e.activation(out=junk[:], in_=h_[:], func=AF.Square, accum_out=st[:, j:j + 1]), r=[hn], w=["junk4", f"ss{j}"])
            P.act(lambda e, j=j: e.activation(out=st[:, 2 + j:3 + j], in_=st[:, j:j + 1], func=AF.Ln, scale=1.0 / D, bias=eps4[:, 0:1]),
                  r=[f"ss{j}", "eps4"], w=[f"rs{j}"])
            P.act(lambda e, j=j: e.activation(out=st[:, 2 + j:3 + j], in_=st[:, 2 + j:3 + j], func=AF.Exp, scale=-1.0), r=[f"rs{j}"], w=[f"rs{j}"])
            P.dve(lambda e, h_=h_: e.tensor_copy(out=hb[:, 0:512], in_=h_[:, 0:512]), r=[hn], w=["hb"])
            P.pool(lambda e, h_=h_: e.tensor_copy(out=hb[:, 512:1024], in_=h_[:, 512:1024]), r=[hn], w=["hb2"])
            pv = psb[j][:].bitcast(BF16)
            P.pe([lambda e, k=k, pv=pv: e.transpose(out=pv[:, k * 128:(k + 1) * 128], in_=hb[:, k * 128:(k + 1) * 128], identity=ident[:]) for k in range(8)],
                 r=["hb", "hb2", "c_ident"], w=[PSN[j]])
            P.dve(lambda e, j=j, pv=pv: e.tensor_tensor(out=hT[:, :, j * 128:(j + 1) * 128], in0=pv[:, 0:1024].rearrange("p (k t) -> p k t", k=8),
                                                        in1=bc(g2[:].unsqueeze(2), [128, 8, 128]), op=ALU.mult),
                  r=[PSN[j], "g2p4"], w=[f"hT{j}"])
        for f in range(32):
            bk = 2 + (f % 4)
            P.pe([lambda e, k=k, f=f, bk=bk: e.matmul(psb[bk][:, 0:TS], lhsT=wf1[:, k, f * 128:(f + 1) * 128], rhs=hT[:, k, :], start=(k == 0), stop=(k == 7))
                  for k in range(8)], r=["hT0", "hT1", "wf1"], w=[PSN[bk]])
            ri = f % 2
            P.act(lambda e, bk=bk, ri=ri: e.activation(out=rl[ri][:], in_=psb[bk][:, 0:TS], func=AF.Relu), r=[PSN[bk]], w=[f"rl{ri}"])
            eng = P.dve if f % 2 == 0 else P.pool
            eng(lambda e, f=f, ri=ri: e.tensor_tensor(out=uT[:, f, :], in0=rl[ri][:], in1=rl[ri][:], op=ALU.mult), r=[f"rl{ri}"], w=["uT"])
        for j in range(2):
            h_ = hts[par][j]
            hn = f"hts{par}{j}"
            for half in range(2):
                bk = 6 + half
                P.pe([lambda e, f=f, j=j, half=half, bk=bk: e.matmul(psb[bk][:], lhsT=uT[:, f, j * 128:(j + 1) * 128], rhs=wf2[:, f, half * 512:(half + 1) * 512],
                                                                     start=(f == 0), stop=(f == 31)) for f in range(32)], r=["uT", "wf2"], w=[PSN[bk]])
                hs = slice(half * 512, (half + 1) * 512)
                P.dve(lambda e, j=j, bk=bk, hs=hs, h_=h_: e.scalar_tensor_tensor(out=yo[j][:, hs], in0=psb[bk][:], scalar=st[:, 2 + j:3 + j], in1=h_[:, hs],
                                                                                 op0=ALU.mult, op1=ALU.add), r=[PSN[bk], f"rs{j}", hn], w=[f"yo{j}"])
            P.act(lambda e, j=j: e.activation(out=junk[:], in_=yo[j][:], func=AF.Square, accum_out=st[:, 4 + j:5 + j]), r=[f"yo{j}"], w=["junk4", f"s3{j}"])
            P.act(lambda e, j=j: e.activation(out=st[:, 6 + j:7 + j], in_=st[:, 4 + j:5 + j], func=AF.Ln, scale=1.0 / D, bias=eps4[:, 0:1]),
                  r=[f"s3{j}", "eps4"], w=[f"r3{j}"])
            P.act(lambda e, j=j: e.activation(out=st[:, 6 + j:7 + j], in_=st[:, 6 + j:7 + j], func=AF.Exp, scale=-0.5), r=[f"r3{j}"], w=[f"r3{j}"])
            P.dve(lambda e, j=j: e.scalar_tensor_tensor(out=oo[j][:], in0=yo[j][:], scalar=st[:, 6 + j:7 + j], in1=fg[:], op0=ALU.mult, op1=ALU.mult),
                  r=[f"yo{j}", f"r3{j}", "fg"], w=[f"oo{j}"])
            r0 = s_ * TS + j * 128
            P.dma("sp", Dm["out"][r0:r0 + 128, :], oo[j][:], r=[f"oo{j}"], w=["out"])

    for s_ in range(NS):
        stile(s_)


_NC_CACHE = {}


def kernel(**inputs):
    if "nc" not in _NC_CACHE:
        ctx = build()
        phase3a(ctx)
        phase3b(ctx)
        phase4(ctx)
        _NC_CACHE["nc"] = finish(ctx, ["out"])
    nc = _NC_CACHE["nc"]
    maps = [prep_inputs(inputs, b) for b in range(8)]
    res = run_bass_kernel_spmd(nc, maps, core_ids=list(range(8)))
    out = np.stack([np.asarray(r["out"], np.float32) for r in res.results], axis=0)
    return out.reshape(8, S, D)
```

```python
import numpy as np
import ml_dtypes
import concourse.bass as bass
import concourse.mybir as mybir
from concourse.bass_utils import run_bass_kernel_spmd

F32, BF16, I32 = mybir.dt.float32, mybir.dt.bfloat16, mybir.dt.int32
AF = mybir.ActivationFunctionType
ALU = mybir.AluOpType
AX = mybir.AxisListType

S = 4096
D = 1024
NT = S // 128
EPS = 1e-6
NMAIN = 4632
BIG = 30000.0
PI = float(np.pi)


class Prog:
    def __init__(self, nc, nslots):
        self.nc = nc
        self.names = ["pe", "act", "dve", "pool", "sp"]
        self.semobj = {}
        for e in self.names:
            self.semobj[e] = nc.alloc_semaphore("sem_" + e)
        self.cnt = {e: 0 for e in self.names}
        self.known = {e: {} for e in self.names}
        self.streams = {e: [] for e in self.names}
        self.lastw = {}
        self.readers = {}
        self.slots = {}
        self.slot_next = {}
        self.qeng = {"sp": "sp", "pool": "pool", "act": "act", "pf": "pool"}
        for q, n in nslots.items():
            self.slots[q] = []
            for i in range(n):
                k = f"dq_{q}{i}"
                self.semobj[k] = nc.alloc_semaphore(k)
                self.slots[q].append([k, 0])
            self.slot_next[q] = 0

    def _deps(self, eng, r, w, same_ok):
        deps = {}

        def add(ev):
            if ev is None:
                return
            k, v = ev
            if deps.get(k, 0) < v:
                deps[k] = v

        for res in r:
            add(self.lastw.get(res))
        for res in w:
            add(self.lastw.get(res))
            for k, v in self.readers.get(res, {}).items():
                if k == eng and same_ok and eng == "pe":
                    continue
                add((k, v))
        waits = []
        for k, v in deps.items():
            if k == eng and eng == "pe":
                continue
            if self.known[eng].get(k, 0) < v:
                self.known[eng][k] = v
                waits.append((k, v))
        return waits

    def _record(self, ev, r, w):
        k, v = ev
        for res in r:
            d = self.readers.setdefault(res, {})
            if d.get(k, 0) < v:
                d[k] = v
        for res in w:
            self.lastw[res] = ev
            self.readers[res] = {}

    def capture(self, fn):
        self._cap = []
        fn()
        lst, self._cap = self._cap, None
        return lst

    def commit(self, lst):
        for kind, a in lst:
            if kind == "op":
                self.op(*a)
            else:
                self.dma(*a)

    def op(self, eng, fns, r=(), w=()):
        if not isinstance(fns, (list, tuple)):
            fns = [fns]
        if getattr(self, "_cap", None) is not None:
            self._cap.append(("op", (eng, fns, tuple(r), tuple(w))))
            return
        waits = self._deps(eng, r, w, True)
        self.cnt[eng] += 1
        ev = (eng, self.cnt[eng])
        self._record(ev, r, w)
        self.streams[eng].append((waits, list(fns), None))

    def pe(self, fns, r=(), w=()):
        self.op("pe", fns, r, w)

    def act(self, fn, r=(), w=()):
        self.op("act", fn, r, w)

    def dve(self, fn, r=(), w=()):
        self.op("dve", fn, r, w)

    def pool(self, fn, r=(), w=()):
        self.op("pool", fn, r, w)

    def dma(self, q, out, in_, r=(), w=()):
        if getattr(self, "_cap", None) is not None:
            self._cap.append(("dma", (q, out, in_, tuple(r), tuple(w))))
            return
        eng = self.qeng[q]
        waits = self._deps(eng, r, w, False)
        i = self.slot_next[q]
        self.slot_next[q] = (i + 1) % len(self.slots[q])
        slot = self.slots[q][i]
        k = slot[0]
        if slot[1] > 0 and self.known[eng].get(k, 0) < 16 * slot[1]:
            self.known[eng][k] = 16 * slot[1]
            waits.append((k, 16 * slot[1]))
        slot[1] += 1
        ev = (k, 16 * slot[1])
        self._record(ev, r, w)
        self.streams[eng].append((waits, [lambda e: e.dma_start(out=out, in_=in_)], k))

    def barrier(self):
        evs = {e: self.cnt[e] for e in self.names if self.cnt[e] > 0}
        for q in self.slots:
            if q == "pf":
                continue
            for k, n in self.slots[q]:
                if n > 0:
                    evs[k] = 16 * n
        for e in self.names:
            waits = []
            for k, v in evs.items():
                if k == e:
                    continue
                if self.known[e].get(k, 0) < v:
                    self.known[e][k] = v
                    waits.append((k, v))
            if waits:
                self.streams[e].append((waits, [], None))
        self.lastw = {r_: ev for r_, ev in self.lastw.items() if ev[0].startswith("dq_pf")}
        self.readers = {}

    def finish(self, out_res):
        waits = self._deps("sp", out_res, [], False)
        self.streams["sp"].append((waits, [], None))

    def emit(self):
        nc = self.nc
        engmap = {"pe": "tensor", "act": "scalar", "dve": "vector", "pool": "gpsimd", "sp": "sync"}
        with nc.Block() as block:
            for name in self.names:
                def body(e, name=name):
                    for waits, fns, dsem in self.streams[name]:
                        for k, v in waits:
                            e.wait_ge(self.semobj[k], v)
                        ins = None
                        for f in fns:
                            ins = f(e)
                        if ins is not None:
                            if dsem is not None:
                                ins.then_inc(self.semobj[dsem], 16)
                            else:
                                ins.then_inc(self.semobj[name], 1)
                getattr(block, engmap[name])(body)


class Mem:
    def __init__(self, nc, base=16512, cap=229344):
        self.nc = nc
        self.off = base
        self.cap = cap
        self.n = 0

    def alloc(self, shape, dtype, name=None):
        sz = {F32: 4, BF16: 2, I32: 4}[dtype]
        nb = int(np.prod(shape[1:])) * sz
        nb = (nb + 63) // 64 * 64
        self.n += 1
        h = self.nc.alloc_sbuf_tensor_at(f"{name or 't'}_{self.n}", list(shape), dtype, offset=self.off)
        self.off += nb
        assert self.off <= self.cap, f"SBUF overflow {self.off}"
        return h

    def alloc_at(self, shape, dtype, name, offset):
        self.n += 1
        return self.nc.alloc_sbuf_tensor_at(f"{name}_{self.n}", list(shape), dtype, offset=offset)

    def mark(self):
        return self.off

    def release(self, m):
        self.off = m


def bc(ap, shape):
    return ap.broadcast_to(list(shape))


def host_consts():
    c = {}
    c["ident"] = np.eye(128, dtype=np.float32).astype(ml_dtypes.bfloat16)
    t = np.arange(128)
    same = (t[:, None] // 64) == (t[None, :] // 64)
    c["U"] = (same & (t[:, None] <= t[None, :])).astype(np.float32)
    c["L"] = (same & (t[:, None] > t[None, :])).astype(np.float32)
    c["cind"] = np.stack([(t < 64), (t >= 64)], axis=1).astype(np.float32)
    inv = (500000.0 ** (-np.arange(0, 16, 2, dtype=np.float32) / 16)).astype(np.float32)
    c["invf"] = np.tile(inv[None, :], (128, 1)).astype(np.float32)
    s = np.arange(S)
    c["E"] = ((s[None, :] // 64) == np.arange(64)[:, None]).astype(np.float32).astype(ml_dtypes.bfloat16)
    q = np.arange(128)
    triA = np.where(t[:, None] <= q[None, :], 0.0, -BIG).astype(np.float32)
    triB = np.where(t[:, None] > q[None, :], 0.0, -BIG).astype(np.float32)
    c["triA"] = np.tile(triA, (1, 4)).astype(ml_dtypes.bfloat16)
    c["triB"] = np.tile(triB, (1, 4)).astype(ml_dtypes.bfloat16)
    cp = np.arange(8) - 1
    c["stair"] = ((16 * cp[None, :] + 31) <= q[:, None]).astype(np.float32)
    rel = np.arange(126) - 62
    ci = (q >= 64).astype(np.int64)
    wb = np.zeros((128, 126), np.float32)
    wb[rel[None, :] > ci[:, None]] = -1e30
    wb[(rel[None, :] == ci[:, None]) | (rel[None, :] == ci[:, None] - 1)] = 1e4
    c["WB"] = wb
    return c


CONST_DT = {"ident": BF16, "U": F32, "L": F32, "cind": F32, "invf": F32, "E": BF16, "triA": BF16,
            "triB": BF16, "stair": F32, "WB": F32}


def build(dbg=()):
    nc = bass.Bass("TRN2", target_bir_lowering=False)
    Dm = {}

    def din(name, shape, dt):
        Dm[name] = nc.dram_tensor(name, list(shape), dt, kind="ExternalInput").ap()

    def dscr(name, shape, dt):
        kind = "ExternalOutput" if name in dbg else "Internal"
        Dm[name] = nc.dram_tensor(name, list(shape), dt, kind=kind).ap()

    hc = host_consts()
    for k, v in hc.items():
        din("c_" + k, v.shape, CONST_DT[k])
    din("x", [S, D], F32)
    din("pos", [128, NT], I32)
    din("g1", [128, 8], F32)
    din("w_main", [D, NMAIN], F32)
    din("w_kv", [D, 768], F32)
    din("lb_param", [1, 1024], F32)
    din("hng", [1, 128], F32)
    din("pe_k", [64, 32], F32)
    din("pe_v", [64, 32], F32)
    din("w1_k", [2048, 256], F32)
    din("w2_k", [256, 64], F32)
    din("w1_v", [2048, 256], F32)
    din("w2_v", [256, 64], F32)
    din("w_br_a", [512, D], F32)
    din("w_br_b", [512, D], F32)
    din("w_out", [D, D], F32)
    din("g2", [128, 8], F32)
    din("w_ff1", [D, 4096], F32)
    din("w_ff2", [4096, D], F32)
    din("final_g", [1, D], F32)
    Dm["out"] = nc.dram_tensor("out", [S, D], F32, kind="ExternalOutput").ap()
    dscr("s_gap", [S, D], F32)
    dscr("s_gb", [S, D], BF16)
    dscr("s_qt", [NT, 128, 512], BF16)
    dscr("s_h", [S, D], F32)
    for nm, shp, dt in (("d_ke0", [128, S], BF16), ("d_kw", [128, S], BF16), ("d_vs", [128, NT * 130], BF16),
                        ("d_kc", [128, 256], BF16), ("d_vc", [128, 256], BF16), ("d_kcT", [128, S], BF16),
                        ("d_sin", [128, NT * 8], F32), ("d_lb", [128, 512], F32), ("d_ya", [S, 512], BF16),
                        ("d_gates", [128, NT * 24], F32), ("d_yb", [S, 512], BF16)):
        if nm in dbg:
            dscr(nm, shp, dt)

    P = Prog(nc, {"sp": 8, "pool": 4, "act": 2, "pf": 40})
    M = Mem(nc)
    psb = [nc.alloc_psum_tensor(f"psb{i}", [128, 512], F32) for i in range(8)]
    PSN = [f"ps{i}" for i in range(8)]

    def load_weight_cast0(dst, dres, src, ncols, nk=8, q="pf"):
        for k in range(nk):
            c0 = 0
            while c0 < ncols:
                cw = min(2048, ncols - c0)
                P.dma(q, dst[:, k, c0:c0 + cw], src[k * 128:(k + 1) * 128, c0:c0 + cw], w=[dres])
                c0 += cw
    def make_pieces(dst, src, ncols, nk, pw):
        out = []
        for k in range(nk):
            c0 = 0
            while c0 < ncols:
                cw = min(pw, ncols - c0)
                out.append((dst[:, k, c0:c0 + cw], src[k * 128:(k + 1) * 128, c0:c0 + cw], cw))
                c0 += cw
        return out

    ring_ctr = [0]

    def issue_piece(piece, dres, eng, ring, ring_names):
        dst, src, cw = piece
        i = ring_ctr[0] % len(ring)
        ring_ctr[0] += 1
        st, sn = ring[i], ring_names[i]
        P.dma("sp", st[:, 0:cw], src, w=[sn])
        if eng == "act":
            P.act(lambda e: e.activation(out=dst, in_=st[:, 0:cw], func=AF.Copy), r=[sn], w=[dres])
        elif eng == "pool":
            P.pool(lambda e: e.tensor_copy(out=dst, in_=st[:, 0:cw]), r=[sn], w=[dres])
        else:
            P.dve(lambda e: e.tensor_copy(out=dst, in_=st[:, 0:cw]), r=[sn], w=[dres])

    TOP = (229344 - (74112 + 8192)) // 64 * 64
    M.cap = TOP
    wm = M.alloc_at([128, 8, NMAIN], BF16, "wm", TOP)
    wbra = M.alloc_at([128, 4, D], BF16, "wbra", TOP + 74112)
    wm_pieces = make_pieces(wm, Dm["w_main"], NMAIN, 8, 2048) + make_pieces(wbra, Dm["w_br_a"], D, 4, 2048)
    C = {}
    for k, v in hc.items():
        if k == "E":
            continue
        C[k] = M.alloc(list(v.shape), CONST_DT[k], "c_" + k)
        P.dma("sp", C[k][:], Dm["c_" + k], w=["c_" + k])
    ident = C["ident"]
    m_consts = M.mark()
    SIN = M.alloc([128, NT, 8], F32, "SIN")
    COS = M.alloc([128, NT, 8], F32, "COS")
    RSTD1 = M.alloc([128, NT], F32, "RSTD1")
    GATES = M.alloc([128, NT, 24], F32, "GATES")
    KE = [M.alloc([128, S], BF16, "KE0"), M.alloc([128, S], BF16, "KE1")]
    KW = M.alloc([128, S], BF16, "KW")
    VS = M.alloc([128, NT, 2, 65], BF16, "VS")
    VW = M.alloc([128, NT, 2, 65], BF16, "VW")
    KC = M.alloc([128, 256], BF16, "KC")
    VC = M.alloc([128, 2, 2, 64], BF16, "VC")
    m_lb = M.mark()
    lbt = M.alloc([128, 512], F32, "lb")
    omlt = M.alloc([128, 512], F32, "oml")
    hng = M.alloc([128, 128], F32, "hng")
    g1t = M.alloc([128, 8], F32, "g1t")
    g2t = M.alloc([128, 8], F32, "g2t")
    P.dma("sp", g1t[:], Dm["g1"], w=["g1t"])
    P.dma("sp", g2t[:], Dm["g2"], w=["g2t"])
    P.dma("sp", hng[:], Dm["hng"].partition_broadcast(128), w=["hng"])
    P.dma("sp", KE[0][64:128, :], Dm["c_E"], w=["KE0e"])
    P.dma("sp", KE[1][0:64, :], Dm["c_E"], w=["KE1e"])
    P.pool(lambda e: e.memset(VS[:, :, :, 64:65], 1.0), w=["VSone"])
    P.pool(lambda e: e.memset(VW[:, :, :, 64:65], 1.0), w=["VWone"])
    P.pool(lambda e: e.memset(KC[:], 0.0), w=["KC"])

    m_phase = M.mark()

    posi = M.alloc([128, NT], I32, "posi")
    posf = M.alloc([128, NT], F32, "posf")
    ang = M.alloc([128, NT, 8], F32, "ang")
    a2 = M.alloc([128, NT, 8], F32, "a2")
    nfi = M.alloc([128, NT, 8], I32, "nfi")
    nff = M.alloc([128, NT, 8], F32, "nff")
    msk = M.alloc([128, NT, 8], F32, "msk")
    lbp = M.alloc([128, 1024], F32, "lbp")
    P.dma("sp", posi[:], Dm["pos"], w=["posi"])
    P.dma("sp", lbp[:], Dm["lb_param"].partition_broadcast(128), w=["lbp"])
    P.dve(lambda e: e.tensor_copy(out=posf[:], in_=posi[:]), r=["posi"], w=["posf"])
    P.dve(lambda e: e.tensor_tensor(out=ang[:], in0=bc(posf[:].unsqueeze(2), [128, NT, 8]),
                                    in1=bc(C["invf"][:].unsqueeze(1), [128, NT, 8]), op=ALU.mult),
          r=["posf", "c_invf"], w=["ang"])
    C1 = 6.28125
    C2 = 2.0 * PI - C1
    for tab, shift, nm in ((SIN, 0.0, "SIN"), (COS, PI / 2, "COS")):
        P.dve(lambda e, shift=shift: e.tensor_scalar(out=a2[:], in0=ang[:], scalar1=shift, scalar2=None, op0=ALU.add),
              r=["ang"], w=["a2"])
        P.dve(lambda e: e.tensor_scalar(out=nff[:], in0=a2[:], scalar1=1.0 / (2 * PI), scalar2=None, op0=ALU.mult),
              r=["a2"], w=["nff"])
        P.dve(lambda e: e.tensor_copy(out=nfi[:], in_=nff[:]), r=["nff"], w=["nfi"])
        P.dve(lambda e: e.tensor_copy(out=nff[:], in_=nfi[:]), r=["nfi"], w=["nff"])
        P.dve(lambda e: e.scalar_tensor_tensor(out=a2[:], in0=nff[:], scalar=-C1, in1=a2[:], op0=ALU.mult, op1=ALU.add),
              r=["nff", "a2"], w=["a2"])
        P.dve(lambda e: e.scalar_tensor_tensor(out=a2[:], in0=nff[:], scalar=-C2, in1=a2[:], op0=ALU.mult, op1=ALU.add),
              r=["nff", "a2"], w=["a2"])
        P.dve(lambda e: e.tensor_scalar(out=msk[:], in0=a2[:], scalar1=PI, scalar2=None, op0=ALU.is_gt), r=["a2"], w=["msk"])
        P.dve(lambda e: e.scalar_tensor_tensor(out=a2[:], in0=msk[:], scalar=-2 * PI, in1=a2[:], op0=ALU.mult, op1=ALU.add),
              r=["msk", "a2"], w=["a2"])
        P.dve(lambda e: e.tensor_scalar(out=msk[:], in0=a2[:], scalar1=-PI, scalar2=None, op0=ALU.is_lt), r=["a2"], w=["msk"])
        P.dve(lambda e: e.scalar_tensor_tensor(out=a2[:], in0=msk[:], scalar=2 * PI, in1=a2[:], op0=ALU.mult, op1=ALU.add),
              r=["msk", "a2"], w=["a2"])
        P.dve(lambda e: e.tensor_scalar(out=a2[:], in0=a2[:], scalar1=-PI, scalar2=PI, op0=ALU.max, op1=ALU.min),
              r=["a2"], w=["a2"])
        P.act(lambda e, tab=tab: e.activation(out=tab[:], in_=a2[:], func=AF.Sin), r=["a2"], w=[nm])
    P.dve(lambda e: e.tensor_tensor(out=lbt[:], in0=lbp[:, 0:512], in1=lbp[:, 512:1024], op=ALU.subtract), r=["lbp"], w=["lb"])
    P.act(lambda e: e.activation(out=lbt[:], in_=lbt[:], func=AF.Exp, scale=-1.0), r=["lb"], w=["lb"])
    P.dve(lambda e: e.tensor_scalar(out=lbt[:], in0=lbt[:], scalar1=1.0, scalar2=None, op0=ALU.add), r=["lb"], w=["lb"])
    P.dve(lambda e: e.reciprocal(out=lbt[:], in_=lbt[:]), r=["lb"], w=["lb"])
    P.dve(lambda e: e.tensor_scalar(out=omlt[:], in0=lbt[:], scalar1=-1.0, scalar2=1.0, op0=ALU.mult, op1=ALU.add),
          r=["lb"], w=["oml"])
    P.dve(lambda e: e.scalar_tensor_tensor(out=lbt[:], in0=omlt[:], scalar=0.5, in1=lbt[:], op0=ALU.mult, op1=ALU.add),
          r=["lb", "oml"], w=["lb"])
    P.dve(lambda e: e.tensor_scalar(out=omlt[:], in0=omlt[:], scalar1=0.5, scalar2=None, op0=ALU.mult), r=["oml", "lb"], w=["oml"])
    if "d_sin" in dbg:
        P.dma("sp", Dm["d_sin"], SIN[:].rearrange("p a b -> p (a b)"), r=["SIN"], w=["d_sin"])
    if "d_lb" in dbg:
        P.dma("sp", Dm["d_lb"], lbt[:], r=["lb"], w=["d_lb"])
    P.barrier()
    M.release(m_phase)

    def load_weight_cast(dst, dres, src, ncols, nk=8, q="pool"):
        for k in range(nk):
            c0 = 0
            while c0 < ncols:
                cw = min(2048, ncols - c0)
                P.dma(q, dst[:, k, c0:c0 + cw], src[k * 128:(k + 1) * 128, c0:c0 + cw], w=[dres])
                c0 += cw

    def load_weight_scaled(dst, dres, src, gcol, gres, ncols, stage, stage_names):
        i = 0
        for k in range(8):
            c0 = 0
            while c0 < ncols:
                cw = min(2048, ncols - c0)
                st = stage[i % 2]
                sn = stage_names[i % 2]
                P.dma("sp", st[:, 0:cw], src[k * 128:(k + 1) * 128, c0:c0 + cw], w=[sn])
                eng = P.dve if i % 2 == 0 else P.pool
                eng(lambda e, st=st, k=k, c0=c0, cw=cw: e.tensor_scalar(
                    out=dst[:, k, c0:c0 + cw], in0=st[:, 0:cw], scalar1=gcol[:, k:k + 1], scalar2=None, op0=ALU.mult),
                    r=[sn, gres], w=[dres])
                c0 += cw
                i += 1

    def x_tile_prep(t, XT, xb, xT, psbank, want_rstd, xTn="xT"):
        xt = XT[t % 2]
        xn = f"xt{t % 2}"
        P.dma("sp", xt[:], Dm["x"][t * 128:(t + 1) * 128, :], w=[xn])
        if want_rstd:
            P.act(lambda e: e.activation(out=junk[:], in_=xt[:], func=AF.Square, accum_out=ssq[:, 0:1]),
                  r=[xn], w=["junk", "ssq"])
            P.act(lambda e: e.activation(out=ssq[:, 1:2], in_=ssq[:, 0:1], func=AF.Ln, scale=1.0 / D, bias=epsb[:, 0:1]),
                  r=["ssq", "epsb"], w=["ssq2"])
            P.act(lambda e: e.activation(out=RSTD1[:, t:t + 1], in_=ssq[:, 1:2], func=AF.Exp, scale=-0.5),
                  r=["ssq2"], w=["RSTD1"])
        P.dve(lambda e: e.tensor_copy(out=xb[:, 0:512], in_=xt[:, 0:512]), r=[xn], w=["xb"])
        P.act(lambda e: e.activation(out=xb[:, 512:1024], in_=xt[:, 512:1024], func=AF.Copy), r=[xn], w=["xb2"])
        pv = psb[psbank][:].bitcast(BF16)
        P.pe([lambda e, k=k: e.transpose(out=pv[:, k * 128:(k + 1) * 128], in_=xb[:, k * 128:(k + 1) * 128], identity=ident[:])
              for k in range(8)], r=["xb", "xb2", "c_ident"], w=[PSN[psbank]])
        P.dve(lambda e: e.tensor_tensor(out=xT[:], in0=pv[:, 0:1024].rearrange("p (k t) -> p k t", k=8),
                                        in1=bc(g1t[:].unsqueeze(2), [128, 8, 128]), op=ALU.mult),
              r=[PSN[psbank], "g1t"], w=[xTn])

    epsb = M.alloc([128, 1], F32, "epsb")
    ssq = M.alloc([128, 2], F32, "ssq")
    junk = M.alloc([128, D], BF16, "junk")
    P.pool(lambda e: e.memset(epsb[:], EPS), w=["epsb"])
    m_phase = M.mark()

    wkv = M.alloc([128, 8, 768], BF16, "wkv")
    kcT = M.alloc([128, S], BF16, "kcT")
    vcT = M.alloc([128, S], BF16, "vcT")
    m_p1 = M.mark()
    stage = [M.alloc([128, 2048], F32, "stg0"), M.alloc([128, 2048], F32, "stg1")]
    stage_n = ["stgA", "stgB"]
    XT = [M.alloc([128, D], F32, "XT0"), M.alloc([128, D], F32, "XT1")]
    xb = M.alloc([128, D], BF16, "xb")
    xT = M.alloc([128, 8, 128], BF16, "xT")
    kvs = M.alloc([128, 768], F32, "kvs")
    kvb = M.alloc([128, 768], BF16, "kvb")
    rt = M.alloc([128, 4, 48], F32, "rt")
    load_weight_cast(wkv, "wkv", Dm["w_kv"], 768)
    def p1_tile(t):
        x_tile_prep(t, XT, xb, xT, 0, True)
        for grp in range(2):
            bk = 1 + grp
            P.pe([lambda e, k=k, grp=grp, bk=bk: e.matmul(psb[bk][:, 0:384], lhsT=xT[:, k, :], rhs=wkv[:, k, grp * 384:(grp + 1) * 384],
                                                           start=(k == 0), stop=(k == 7)) for k in range(8)],
                 r=["xT", "wkv"], w=[PSN[bk]])
            P.act(lambda e, grp=grp, bk=bk: e.activation(out=kvs[:, grp * 384:(grp + 1) * 384], in_=psb[bk][:, 0:384], func=AF.Copy,
                                                         scale=RSTD1[:, t:t + 1]), r=[PSN[bk], "RSTD1"], w=[f"kvs{grp}"])
        K4 = kvs[:].rearrange("p (a g d) -> p a g d", a=6, g=2)
        t1 = K4[:, 0:6:2, :, 0:8]
        t2 = K4[:, 0:6:2, :, 8:16]
        cosb = bc(COS[:, t, :].unsqueeze(1).unsqueeze(1), [128, 3, 2, 8])
        sinb = bc(SIN[:, t, :].unsqueeze(1).unsqueeze(1), [128, 3, 2, 8])
        rts = [rt[:, i, :].rearrange("p (a g d) -> p a g d", a=3, g=2) for i in range(4)]
        kr = ["kvs0", "kvs1", "SIN", "COS"]
        P.dve(lambda e: e.tensor_tensor(out=rts[0], in0=t1, in1=cosb, op=ALU.mult), r=kr, w=["rt0"])
        P.dve(lambda e: e.tensor_tensor(out=rts[1], in0=t2, in1=sinb, op=ALU.mult), r=kr, w=["rt1"])
        P.dve(lambda e: e.tensor_tensor(out=rts[2], in0=t2, in1=cosb, op=ALU.mult), r=kr, w=["rt2"])
        P.dve(lambda e: e.tensor_tensor(out=rts[3], in0=t1, in1=sinb, op=ALU.mult), r=kr, w=["rt3"])
        P.dve(lambda e: e.tensor_tensor(out=t1, in0=rts[0], in1=rts[1], op=ALU.subtract), r=["rt0", "rt1", "rt2", "rt3"], w=["kvs0", "kvs1"])
        P.dve(lambda e: e.tensor_tensor(out=t2, in0=rts[2], in1=rts[3], op=ALU.add), r=["rt2", "rt3"], w=["kvs0", "kvs1"])
        P.dve(lambda e: e.tensor_copy(out=kvb[:], in_=kvs[:]), r=["kvs0", "kvs1"], w=["kvb"])
        pv = psb[3][:].bitcast(BF16)
        srcs = [0, 128, 256, 512]
        P.pe([lambda e, i=i: e.transpose(out=pv[:, i * 128:(i + 1) * 128], in_=kvb[:, srcs[i]:srcs[i] + 128], identity=ident[:])
              for i in range(4)], r=["kvb", "c_ident"], w=[PSN[3]])
        cs = slice(t * 128, (t + 1) * 128)
        P.act(lambda e, cs=cs: e.activation(out=kcT[:, cs], in_=pv[:, 0:128], func=AF.Copy), r=[PSN[3]], w=["kcT"])
        P.act(lambda e, cs=cs: e.activation(out=vcT[:, cs], in_=pv[:, 128:256], func=AF.Copy), r=[PSN[3]], w=["vcT"])
        P.dve(lambda e, cs=cs: e.tensor_copy(out=KE[0][0:64, cs], in_=pv[0:64, 256:384]), r=[PSN[3]], w=["KE0"])
        P.dve(lambda e, cs=cs: e.tensor_copy(out=KE[1][64:128, cs], in_=pv[64:128, 256:384]), r=[PSN[3]], w=["KE1"])
        P.act(lambda e, cs=cs: e.activation(out=KW[:, cs], in_=pv[:, 384:512], func=AF.Copy), r=[PSN[3]], w=["KW"])
        P.pool(lambda e, t=t: e.tensor_copy(out=VS[:, t, :, 0:64], in_=kvb[:, 384:512].rearrange("p (g d) -> p g d", g=2)),
               r=["kvb"], w=["VS"])
        P.pool(lambda e, t=t: e.tensor_copy(out=VW[:, t, :, 0:64], in_=kvb[:, 640:768].rearrange("p (g d) -> p g d", g=2)),
               r=["kvb"], w=["VW"])

    for t in range(NT):
        if t < len(wm_pieces):
            issue_piece(wm_pieces[t], "wmres", "act" if t % 2 == 0 else "pool", stage, stage_n)
        p1_tile(t)
    for t in range(NT, len(wm_pieces)):
        issue_piece(wm_pieces[t], "wmres", "act", stage, stage_n)
    P.barrier()
    M.release(m_p1)

    W1 = [M.alloc([128, 32, 256], BF16, "W1k"), M.alloc([128, 32, 256], BF16, "W1v")]
    W2f = M.alloc([128, 2, 2, 64], F32, "W2f")
    W2k = M.alloc([128, 2, 2, 128], BF16, "W2k")
    W2v = M.alloc([128, 2, 64], BF16, "W2v")
    pef = M.alloc([64, 2, 32], F32, "pef")
    peb = M.alloc([64, 2, 32], BF16, "peb")
    cst = M.alloc([128, 2, 2], F32, "cst")
    hx = M.alloc([128, 256], F32, "hx")
    hu = M.alloc([128, 256], F32, "hu")
    hid = M.alloc([128, 2, 2, 2, 256], BF16, "hid")
    for kv, (w1n, w2n, pen) in enumerate((("w1_k", "w2_k", "pe_k"), ("w1_v", "w2_v", "pe_v"))):
        w1v = Dm[w1n].rearrange("(l d) j -> d l j", d=64)
        for hh in range(2):
            for lc in range(4):
                P.dma("pool", W1[kv][hh * 64:(hh + 1) * 64, lc * 8:(lc + 1) * 8, :], w1v[:, lc * 8:(lc + 1) * 8, :], w=[f"W1_{kv}"])
        P.dma("sp", W2f[:, kv, :, :], Dm[w2n].rearrange("(h j) d -> j h d", j=128), w=[f"W2f{kv}"])
        P.dma("sp", pef[:, kv, :], Dm[pen], w=[f"pef{kv}"])
    P.pool(lambda e: e.memset(W2k[:], 0.0), w=["W2k"])
    P.dve(lambda e: e.tensor_copy(out=W2k[:, 0, :, 0:64], in_=W2f[:, 0, :, :]), r=["W2f0", "W2k"], w=["W2k"])
    P.dve(lambda e: e.tensor_copy(out=W2k[:, 1, :, 64:128], in_=W2f[:, 0, :, :]), r=["W2f0", "W2k"], w=["W2k"])
    P.dve(lambda e: e.tensor_copy(out=W2v[:], in_=W2f[:, 1, :, :]), r=["W2f1"], w=["W2v"])
    P.dve(lambda e: e.tensor_copy(out=peb[:], in_=pef[:]), r=["pef0", "pef1"], w=["peb"])
    for kv in range(2):
        for half in range(2):
            col = kv * 2 + half
            P.pe([lambda e, kv=kv, half=half, col=col, m=m: e.matmul(psb[4][:, col:col + 1], lhsT=W1[kv][0:64, m, half * 128:(half + 1) * 128],
                                                                    rhs=peb[:, kv, m:m + 1], start=(m == 0), stop=(m == 31))
                  for m in range(32)], r=[f"W1_{kv}", "peb"], w=[PSN[4]])
            P.act(lambda e, kv=kv, half=half, col=col: e.activation(out=cst[:, kv, half:half + 1], in_=psb[4][:, col:col + 1], func=AF.Copy),
                  r=[PSN[4]], w=["cst"])
    tokT = [kcT, vcT]
    for kv in range(2):
        for g in range(2):
            for half in range(2):
                bk = g * 2 + half
                pr = slice(g * 64, (g + 1) * 64)
                P.pe([lambda e, kv=kv, half=half, bk=bk, pr=pr, l=l: e.matmul(
                    psb[bk][:, 0:255], lhsT=W1[kv][pr, l, half * 128:(half + 1) * 128],
                    rhs=tokT[kv][pr, l:l + 16 * 254 + 1:16], start=(l == 0), stop=(l == 31)) for l in range(32)],
                    r=[f"W1_{kv}", "kcT", "vcT"], w=[PSN[bk]])
                P.act(lambda e, kv=kv, half=half, bk=bk: e.activation(out=hx[:, 0:255], in_=psb[bk][:, 0:255], func=AF.Identity,
                                                                      bias=cst[:, kv, half:half + 1]), r=[PSN[bk], "cst"], w=["hx"])
                P.dve(lambda e: e.tensor_tensor(out=hu[:, 0:255], in0=hx[:, 0:255], in1=hx[:, 0:255], op=ALU.mult), r=["hx"], w=["hu"])
                P.dve(lambda e: e.tensor_scalar(out=hu[:, 0:255], in0=hu[:, 0:255], scalar1=0.044715, scalar2=1.0, op0=ALU.mult, op1=ALU.add),
                      r=["hu"], w=["hu"])
                P.dve(lambda e: e.tensor_tensor(out=hu[:, 0:255], in0=hu[:, 0:255], in1=hx[:, 0:255], op=ALU.mult), r=["hu", "hx"], w=["hu"])
                P.act(lambda e: e.activation(out=hu[:, 0:255], in_=hu[:, 0:255], func=AF.Exp, scale=-1.5957691216), r=["hu"], w=["hu"])
                P.dve(lambda e: e.tensor_scalar(out=hu[:, 0:255], in0=hu[:, 0:255], scalar1=1.0, scalar2=None, op0=ALU.add), r=["hu"], w=["hu"])
                P.dve(lambda e: e.reciprocal(out=hu[:, 0:255], in_=hu[:, 0:255]), r=["hu"], w=["hu"])
                P.dve(lambda e, kv=kv, g=g, half=half: e.tensor_tensor(out=hid[:, kv, g, half, 0:255], in0=hu[:, 0:255], in1=hx[:, 0:255], op=ALU.mult),
                      r=["hu", "hx"], w=["hid"])
    mm = []
    for g in range(2):
        for half in range(2):
            first = (g == 0 and half == 0)
            last = (g == 1 and half == 1)
            mm.append(lambda e, g=g, half=half, first=first, last=last: e.matmul(
                psb[4][:, 0:255], lhsT=W2k[:, g, half, :], rhs=hid[:, 0, g, half, 0:255], start=first, stop=last))
    P.pe(mm, r=["W2k", "hid"], w=[PSN[4]])
    P.act(lambda e: e.activation(out=KC[:, 0:255], in_=psb[4][:, 0:255], func=AF.Copy), r=[PSN[4], "KC"], w=["KC"])
    for ch in range(2):
        cn = 128 if ch == 0 else 127
        for g in range(2):
            P.pe([lambda e, ch=ch, cn=cn, g=g, half=half: e.matmul(
                psb[5][0:cn, (ch * 2 + g) * 64:(ch * 2 + g + 1) * 64], lhsT=hid[:, 1, g, half, ch * 128:ch * 128 + cn],
                rhs=W2v[:, half, :], start=(half == 0), stop=(half == 1)) for half in range(2)],
                r=["hid", "W2v"], w=[PSN[5]])
    P.pool(lambda e: e.memset(VC[:], 0.0), w=["VC"])
    P.act(lambda e: e.activation(out=VC[:, 0, :, :], in_=psb[5][:, 0:128].rearrange("p (g d) -> p g d", g=2), func=AF.Copy),
          r=[PSN[5], "VC"], w=["VC"])
    P.act(lambda e: e.activation(out=VC[0:127, 1, :, :], in_=psb[5][0:127, 128:256].rearrange("p (g d) -> p g d", g=2), func=AF.Copy),
          r=[PSN[5], "VC"], w=["VC"])
    for nm, src in (("d_ke0", KE[0][:]), ("d_kw", KW[:]), ("d_vs", VS[:].rearrange("p a g d -> p (a g d)")),
                    ("d_kc", KC[:]), ("d_vc", VC[:].rearrange("p a g d -> p (a g d)")), ("d_kcT", kcT[:])):
        if nm in dbg:
            P.dma("sp", Dm[nm], src, r=["KE0", "KE0e", "KW", "VS", "VSone", "KC", "VC", "kcT"], w=[nm])
    P.barrier()
    M.release(m_phase)

    ctx = dict(make_pieces=make_pieces, issue_piece=issue_piece, m_lb=m_lb, wm=wm, wbra=wbra, TOP=TOP, load_weight_cast=load_weight_cast, m_consts=m_consts, nc=nc, P=P, M=M, Dm=Dm, psb=psb, PSN=PSN, C=C, SIN=SIN, COS=COS, RSTD1=RSTD1, GATES=GATES, KE=KE, KW=KW,
               VS=VS, VW=VW, KC=KC, VC=VC, lbt=lbt, omlt=omlt, hng=hng, g1t=g1t, g2t=g2t, dbg=dbg,
               x_tile_prep=x_tile_prep, load_weight_scaled=load_weight_scaled, epsb=epsb, ssq=ssq)
    return ctx


def finish(ctx, out_res):
    P = ctx["P"]
    P.finish(out_res)
    P.emit()
    return ctx["nc"]


def prep_inputs(inputs, b):
    f = np.float32
    w_in = np.asarray(inputs["w_in"][0], f)
    a = w_in[:, 0:2048]
    bq = w_in[:, 2048:2560].reshape(D, 2, 4, 64).transpose(0, 2, 1, 3).reshape(D, 512)
    bkv = w_in[:, 2560:3328]
    bg = w_in[:, 3328:3352]
    mg = w_in[:, 3352:5400]
    w_main = np.ascontiguousarray(np.concatenate([a, bq, bg, mg], axis=1))
    m = {}
    for k, v in host_consts().items():
        m["c_" + k] = v
    m["x"] = np.ascontiguousarray(inputs["x"][b], f)
    m["pos"] = np.ascontiguousarray(np.asarray(inputs["positions"][b], np.int32).reshape(NT, 128).T)
    m["g1"] = np.ascontiguousarray(np.asarray(inputs["norm1_g"][0], f).reshape(8, 128).T)
    m["w_main"] = w_main
    m["w_kv"] = np.ascontiguousarray(bkv)
    m["lb_param"] = np.ascontiguousarray(np.asarray(inputs["lb_param"], f).reshape(1, 1024))
    m["hng"] = np.ascontiguousarray(np.asarray(inputs["hgrn_norm_g"][0], f).reshape(1, 128))
    m["pe_k"] = np.ascontiguousarray(np.asarray(inputs["cmp_pe_k"][0], f).T)
    m["pe_v"] = np.ascontiguousarray(np.asarray(inputs["cmp_pe_v"][0], f).T)
    m["w1_k"] = np.ascontiguousarray(inputs["cmp_w1_k"][0], f)
    m["w2_k"] = np.ascontiguousarray(inputs["cmp_w2_k"][0], f)
    m["w1_v"] = np.ascontiguousarray(inputs["cmp_w1_v"][0], f)
    m["w2_v"] = np.ascontiguousarray(inputs["cmp_w2_v"][0], f)
    m["w_br_a"] = np.ascontiguousarray(inputs["w_br_a"][0], f)
    m["w_br_b"] = np.ascontiguousarray(inputs["w_br_b"][0], f)
    m["w_out"] = np.ascontiguousarray(inputs["w_out"][0], f)
    m["g2"] = np.ascontiguousarray(np.asarray(inputs["norm2_g"][0], f).reshape(8, 128).T)
    m["w_ff1"] = np.ascontiguousarray(inputs["w_ff1"][0], f)
    m["w_ff2"] = np.ascontiguousarray(inputs["w_ff2"][0], f)
    m["final_g"] = np.ascontiguousarray(np.asarray(inputs["final_g"], f).reshape(1, D))
    return m


def phase3a(ctx):
    nc, P, M, Dm, psb, PSN, C = (ctx[k] for k in ("nc", "P", "M", "Dm", "psb", "PSN", "C"))
    SIN, COS, RSTD1, GATES, lbt, omlt, hng, g1t = (ctx[k] for k in ("SIN", "COS", "RSTD1", "GATES", "lbt", "omlt", "hng", "g1t"))
    dbg = ctx["dbg"]
    ident = C["ident"]
    m0 = M.mark()
    wm, wbra = ctx["wm"], ctx["wbra"]
    NRS = M.alloc([128, NT], F32, "NRS")
    RS8 = M.alloc([128, NT], F32, "RS8")
    XT = [M.alloc([128, D], F32, "XT0"), M.alloc([128, D], F32, "XT1")]
    xb = M.alloc([128, D], BF16, "xb")
    xTs = [M.alloc([128, 8, 128], BF16, "xTa"), M.alloc([128, 8, 128], BF16, "xTb")]
    HRS = M.alloc([128, NT], F32, "HRS")
    P.dve(lambda e: e.tensor_scalar(out=NRS[:], in0=RSTD1[:], scalar1=-1.0, scalar2=None, op0=ALU.mult), r=["RSTD1"], w=["NRS"])
    P.dve(lambda e: e.tensor_scalar(out=HRS[:], in0=RSTD1[:], scalar1=0.5, scalar2=None, op0=ALU.mult), r=["RSTD1"], w=["HRS"])
    P.dve(lambda e: e.tensor_scalar(out=RS8[:], in0=RSTD1[:], scalar1=0.125, scalar2=None, op0=ALU.mult), r=["RSTD1"], w=["RS8"])

    def A(shape, dt, nm):
        return M.alloc(shape, dt, nm)

    def A2(shape, dt, nm):
        return [M.alloc(shape, dt, nm + "0"), M.alloc(shape, dt, nm + "1")]
    ft = A([128, 512], F32, "ft")
    kk = A([128, 512], F32, "kk")
    lf = A([128, 512], F32, "lf")
    eb = A([128, 512], F32, "eb")
    enb = A([128, 512], F32, "enb")
    eb2 = A([128, 512], F32, "eb2")
    egs = A2([128, 512], F32, "eg")
    qds = A2([128, 512], BF16, "qd")
    kds = A2([128, 512], BF16, "kd")
    kd2s = [A2([128, 512], BF16, "kd2a"), A2([128, 512], BF16, "kd2b")]
    vbs = A2([128, 512], BF16, "vb")
    decs = A2([128, 8], F32, "dec")
    qTf = A([128, 4, 128], BF16, "qTf")
    qTc = [A([128, 4, 128], BF16, "qT1"), A([128, 4, 128], BF16, "qT2")]
    kT = A([128, 4, 128], BF16, "kT")
    Asb = A([128, 4, 128], BF16, "Asb")
    Sf = A([128, 4, 128], F32, "Sf")
    Sbf = [A([128, 4, 128], BF16, "Sbf0"), A([128, 4, 128], BF16, "Sbf1")]
    oss = A([128, 8], F32, "oss")
    junk = A([128, 128], F32, "junk3")
    ya = A([128, 512], BF16, "ya")
    yaT = A([128, 4, 128], BF16, "yaT")
    sga = A([128, D], F32, "sga")
    sgb = A([128, D], F32, "sgb")
    gap_ = A([128, D], F32, "gap0")
    gap = [gap_, gap_]
    qn = A([128, 512], F32, "qn")
    qnb = A([128, 512], BF16, "qnb")
    qTt_ = A([128, 512], BF16, "qTt0")
    qTt = [qTt_, qTt_]
    rt = A([128, 4, 64], F32, "rt3")
    gt = A([128, 24], F32, "gt")
    P.pool(lambda e: e.memset(qTc[0][:], 0.0), w=["qT1"])
    P.pool(lambda e: e.memset(qTc[1][:], 0.0), w=["qT2"])
    P.pool(lambda e: e.memset(Sf[:], 0.0), w=["Sf"])
    P.pool(lambda e: e.memset(Sbf[0][:], 0.0), w=["Sbf0"])
    Ub = bc(C["U"][:].unsqueeze(1), [128, 4, 128])
    hngb = bc(hng[:].unsqueeze(1), [128, 4, 128])

    def mkproj(xT, xTn):
        def proj(bank, c0, n):
            P.pe([lambda e, k=k: e.matmul(psb[bank][:, 0:n], lhsT=xT[:, k, :], rhs=wm[:, k, c0:c0 + n], start=(k == 0), stop=(k == 7))
                  for k in range(8)], r=[xTn, "wm"], w=[PSN[bank]])
        return proj

    def partA(t):
        p = t % 2
        xT = xTs[p]
        xTn = f"xT{p}"
        ctx["x_tile_prep"](t, XT, xb, xT, 0, False, xTn)
        proj = mkproj(xT, xTn)
        rs = RSTD1[:, t:t + 1]
        nrs = NRS[:, t:t + 1]
        hrs = HRS[:, t:t + 1]
        eg, qd, kd, vb, dec = egs[p], qds[p], kds[p], vbs[p], decs[p]
        kd2 = [kd2s[0][p], kd2s[1][p]]
        BF, BQ, BI, BG = 0, 1, 2, 3
        proj(BF, 512, 512)
        proj(BQ, 0, 512)
        proj(BI, 1024, 512)
        proj(BG, 1536, 512)
        P.act(lambda e: e.activation(out=ft[:], in_=psb[BF][:], func=AF.Tanh, scale=hrs), r=[PSN[BF], "HRS"], w=["ft"])
        P.act(lambda e: e.activation(out=vb[:], in_=psb[BI][:], func=AF.Copy, scale=rs), r=[PSN[BI], "RSTD1"], w=[f"vb{p}"])
        P.act(lambda e: e.activation(out=eg[:], in_=psb[BG][:], func=AF.Tanh, scale=hrs), r=[PSN[BG], "HRS"], w=[f"eg{p}"])
        P.dve(lambda e: e.tensor_tensor(out=ft[:], in0=ft[:], in1=omlt[:], op=ALU.mult), r=["ft", "oml"], w=["ft"])
        P.pool(lambda e: e.tensor_tensor(out=ft[:], in0=ft[:], in1=lbt[:], op=ALU.add), r=["ft", "lb"], w=["ft"])
        P.pool(lambda e: e.tensor_scalar(out=kk[:], in0=ft[:], scalar1=-1.0, scalar2=1.0, op0=ALU.mult, op1=ALU.add), r=["ft"], w=["kk"])
        P.act(lambda e: e.activation(out=lf[:], in_=ft[:], func=AF.Ln), r=["ft"], w=["lf"])
        P.dve(lambda e: e.scalar_tensor_tensor(out=eg[:], in0=eg[:], scalar=1.0, in1=psb[BG][:], op0=ALU.add, op1=ALU.mult),
              r=[PSN[BG], f"eg{p}"], w=[f"eg{p}"])
        P.pool(lambda e: e.tensor_tensor(out=eg[:].rearrange("p (h v) -> p h v", h=4), in0=eg[:].rearrange("p (h v) -> p h v", h=4),
                                         in1=hngb, op=ALU.mult), r=[f"eg{p}", "hng"], w=[f"eg{p}"])
        P.pe([lambda e: e.matmul(psb[BF][:], lhsT=C["U"][:], rhs=lf[:], start=True, stop=True)], r=["lf", "c_U"], w=[PSN[BF]])
        P.pe([lambda e: e.matmul(psb[BI][:], lhsT=C["L"][:], rhs=lf[:], start=True, stop=True)], r=["lf", "c_L"], w=[PSN[BI]])
        P.pe([lambda e, h=h: e.matmul(psb[BG][:, 2 * h:2 * h + 2], lhsT=lf[:, h * 128:(h + 1) * 128], rhs=C["cind"][:], start=True, stop=True)
              for h in range(4)], r=["lf", "c_cind"], w=[PSN[BG]])
        P.act(lambda e: e.activation(out=eb[:], in_=psb[BF][:], func=AF.Exp), r=[PSN[BF]], w=["eb"])
        P.act(lambda e: e.activation(out=enb[:], in_=psb[BF][:], func=AF.Exp, scale=-1.0), r=[PSN[BF]], w=["enb"])
        P.act(lambda e: e.activation(out=eb2[:], in_=psb[BI][:], func=AF.Exp), r=[PSN[BI]], w=["eb2"])
        P.act(lambda e: e.activation(out=dec[:], in_=psb[BG][:, 0:8], func=AF.Exp), r=[PSN[BG]], w=[f"dec{p}"])
        P.dve(lambda e: e.scalar_tensor_tensor(out=qd[:], in0=psb[BQ][:], scalar=rs, in1=eb[:], op0=ALU.mult, op1=ALU.mult),
              r=[PSN[BQ], "eb", "RSTD1"], w=[f"qd{p}"])
        P.pool(lambda e: e.tensor_tensor(out=kd[:], in0=kk[:], in1=enb[:], op=ALU.mult), r=["kk", "enb"], w=[f"kd{p}"])
        for c in range(2):
            P.dve(lambda e, c=c: e.scalar_tensor_tensor(out=kd2[c][:], in0=eb2[:], scalar=C["cind"][:, c:c + 1], in1=kk[:],
                                                        op0=ALU.mult, op1=ALU.mult), r=["eb2", "kk", "c_cind"], w=[f"kd2{c}{p}"])

    def partBC(t):
        p = t % 2
        xT = xTs[p]
        xTn = f"xT{p}"
        proj = mkproj(xT, xTn)
        rs = RSTD1[:, t:t + 1]
        nrs = NRS[:, t:t + 1]
        hrs = HRS[:, t:t + 1]
        rs8 = RS8[:, t:t + 1]
        par = p
        eg, qd, kd, vb, dec = egs[p], qds[p], kds[p], vbs[p], decs[p]
        kd2 = [kd2s[0][p], kd2s[1][p]]
        pv = psb[5][:].bitcast(BF16)
        P.pe([lambda e, h=h: e.transpose(out=pv[:, h * 128:(h + 1) * 128], in_=qd[:, h * 128:(h + 1) * 128], identity=ident[:]) for h in range(4)] +
             [lambda e, h=h: e.transpose(out=pv[:, 512 + h * 128:512 + (h + 1) * 128], in_=kd[:, h * 128:(h + 1) * 128], identity=ident[:]) for h in range(4)],
             r=[f"qd{p}", f"kd{p}", "c_ident"], w=[PSN[5]])
        if ctx.get("bc_stop", 99) <= 0:
            return
        pq = pv[:, 0:512].rearrange("p (h s) -> p h s", h=4)
        P.act(lambda e: e.activation(out=qTf[:], in_=pq, func=AF.Copy), r=[PSN[5]], w=["qTf"])
        if ctx.get("bc_stop", 99) <= 0.5:
            return
        P.act(lambda e: e.activation(out=qTc[0][:, :, 0:64], in_=pq[:, :, 0:64], func=AF.Copy), r=[PSN[5]], w=["qT1"])
        P.act(lambda e: e.activation(out=qTc[1][:, :, 64:128], in_=pq[:, :, 64:128], func=AF.Copy), r=[PSN[5]], w=["qT2"])
        if ctx.get("bc_stop", 99) <= 0.75:
            return
        P.act(lambda e: e.activation(out=kT[:], in_=pv[:, 512:1024].rearrange("p (h s) -> p h s", h=4), func=AF.Copy), r=[PSN[5]], w=["kT"])
        if ctx.get("bc_stop", 99) <= 1:
            return
        P.pe([lambda e, h=h: e.matmul(psb[4][:, h * 128:(h + 1) * 128], lhsT=kT[:, h, :], rhs=qTf[:, h, :], start=True, stop=True)
              for h in range(4)], r=["kT", "qTf"], w=[PSN[4]])
        P.dve(lambda e: e.tensor_tensor(out=Asb[:], in0=psb[4][:].rearrange("p (h s) -> p h s", h=4), in1=Ub, op=ALU.mult),
              r=[PSN[4], "c_U"], w=["Asb"])
        if ctx.get("bc_stop", 99) <= 2:
            return
        for c in range(2):
            P.pe([lambda e, h=h, c=c: e.matmul(psb[6][:, h * 128:(h + 1) * 128], lhsT=kd2[c][:, h * 128:(h + 1) * 128], rhs=vb[:, h * 128:(h + 1) * 128],
                                               start=True, stop=True) for h in range(4)], r=[f"kd2{c}{p}", f"vb{p}"], w=[PSN[6]])
            for h in range(4):
                P.dve(lambda e, h=h, c=c: e.scalar_tensor_tensor(out=Sf[:, h, :], in0=Sf[:, h, :], scalar=dec[:, 2 * h + c:2 * h + c + 1],
                                                                 in1=psb[6][:, h * 128:(h + 1) * 128], op0=ALU.mult, op1=ALU.add),
                      r=["Sf", f"dec{p}", PSN[6]], w=["Sf"])
            dst = 1 - c
            P.pool(lambda e, dst=dst: e.tensor_copy(out=Sbf[dst][:], in_=Sf[:]), r=["Sf"], w=[f"Sbf{dst}"])
            if c == 0:
                mm = []
                for h in range(4):
                    osl = psb[7][:, h * 128:(h + 1) * 128]
                    mm.append(lambda e, h=h, osl=osl: e.matmul(osl, lhsT=Asb[:, h, :], rhs=vb[:, h * 128:(h + 1) * 128], start=True, stop=False))
                    mm.append(lambda e, h=h, osl=osl: e.matmul(osl, lhsT=qTc[0][:, h, :], rhs=Sbf[0][:, h, :], start=False, stop=False))
                    mm.append(lambda e, h=h, osl=osl: e.matmul(osl, lhsT=qTc[1][:, h, :], rhs=Sbf[1][:, h, :], start=False, stop=True))
                P.pe(mm, r=["Asb", f"vb{p}", "qT1", "qT2", "Sbf0", "Sbf1"], w=[PSN[7]])
        if ctx.get("bc_stop", 99) <= 3:
            return
        for h in range(4):
            P.act(lambda e, h=h: e.activation(out=junk[:], in_=psb[7][:, h * 128:(h + 1) * 128], func=AF.Square, accum_out=oss[:, h:h + 1]),
                  r=[PSN[7]], w=["junk3", "oss"])
        P.act(lambda e: e.activation(out=oss[:, 4:8], in_=oss[:, 0:4], func=AF.Ln, scale=1.0 / 128, bias=ctx["epsb"][:, 0:1]), r=["oss", "epsb"], w=["oss2"])
        P.act(lambda e: e.activation(out=oss[:, 4:8], in_=oss[:, 4:8], func=AF.Exp, scale=-0.5), r=["oss2"], w=["oss2"])
        P.dve(lambda e: e.tensor_scalar(out=oss[:, 4:8], in0=oss[:, 4:8], scalar1=hrs, scalar2=None, op0=ALU.mult), r=["oss2", "HRS"], w=["oss2"])
        for h in range(4):
            P.dve(lambda e, h=h: e.scalar_tensor_tensor(out=ya[:, h * 128:(h + 1) * 128], in0=psb[7][:, h * 128:(h + 1) * 128], scalar=oss[:, 4 + h:5 + h],
                                                        in1=eg[:, h * 128:(h + 1) * 128], op0=ALU.mult, op1=ALU.mult),
                  r=[PSN[7], "oss2", f"eg{p}"], w=["ya"])
        if "d_ya" in dbg:
            P.dma("sp", Dm["d_ya"][t * 128:(t + 1) * 128, :], ya[:], r=["ya"], w=["d_ya"])
        if ctx.get("bc_stop", 99) <= 4:
            return
        pv0 = psb[5][:].bitcast(BF16)
        P.pe([lambda e, h=h: e.transpose(out=pv0[:, h * 128:(h + 1) * 128], in_=ya[:, h * 128:(h + 1) * 128], identity=ident[:]) for h in range(4)],
             r=["ya", "c_ident"], w=[PSN[5]])
        P.act(lambda e: e.activation(out=yaT[:], in_=pv0[:, 0:512].rearrange("p (h s) -> p h s", h=4), func=AF.Copy), r=[PSN[5]], w=["yaT"])
        for half, bank in ((0, 4), (1, 6)):
            P.pe([lambda e, c=c, half=half, bank=bank: e.matmul(psb[bank][:], lhsT=yaT[:, c, :], rhs=wbra[:, c, half * 512:(half + 1) * 512],
                                                                start=(c == 0), stop=(c == 3)) for c in range(4)], r=["yaT", "wbra"], w=[PSN[bank]])
        if ctx.get("bc_stop", 99) <= 5:
            return
        MG0 = 2048 + 512 + 24
        for half, bank in ((0, 5), (1, 7)):
            proj(bank, MG0 + half * 512, 512)
            P.act(lambda e, half=half, bank=bank: e.activation(out=sga[:, half * 512:(half + 1) * 512], in_=psb[bank][:], func=AF.Tanh, scale=hrs),
                  r=[PSN[bank], "HRS"], w=[f"sga{half}"])
        for half, bank in ((0, 4), (1, 6)):
            P.dve(lambda e, half=half, bank=bank: e.scalar_tensor_tensor(out=gap[par][:, half * 512:(half + 1) * 512], in0=sga[:, half * 512:(half + 1) * 512],
                                                                         scalar=1.0, in1=psb[bank][:], op0=ALU.add, op1=ALU.mult),
                  r=[PSN[bank], "sga0", "sga1"], w=["gap"])
        P.dma("sp", Dm["s_gap"][t * 128:(t + 1) * 128, :], gap[par][:], r=["gap"], w=["s_gap"])

    def partC2(t):
        p = t % 2
        xT = xTs[p]
        xTn = f"xT{p}"
        proj = mkproj(xT, xTn)
        rs = RSTD1[:, t:t + 1]
        nrs = NRS[:, t:t + 1]
        hrs = HRS[:, t:t + 1]
        rs8 = RS8[:, t:t + 1]
        par = p
        MG0 = 2048 + 512 + 24
        if ctx.get("bc_stop", 99) <= 6:
            return
        for half, bank in ((0, 0), (1, 1)):
            proj(bank, MG0 + 1024 + half * 512, 512)
            P.act(lambda e, half=half, bank=bank: e.activation(out=sgb[:, half * 512:(half + 1) * 512], in_=psb[bank][:], func=AF.Tanh, scale=hrs),
                  r=[PSN[bank], "HRS"], w=[f"sgb{half}"])
        P.dma("pool", Dm["s_gb"][t * 128:(t + 1) * 128, :], sgb[:], r=["sgb0", "sgb1"], w=["s_gb"])
        if ctx.get("bc_stop", 99) <= 7:
            return
        proj(2, 2048, 512)
        P.act(lambda e: e.activation(out=qn[:], in_=psb[2][:], func=AF.Copy, scale=rs8), r=[PSN[2], "RS8"], w=["qn"])
        Q3 = qn[:].rearrange("p (a d) -> p a d", a=8)
        t1 = Q3[:, :, 0:8]
        t2 = Q3[:, :, 8:16]
        cosb = bc(COS[:, t, :].unsqueeze(1), [128, 8, 8])
        sinb = bc(SIN[:, t, :].unsqueeze(1), [128, 8, 8])
        rts = [rt[:, i, :].rearrange("p (a d) -> p a d", a=8) for i in range(4)]
        kr = ["qn", "SIN", "COS"]
        P.dve(lambda e: e.tensor_tensor(out=rts[0], in0=t1, in1=cosb, op=ALU.mult), r=kr, w=["rt0"])
        P.dve(lambda e: e.tensor_tensor(out=rts[1], in0=t2, in1=sinb, op=ALU.mult), r=kr, w=["rt1"])
        P.pool(lambda e: e.tensor_tensor(out=rts[2], in0=t2, in1=cosb, op=ALU.mult), r=kr, w=["rt2"])
        P.pool(lambda e: e.tensor_tensor(out=rts[3], in0=t1, in1=sinb, op=ALU.mult), r=kr, w=["rt3"])
        P.dve(lambda e: e.tensor_tensor(out=t1, in0=rts[0], in1=rts[1], op=ALU.subtract), r=["rt0", "rt1", "rt2", "rt3"], w=["qn"])
        P.dve(lambda e: e.tensor_tensor(out=t2, in0=rts[2], in1=rts[3], op=ALU.add), r=["rt2", "rt3"], w=["qn"])
        P.pool(lambda e: e.tensor_copy(out=qnb[:], in_=qn[:]), r=["qn"], w=["qnb"])
        pv6 = psb[3][:].bitcast(BF16)
        P.pe([lambda e, h=h: e.transpose(out=pv6[:, h * 128:(h + 1) * 128], in_=qnb[:, h * 128:(h + 1) * 128], identity=ident[:]) for h in range(4)],
             r=["qnb", "c_ident"], w=[PSN[3]])
        P.act(lambda e: e.activation(out=qTt[par][:], in_=pv6[:, 0:512], func=AF.Copy), r=[PSN[3]], w=["qTt"])
        P.dma("sp", Dm["s_qt"][t], qTt[par][:], r=["qTt"], w=["s_qt"])
        if ctx.get("bc_stop", 99) <= 8:
            return
        proj(0, 2048 + 512, 24)
        P.act(lambda e: e.activation(out=gt[:], in_=psb[0][:, 0:24], func=AF.Exp, scale=nrs), r=[PSN[0], "NRS"], w=["gt"])
        P.dve(lambda e: e.tensor_scalar(out=gt[:], in0=gt[:], scalar1=1.0, scalar2=None, op0=ALU.add), r=["gt"], w=["gt"])
        P.dve(lambda e: e.reciprocal(out=GATES[:, t, :], in_=gt[:]), r=["gt"], w=["GATES"])

    def interleave(a, b):
        if not b:
            return a
        out = []
        na, nb_ = len(a), len(b)
        j = 0
        for i, x in enumerate(a):
            out.append(x)
            tgt = (i + 1) * nb_ // na
            while j < tgt:
                out.append(b[j])
                j += 1
        out += b[j:]
        return out

    NTA = ctx.get("nt3a", NT)
    partA(0)
    if ctx.get("onlyA"):
        NTA = 0
    for t in range(NTA):
        a = P.capture(lambda: partBC(t))
        b = P.capture(lambda: partA(t + 1)) if t + 1 < NTA else []
        b = b + P.capture(lambda: partC2(t))
        if len(a) >= len(b):
            P.commit(interleave(a, b))
        else:
            P.commit(interleave(b, a))
    if "d_gates" in dbg:
        P.dma("sp", Dm["d_gates"], GATES[:].rearrange("p a b -> p (a b)"), r=["GATES"], w=["d_gates"])
    P.barrier()
    M.release(m0)


def phase3b(ctx):
    nc, P, M, Dm, psb, PSN, C = (ctx[k] for k in ("nc", "P", "M", "Dm", "psb", "PSN", "C"))
    GATES, KE, KW, VS, VW, KC, VC = (ctx[k] for k in ("GATES", "KE", "KW", "VS", "VW", "KC", "VC"))
    dbg = ctx["dbg"]
    ident = C["ident"]
    M.release(ctx["m_lb"])
    m0 = M.mark()
    TOP2 = 229344 - 65536
    M.cap = TOP2
    wf1 = M.alloc_at([128, 8, 4096], BF16, "wf1", TOP2)
    ctx["wf1"] = wf1
    wbrb = M.alloc([128, 4, D], BF16, "wbrb")
    wout = M.alloc([128, 8, D], BF16, "wout")
    stg3 = [M.alloc([128, 512], F32, "stg3a"), M.alloc([128, 512], F32, "stg3b")]
    stg3n = ["stg3a", "stg3b"]
    for i_, pc_ in enumerate(ctx["make_pieces"](wbrb, Dm["w_br_b"], D, 4, 512)):
        ctx["issue_piece"](pc_, "wbrb", "pool" if i_ % 2 else "dve", stg3, stg3n)
    for i_, pc_ in enumerate(ctx["make_pieces"](wout, Dm["w_out"], D, 8, 512)):
        ctx["issue_piece"](pc_, "wout", "pool" if i_ % 2 else "dve", stg3, stg3n)
    wf1_pieces = ctx["make_pieces"](wf1, Dm["w_ff1"], 4096, 8, 512)

    def A(shape, dt, nm):
        return M.alloc(shape, dt, nm)
    QB = [[A([128, 512], BF16, f"QB{p}{g}") for g in range(2)] for p in range(2)]
    QW = [[A([128, 512], BF16, f"QW{p}{g}") for g in range(2)] for p in range(2)]
    Ec_ = A([128, 4, 256], F32, "Ec0")
    Ecs = [Ec_, Ec_]
    Eb = [A([128, 4, 256], BF16, "Eb0"), A([128, 4, 256], BF16, "Eb1")]
    EbTs = [A([128, 8, 128], BF16, "EbT0"), A([128, 8, 128], BF16, "EbT1")]
    ppad = A([128, 2, 260], F32, "ppad")
    imps = [A([128, 64], F32, "imp0"), A([128, 64], F32, "imp1")]
    imp2s = [A([128, 64], F32, "imp20"), A([128, 64], F32, "imp21")]
    m8s = [A([128, 16], F32, "m80"), A([128, 16], F32, "m81")]
    mxs = [A([128, 4], F32, "mx0"), A([128, 4], F32, "mx1")]
    sm = [A([128, 2, 4], F32, "sm0"), A([128, 2, 4], F32, "sm1")]
    ocs = [A([128, 512], F32, "ocs0"), A([128, 512], F32, "ocs1")]
    rsw = A([128, 3, 4], F32, "rsw")
    NB = A([128, 128], BF16, "NB")
    Es = [A([128, 512], BF16, f"Es{i}") for i in range(4)]
    YBf = A([128, 256], F32, "YBf")
    Ytmp = A([128, 256], F32, "Ytmp")
    YBs = [A([128, 512], BF16, "YB0"), A([128, 512], BF16, "YB1")]
    ybT = A([128, 4, 128], BF16, "ybT")
    gapt = [A([128, D], F32, "gapt0"), A([128, D], F32, "gapt1")]
    gbt = [A([128, D], BF16, "gbt0"), A([128, D], BF16, "gbt1")]
    xt0_ = A([128, D], F32, "xt0")
    xt = [xt0_, xt0_]
    mgf = A([128, 512], F32, "mgf")
    mg = A([128, D], BF16, "mg")
    mgT = A([128, 8, 128], BF16, "mgT")
    ht0_ = A([128, D], F32, "ht0")
    ht = [ht0_, ht0_]
    for p in range(2):
        for g in range(2):
            P.pool(lambda e, p=p, g=g: e.memset(QW[p][g][:], 0.0), w=[f"QW{p}{g}"])
            P.pool(lambda e, p=p, g=g: e.memset(QB[p][g][:], 0.0), w=[f"QB{p}{g}"])
    P.pool(lambda e: e.memset(Eb[0][:], 0.0), w=["Eb0"])
    P.pool(lambda e: e.memset(Eb[1][:], 0.0), w=["Eb1"])
    P.pool(lambda e: e.memset(ppad[:], 0.0), w=["ppadg0", "ppadg1"])
    P.pool(lambda e: e.memset(NB[:], 0.0), w=["NBg0", "NBg1"])
    es_ctr = [0]
    sc_ctr = [0]

    def comp_part(qb):
        par = qb % 2
        n = min(8 * qb + 7, 255)
        sq = Dm["s_qt"][qb]
        P.dma("sp", QB[par][0][0:64, :], sq[0:64, :], r=["s_qt"], w=[f"QB{par}0"])
        P.dma("sp", QB[par][1][64:128, :], sq[64:128, :], r=["s_qt"], w=[f"QB{par}1"])
        P.dma("sp", QW[par][0][0:64, :], sq[0:64, :], r=["s_qt"], w=[f"QW{par}0"])
        P.dma("sp", QW[par][1][64:128, :], sq[64:128, :], r=["s_qt"], w=[f"QW{par}1"])
        P.dma("sp", gapt[par][:], Dm["s_gap"][qb * 128:(qb + 1) * 128, :], r=["s_gap"], w=[f"gapt{par}"])
        P.dma("sp", gbt[par][:], Dm["s_gb"][qb * 128:(qb + 1) * 128, :], r=["s_gb"], w=[f"gbt{par}"])

        def comp(g):
            qw = QW[par][g]
            smg = sm[par][:, g, :]
            smn = f"sm{par}{g}"
            Ec, EbT, imp, imp2, m8, mx = Ecs[g], EbTs[g], imps[g], imp2s[g], m8s[g], mxs[g]
            B0 = 0
            B1 = 1
            G = f"g{g}"
            for hp in range(2):
                P.pe([lambda e, hp=hp, hh=hh: e.matmul(psb[B0 + hp][:, hh * 256:hh * 256 + n], lhsT=qw[:, (hp * 2 + hh) * 128:(hp * 2 + hh + 1) * 128],
                                                       rhs=KC[:, 0:n], start=True, stop=True) for hh in range(2)],
                     r=[f"QW{par}{g}", "KC"], w=[PSN[B0 + hp]])
            sv = [psb[B0 + hp][:].rearrange("p (h c) -> p h c", h=2)[:, :, 0:n] for hp in range(2)]
            for hp in range(2):
                P.dve(lambda e, hp=hp: e.tensor_reduce(out=mx[:, hp:hp + 1], in_=sv[hp], axis=AX.XY, op=ALU.max), r=[PSN[B0 + hp]], w=[f"mx{hp}" + G])
            P.dve(lambda e: e.tensor_tensor(out=mx[:, 2:3], in0=mx[:, 0:1], in1=mx[:, 1:2], op=ALU.max), r=["mx0" + G, "mx1" + G], w=["mx2" + G])
            P.dve(lambda e: e.tensor_scalar(out=mx[:, 3:4], in0=mx[:, 2:3], scalar1=-1.0, scalar2=None, op0=ALU.mult), r=["mx2" + G], w=["mx3" + G])
            for hp in range(2):
                P.act(lambda e, hp=hp: e.activation(out=Ec[:, hp * 2:hp * 2 + 2, 0:n], in_=sv[hp], func=AF.Exp, bias=mx[:, 3:4]),
                      r=[PSN[B0 + hp], "mx3" + G], w=[f"Ec{hp}"])
            if qb == 0:
                P.dve(lambda e: e.tensor_tensor(out=Ec[:, :, 0:7], in0=Ec[:, :, 0:7], in1=bc(C["stair"][:, 1:8].unsqueeze(1), [128, 4, 7]), op=ALU.mult),
                      r=["Ec0", "Ec1", "c_stair"], w=["Ec0", "Ec1"])
            else:
                P.dve(lambda e: e.tensor_tensor(out=Ec[:, :, n - 8:n], in0=Ec[:, :, n - 8:n], in1=bc(C["stair"][:].unsqueeze(1), [128, 4, 8]), op=ALU.mult),
                      r=["Ec0", "Ec1", "c_stair"], w=["Ec0", "Ec1"])
            P.pool(lambda e: e.tensor_copy(out=Eb[g][:, :, 0:n], in_=Ec[:, :, 0:n]), r=["Ec0", "Ec1"], w=[f"Eb{g}"])
            P.dve(lambda e: e.tensor_reduce(out=smg, in_=Ec[:, :, 0:n], axis=AX.X, op=ALU.add), r=["Ec0", "Ec1"], w=[smn])
            P.dve(lambda e: e.tensor_scalar(out=smg, in0=smg, scalar1=1e-30, scalar2=None, op0=ALU.max), r=[smn], w=[smn])
            P.dve(lambda e: e.reciprocal(out=smg, in_=smg), r=[smn], w=[smn])
            pp = ppad[:, g, 1:n + 1]
            P.dve(lambda e: e.tensor_scalar(out=pp, in0=Ec[:, 0, 0:n], scalar1=sm[par][:, g, 0:1], scalar2=None, op0=ALU.mult),
                  r=["Ec0", "Ec1", smn], w=["ppad" + G])
            for h in range(1, 4):
                P.dve(lambda e, h=h: e.scalar_tensor_tensor(out=pp, in0=Ec[:, h, 0:n], scalar=sm[par][:, g, h:h + 1], in1=pp, op0=ALU.mult, op1=ALU.add),
                      r=["Ec0", "Ec1", smn, "ppad" + G], w=["ppad" + G])
            w4 = ppad[:, g, 0:256].rearrange("p (j r) -> p j r", r=4)
            w4n = ppad[:, g, 4:260].rearrange("p (j r) -> p j r", r=4)
            P.dve(lambda e: e.tensor_reduce(out=imp[:], in_=w4, axis=AX.X, op=ALU.add), r=["ppad" + G], w=["imp" + G])
            P.dve(lambda e: e.scalar_tensor_tensor(out=imp[:], in0=w4[:, :, 0], scalar=-0.5, in1=imp[:], op0=ALU.mult, op1=ALU.add), r=["ppad" + G, "imp" + G], w=["imp" + G])
            P.dve(lambda e: e.scalar_tensor_tensor(out=imp[:], in0=w4n[:, :, 0], scalar=0.5, in1=imp[:], op0=ALU.mult, op1=ALU.add), r=["ppad" + G, "imp" + G], w=["imp" + G])
            P.dve(lambda e: e.tensor_tensor(out=imp[:], in0=imp[:], in1=C["WB"][:, 62 - 2 * qb:126 - 2 * qb], op=ALU.add), r=["imp" + G, "c_WB"], w=["imp" + G])
            P.dve(lambda e: e.memset(imp[:, 0:1], 2e4), r=["imp" + G], w=["imp" + G])
            P.dve(lambda e: e.max(out=m8[:, 0:8], in_=imp[:]), r=["imp" + G], w=["m8a" + G])
            P.dve(lambda e: e.match_replace(out=imp2[:], in_to_replace=m8[:, 0:8], in_values=imp[:], imm_value=-3e38), r=["imp" + G, "m8a" + G], w=["imp2" + G])
            P.dve(lambda e: e.max(out=m8[:, 8:16], in_=imp2[:]), r=["imp2" + G], w=["m8b" + G])
            nbc = slice(64, 128) if g == 0 else slice(0, 64)
            P.dve(lambda e: e.tensor_scalar(out=NB[:, nbc], in0=imp[:], scalar1=m8[:, 15:16], scalar2=-BIG, op0=ALU.is_lt, op1=ALU.mult),
                  r=["imp" + G, "m8b" + G], w=["NB" + G])
            pv = psb[B0][:].bitcast(BF16)
            nch = 2 if n > 128 else 1
            P.pe([lambda e, h=h, ch=ch: e.transpose(out=pv[:, (h * 2 + ch) * 128:(h * 2 + ch + 1) * 128], in_=Eb[g][:, h, ch * 128:(ch + 1) * 128],
                                                    identity=ident[:]) for h in range(4) for ch in range(nch)],
                 r=[f"Eb{g}", "c_ident"], w=[PSN[B0]])
            P.act(lambda e: e.activation(out=EbT[:].rearrange("p (h c) q -> p h c q", c=2)[:, :, 0:nch, :],
                                         in_=pv[:, 0:1024].rearrange("p (h c q) -> p h c q", h=4, c=2)[:, :, 0:nch, :], func=AF.Copy),
                  r=[PSN[B0]], w=["EbT" + G])
            mm = []
            for h in range(4):
                for ch in range(nch):
                    mm.append(lambda e, h=h, ch=ch: e.matmul(psb[B1][:, h * 64:(h + 1) * 64], lhsT=EbT[:, h * 2 + ch, :],
                                                             rhs=VC[:, ch, g, :], start=(ch == 0), stop=(ch == nch - 1)))
            P.pe(mm, r=["EbT" + G, "VC"], w=[PSN[B1]])
            P.act(lambda e: e.activation(out=ocs[par][:, g * 256:(g + 1) * 256], in_=psb[B1][:, 0:256], func=AF.Copy), r=[PSN[B1]], w=[f"ocs{par}{g}"])
        comp(0)
        comp(1)
        pv = psb[0][:].bitcast(BF16)
        P.pe([lambda e: e.transpose(out=pv[:, 0:128], in_=NB[:], identity=ident[:])], r=["NBg0", "NBg1", "c_ident"], w=[PSN[0]])
        P.act(lambda e: e.activation(out=QB[par][0][64:128, :].rearrange("p (h q) -> p h q", h=4), in_=bc(pv[64:128, 0:128].unsqueeze(1), [64, 4, 128]),
                                     func=AF.Copy), r=[PSN[0]], w=[f"QB{par}0"])
        P.dve(lambda e: e.tensor_copy(out=QB[par][1][0:64, :].rearrange("p (h q) -> p h q", h=4), in_=bc(pv[0:64, 0:128].unsqueeze(1), [64, 4, 128])),
              r=[PSN[0]], w=[f"QB{par}1"])

    def selwin_part(qb):
        par = qb % 2
        for i_ in (2 * qb, 2 * qb + 1):
            if i_ < len(wf1_pieces):
                ctx["issue_piece"](wf1_pieces[i_], "wf1", "pool", stg3, stg3n)
        G4 = GATES[:, qb, :].rearrange("p (g h b) -> p g h b", g=2, h=4)
        steps = []

        def add_branch(g, kts, lhs_arr, lhs_res, rhs_t, rhs_res, Vt, Vres, obank, masks):
            for i, kt in enumerate(kts):
                first = (i == 0)
                last = (kt == kts[-1])
                mk = masks.get(kt)

                def qk(kt=kt, mk=mk):
                    sb = 2 + (sc_ctr[0] % 4)
                    sc_ctr[0] += 1
                    mm = [lambda e: e.matmul(psb[sb][:], lhsT=lhs_arr[:, kt * 128:(kt + 1) * 128], rhs=rhs_t[:], start=True, stop=(mk is None))]
                    rr = [lhs_res, rhs_res]
                    if mk is not None:
                        mm.append(lambda e: e.matmul(psb[sb][:], lhsT=ident[:], rhs=C[mk][:], start=False, stop=True))
                        rr += ["c_ident", "c_" + mk]
                    P.pe(mm, r=rr, w=[PSN[sb]])
                    return sb

                def ex(sb):
                    ei = es_ctr[0] % 4
                    es_ctr[0] += 1
                    P.act(lambda e: e.activation(out=Es[ei][:], in_=psb[sb][:], func=AF.Exp), r=[PSN[sb]], w=[f"Es{ei}"])
                    return ei

                def pvf(ei, kt=kt, first=first, last=last):
                    P.pe([lambda e, h=h: e.matmul(psb[obank][:, h * 65:(h + 1) * 65], lhsT=Es[ei][:, h * 128:(h + 1) * 128], rhs=Vt[:, kt, g, :],
                                                  start=(first and h == 0), stop=(last and h == 3), skip_group_check=True) for h in range(4)],
                         r=[f"Es{ei}", Vres, "VSone", "VWone"], w=[PSN[obank]])
                steps.append((qk, ex, pvf, None))

        def combine(g):
            os_v = psb[6][:, 0:260].rearrange("p (h e) -> p h e", e=65)
            ow_v = psb[7][:, 0:260].rearrange("p (h e) -> p h e", e=65)
            oc_v = ocs[par][:, g * 256:(g + 1) * 256].rearrange("p (h d) -> p h d", d=64)
            smn = f"sm{par}{g}"
            P.dve(lambda e: e.reciprocal(out=rsw[:, 1, :], in_=os_v[:, :, 64]), r=[PSN[6]], w=["rsw1"])
            P.dve(lambda e: e.reciprocal(out=rsw[:, 2, :], in_=ow_v[:, :, 64]), r=[PSN[7]], w=["rsw2"])
            P.dve(lambda e: e.tensor_tensor(out=rsw[:, 0, :], in0=sm[par][:, g, :], in1=G4[:, g, :, 0], op=ALU.mult), r=[smn, "GATES"], w=["rsw0"])
            P.dve(lambda e: e.tensor_tensor(out=rsw[:, 1, :], in0=rsw[:, 1, :], in1=G4[:, g, :, 1], op=ALU.mult), r=["rsw1", "GATES"], w=["rsw1"])
            P.dve(lambda e: e.tensor_tensor(out=rsw[:, 2, :], in0=rsw[:, 2, :], in1=G4[:, g, :, 2], op=ALU.mult), r=["rsw2", "GATES"], w=["rsw2"])
            yv = YBf[:].rearrange("p (h d) -> p h d", d=64)
            tv = Ytmp[:].rearrange("p (h d) -> p h d", d=64)
            P.pool(lambda e: e.tensor_tensor(out=yv, in0=oc_v, in1=bc(rsw[:, 0, :].unsqueeze(2), [128, 4, 64]), op=ALU.mult), r=[f"ocs{par}{g}", "rsw0"], w=["YBf"])
            P.dve(lambda e: e.tensor_tensor(out=tv, in0=os_v[:, :, 0:64], in1=bc(rsw[:, 1, :].unsqueeze(2), [128, 4, 64]), op=ALU.mult), r=[PSN[6], "rsw1"], w=["Ytmp"])
            P.pool(lambda e: e.tensor_tensor(out=YBf[:], in0=YBf[:], in1=Ytmp[:], op=ALU.add), r=["YBf", "Ytmp"], w=["YBf"])
            P.dve(lambda e: e.tensor_tensor(out=tv, in0=ow_v[:, :, 0:64], in1=bc(rsw[:, 2, :].unsqueeze(2), [128, 4, 64]), op=ALU.mult), r=[PSN[7], "rsw2"], w=["Ytmp"])
            P.pool(lambda e: e.tensor_tensor(out=YBs[par][:, g * 256:(g + 1) * 256], in0=YBf[:], in1=Ytmp[:], op=ALU.add), r=["YBf", "Ytmp"], w=[f"YB{par}"])

        for g in range(2):
            kts_s = list(range(0, qb + 1))
            add_branch(g, kts_s, KE[g], f"KE{g}", QB[par][g], f"QB{par}{g}", VS, "VS", 6, {qb: "triA"})
            kts_w = list(range(max(0, qb - 4), qb + 1))
            mw = {qb: "triA"}
            if qb - 4 >= 0:
                mw[qb - 4] = "triB"
            add_branch(g, kts_w, KW, "KW", QW[par][g], f"QW{par}{g}", VW, "VW", 7, mw)
            steps.append((None, None, None, lambda g=g: combine(g)))
        real = [s_ for s_ in steps]
        pend = None
        sb_next = None
        i = 0
        nsteps = len(real)
        idx_q = 0
        def next_q(start):
            j = start
            while j < nsteps and real[j][0] is None:
                j += 1
            return j
        LOOK = 3
        sbs = {}
        qpos = -1
        for _ in range(LOOK):
            qpos = next_q(qpos + 1)
            if qpos < nsteps and qpos not in sbs:
                sbs[qpos] = real[qpos][0]()
        for i in range(nsteps):
            qk, ex, pvf, cb = real[i]
            if cb is not None:
                cb()
                continue
            sb = sbs.pop(i)
            ei = ex(sb)
            nq = i
            for _ in range(LOOK):
                nq = next_q(nq + 1)
                if nq < nsteps and nq not in sbs:
                    sbs[nq] = real[nq][0]()
            pvf(ei)

    def merge_part(qb):
        par = qb % 2
        YB = YBs[par]
        P.dma("sp", xt[par][:], Dm["x"][qb * 128:(qb + 1) * 128, :], w=["xt3"])
        if "d_yb" in dbg:
            P.dma("sp", Dm["d_yb"][qb * 128:(qb + 1) * 128, :], YB[:], r=[f"YB{par}"], w=["d_yb"])
        pv = psb[0][:].bitcast(BF16)
        P.pe([lambda e, c=c: e.transpose(out=pv[:, c * 128:(c + 1) * 128], in_=YB[:, c * 128:(c + 1) * 128], identity=ident[:]) for c in range(4)],
             r=[f"YB{par}", "c_ident"], w=[PSN[0]])
        P.act(lambda e: e.activation(out=ybT[:].rearrange("p c q -> p (c q)"), in_=pv[:, 0:512], func=AF.Copy), r=[PSN[0]], w=["ybT"])
        for half in range(2):
            P.pe([lambda e, c=c, half=half: e.matmul(psb[1 - half][:], lhsT=ybT[:, c, :], rhs=wbrb[:, c, half * 512:(half + 1) * 512], start=(c == 0), stop=(c == 3))
                  for c in range(4)], r=["ybT", "wbrb"], w=[PSN[1 - half]])
            hs = slice(half * 512, (half + 1) * 512)
            P.dve(lambda e, half=half, hs=hs: e.scalar_tensor_tensor(out=mgf[:], in0=gbt[par][:, hs], scalar=1.0, in1=psb[1 - half][:],
                                                                     op0=ALU.add, op1=ALU.mult),
                  r=[PSN[1 - half], f"gbt{par}"], w=["mgf"])
            P.pool(lambda e, hs=hs: e.tensor_tensor(out=mg[:, hs], in0=mgf[:], in1=gapt[par][:, hs], op=ALU.add),
                   r=["mgf", f"gapt{par}"], w=[f"mg{half}"])
        P.pe([lambda e, c=c: e.transpose(out=pv[:, c * 128:(c + 1) * 128], in_=mg[:, c * 128:(c + 1) * 128], identity=ident[:]) for c in range(8)],
             r=["mg0", "mg1", "c_ident"], w=[PSN[0]])
        P.act(lambda e: e.activation(out=mgT[:].rearrange("p c q -> p (c q)"), in_=pv[:, 0:1024], func=AF.Copy, scale=0.5), r=[PSN[0]], w=["mgT"])
        for half, bank in ((0, 0), (1, 1)):
            P.pe([lambda e, c=c, half=half, bank=bank: e.matmul(psb[bank][:], lhsT=mgT[:, c, :], rhs=wout[:, c, half * 512:(half + 1) * 512],
                                                                start=(c == 0), stop=(c == 7)) for c in range(8)], r=["mgT", "wout"], w=[PSN[bank]])
            hs = slice(half * 512, (half + 1) * 512)
            P.dve(lambda e, bank=bank, hs=hs: e.tensor_tensor(out=ht[par][:, hs], in0=psb[bank][:], in1=xt[par][:, hs], op=ALU.add),
                  r=[PSN[bank], "xt3"], w=["ht"])
        P.dma("sp", Dm["s_h"][qb * 128:(qb + 1) * 128, :], ht[par][:], r=["ht"], w=["s_h"])

    def interleave(a, b):
        if not b:
            return a
        out = []
        na, nb_ = len(a), len(b)
        j = 0
        for i, x in enumerate(a):
            out.append(x)
            tgt = (i + 1) * nb_ // na
            while j < tgt:
                out.append(b[j])
                j += 1
        out += b[j:]
        return out

    NQ = ctx.get("nt3b", NT)
    comp_part(0)
    for qb in range(NQ):
        a = P.capture(lambda: selwin_part(qb))
        b = P.capture(lambda: merge_part(qb - 1)) if qb >= 1 else []
        if qb + 1 < NQ:
            b = b + P.capture(lambda: comp_part(qb + 1))
        P.commit(interleave(a, b))
    merge_part(NQ - 1)
    P.barrier()
    M.release(m0)


def phase4(ctx):
    nc, P, M, Dm, psb, PSN, C = (ctx[k] for k in ("nc", "P", "M", "Dm", "psb", "PSN", "C"))
    ident = C["ident"]
    M.release(ctx["m_consts"])
    wf1 = ctx["wf1"]
    wf2 = M.alloc([128, 32, D], BF16, "wf2")
    fg = M.alloc([128, D], F32, "fg")
    g2 = M.alloc([128, 8], F32, "g2p4")
    eps4 = M.alloc([128, 1], F32, "eps4")
    P.dma("sp", g2[:], Dm["g2"], w=["g2p4"])
    P.dma("sp", fg[:], Dm["final_g"].partition_broadcast(128), w=["fg"])
    P.pool(lambda e: e.memset(eps4[:], EPS), w=["eps4"])
    stg4 = [M.alloc([128, 1024], F32, f"stg4{i}") for i in range(2)]
    stg4n = [f"stg4{i}" for i in range(2)]
    for i_, pc_ in enumerate(ctx["make_pieces"](wf2, Dm["w_ff2"], D, 32, 1024)):
        ctx["issue_piece"](pc_, "wf2", "act" if i_ % 2 == 0 else "dve", stg4, stg4n)
    TS = 256
    NS = S // TS
    hT = M.alloc([128, 8, TS], BF16, "hT")
    uT = M.alloc([128, 32, TS], BF16, "uT")
    hts = [[M.alloc([128, D], F32, f"hts{p}{j}") for j in range(2)] for p in range(2)]
    hb = M.alloc([128, D], BF16, "hb")
    junk = M.alloc([128, D], BF16, "junk4")
    st = M.alloc([128, 8], F32, "st4")
    rl = [M.alloc([128, TS], F32, "rl0"), M.alloc([128, TS], F32, "rl1")]
    yo = [M.alloc([128, D], F32, "yo0"), M.alloc([128, D], F32, "yo1")]
    oo = [M.alloc([128, D], F32, "oo0"), M.alloc([128, D], F32, "oo1")]

    def stile(s_):
        par = s_ % 2
        for j in range(2):
            r0 = s_ * TS + j * 128
            h_ = hts[par][j]
            hn = f"hts{par}{j}"
            P.dma("sp", h_[:], Dm["s_h"][r0:r0 + 128, :], r=["s_h"], w=[hn])
            P.act(lambda e, h_=h_, j=j: e.activation(out=junk[:], in_=h_[:], func=AF.Square, accum_out=st[:, j:j + 1]), r=[hn], w=["junk4", f"ss{j}"])
            P.act(lambda e, j=j: e.activation(out=st[:, 2 + j:3 + j], in_=st[:, j:j + 1], func=AF.Ln, scale=1.0 / D, bias=eps4[:, 0:1]),
                  r=[f"ss{j}", "eps4"], w=[f"rs{j}"])
            P.act(lambda e, j=j: e.activation(out=st[:, 2 + j:3 + j], in_=st[:, 2 + j:3 + j], func=AF.Exp, scale=-1.0), r=[f"rs{j}"], w=[f"rs{j}"])
            P.dve(lambda e, h_=h_: e.tensor_copy(out=hb[:, 0:512], in_=h_[:, 0:512]), r=[hn], w=["hb"])
            P.pool(lambda e, h_=h_: e.tensor_copy(out=hb[:, 512:1024], in_=h_[:, 512:1024]), r=[hn], w=["hb2"])
            pv = psb[j][:].bitcast(BF16)
            P.pe([lambda e, k=k, pv=pv: e.transpose(out=pv[:, k * 128:(k + 1) * 128], in_=hb[:, k * 128:(k + 1) * 128], identity=ident[:]) for k in range(8)],
                 r=["hb", "hb2", "c_ident"], w=[PSN[j]])
            P.dve(lambda e, j=j, pv=pv: e.tensor_tensor(out=hT[:, :, j * 128:(j + 1) * 128], in0=pv[:, 0:1024].rearrange("p (k t) -> p k t", k=8),
                                                        in1=bc(g2[:].unsqueeze(2), [128, 8, 128]), op=ALU.mult),
                  r=[PSN[j], "g2p4"], w=[f"hT{j}"])
        for f in range(32):
            bk = 2 + (f % 4)
            P.pe([lambda e, k=k, f=f, bk=bk: e.matmul(psb[bk][:, 0:TS], lhsT=wf1[:, k, f * 128:(f + 1) * 128], rhs=hT[:, k, :], start=(k == 0), stop=(k == 7))
                  for k in range(8)], r=["hT0", "hT1", "wf1"], w=[PSN[bk]])
            ri = f % 2
            P.act(lambda e, bk=bk, ri=ri: e.activation(out=rl[ri][:], in_=psb[bk][:, 0:TS], func=AF.Relu), r=[PSN[bk]], w=[f"rl{ri}"])
            eng = P.dve if f % 2 == 0 else P.pool
            eng(lambda e, f=f, ri=ri: e.tensor_tensor(out=uT[:, f, :], in0=rl[ri][:], in1=rl[ri][:], op=ALU.mult), r=[f"rl{ri}"], w=["uT"])
        for j in range(2):
            h_ = hts[par][j]
            hn = f"hts{par}{j}"
            for half in range(2):
                bk = 6 + half
                P.pe([lambda e, f=f, j=j, half=half, bk=bk: e.matmul(psb[bk][:], lhsT=uT[:, f, j * 128:(j + 1) * 128], rhs=wf2[:, f, half * 512:(half + 1) * 512],
                                                                     start=(f == 0), stop=(f == 31)) for f in range(32)], r=["uT", "wf2"], w=[PSN[bk]])
                hs = slice(half * 512, (half + 1) * 512)
                P.dve(lambda e, j=j, bk=bk, hs=hs, h_=h_: e.scalar_tensor_tensor(out=yo[j][:, hs], in0=psb[bk][:], scalar=st[:, 2 + j:3 + j], in1=h_[:, hs],
                                                                                 op0=ALU.mult, op1=ALU.add), r=[PSN[bk], f"rs{j}", hn], w=[f"yo{j}"])
            P.act(lambda e, j=j: e.activation(out=junk[:], in_=yo[j][:], func=AF.Square, accum_out=st[:, 4 + j:5 + j]), r=[f"yo{j}"], w=["junk4", f"s3{j}"])
            P.act(lambda e, j=j: e.activation(out=st[:, 6 + j:7 + j], in_=st[:, 4 + j:5 + j], func=AF.Ln, scale=1.0 / D, bias=eps4[:, 0:1]),
                  r=[f"s3{j}", "eps4"], w=[f"r3{j}"])
            P.act(lambda e, j=j: e.activation(out=st[:, 6 + j:7 + j], in_=st[:, 6 + j:7 + j], func=AF.Exp, scale=-0.5), r=[f"r3{j}"], w=[f"r3{j}"])
            P.dve(lambda e, j=j: e.scalar_tensor_tensor(out=oo[j][:], in0=yo[j][:], scalar=st[:, 6 + j:7 + j], in1=fg[:], op0=ALU.mult, op1=ALU.mult),
                  r=[f"yo{j}", f"r3{j}", "fg"], w=[f"oo{j}"])
            r0 = s_ * TS + j * 128
            P.dma("sp", Dm["out"][r0:r0 + 128, :], oo[j][:], r=[f"oo{j}"], w=["out"])

    for s_ in range(NS):
        stile(s_)


_NC_CACHE = {}


def kernel(**inputs):
    if "nc" not in _NC_CACHE:
        ctx = build()
        phase3a(ctx)
        phase3b(ctx)
        phase4(ctx)
        _NC_CACHE["nc"] = finish(ctx, ["out"])
    nc = _NC_CACHE["nc"]
    maps = [prep_inputs(inputs, b) for b in range(8)]
    res = run_bass_kernel_spmd(nc, maps, core_ids=list(range(8)))
    out = np.stack([np.asarray(r["out"], np.float32) for r in res.results], axis=0)
    return out.reshape(8, S, D)
```
